# Optimizing a Trainium2 kernel written in Bass

```python
import jax, jax.numpy as jnp
from jax import lax
import numpy as np

D_MODEL = 1024
BATCH = 8
SEQ = 4096
DEPTH = 4

CHUNK = 64
MEM_TOKENS = 256
N_EVEN = (DEPTH + 1) // 2
N_ODD = DEPTH // 2
HGRN_HEADS = 4
HGRN_DK = 128
HGRN_DV = 128
HGRN_KEY = HGRN_HEADS * HGRN_DK
HGRN_VAL = HGRN_HEADS * HGRN_DV
SSD_HEADS = 8
SSD_HEADDIM = 64
SSD_DINNER = SSD_HEADS * SSD_HEADDIM
SSD_GROUPS = 2
SSD_STATE = 128
SSD_CONV = 4
SSD_CONV_CH = SSD_DINNER + 2 * SSD_GROUPS * SSD_STATE
AB_SPLIT_SIZES = (HGRN_KEY, HGRN_KEY, HGRN_VAL, HGRN_VAL, SSD_DINNER, SSD_CONV_CH, SSD_HEADS)
AB_IN = sum(AB_SPLIT_SIZES)
AB_OUT = HGRN_VAL + SSD_DINNER
CONF_KERNEL = 31
XATTN_HEADS = 4
XATTN_HD = D_MODEL // XATTN_HEADS
D_FF = 4 * D_MODEL
EPS = 1e-6

kernel_name = "hybrid_hgrn2_ssd_conformer_trunk"


def rms_normalize(x):
    xf = x.astype(jnp.float32)
    return xf * lax.rsqrt(jnp.mean(xf * xf, axis=-1, keepdims=True) + EPS)


def rms_norm(x, w):
    return (rms_normalize(x) * w.astype(jnp.float32)).astype(x.dtype)


def layer_norm(x, w, b):
    xf = x.astype(jnp.float32)
    mu = jnp.mean(xf, axis=-1, keepdims=True)
    var = jnp.mean(jnp.square(xf - mu), axis=-1, keepdims=True)
    y = (xf - mu) * lax.rsqrt(var + EPS) * w.astype(jnp.float32) + b.astype(jnp.float32)
    return y.astype(x.dtype)


def causal_depthwise_conv(x, w, b):
    K, C = w.shape
    xp = jnp.pad(x, ((0, 0), (K - 1, 0), (0, 0)))
    y = lax.conv_general_dilated(xp, w[:, None, :].astype(x.dtype), window_strides=(1,), padding='VALID',
                                 dimension_numbers=('NWC', 'WIO', 'NWC'), feature_group_count=C)
    return y + b


def masked_exp(diff, mask):
    return jnp.where(mask, jnp.exp(jnp.where(mask, diff, 0.0)), 0.0)


def segsum_exp(a):
    L = a.shape[-1]
    cs = jnp.cumsum(a, axis=-1)
    mask = jnp.tril(jnp.ones((L, L), dtype=bool))
    return masked_exp(cs[..., :, None] - cs[..., None, :], mask)


def hgrn2_chunk_scan(q, k, v, log_f):
    Bsz, T, H, DK = q.shape
    DV = v.shape[-1]
    n_chunks = T // CHUNK

    def to_chunks(a):
        return a.reshape(Bsz, n_chunks, CHUNK, H, a.shape[-1]).transpose(1, 0, 3, 2, 4)

    mask = jnp.tril(jnp.ones((CHUNK, CHUNK), dtype=bool))[:, :, None]

    def step(S, inp):
        qi, ki, vi, gi = inp
        b = jnp.cumsum(gi, axis=2)
        rel = b[:, :, :, None, :] - b[:, :, None, :, :]
        decay = masked_exp(rel, mask)
        scores = jnp.einsum('bhtd,bhsd,bhtsd->bhts', qi, ki, decay)
        o = jnp.einsum('bhts,bhsv->bhtv', scores, vi) + jnp.einsum('bhtd,bhdv->bhtv', qi * jnp.exp(b), S)
        b_last = b[:, :, -1:, :]
        S_new = S * jnp.exp(b_last[:, :, 0, :, None]) + jnp.einsum('bhsd,bhsv->bhdv', ki * jnp.exp(b_last - b), vi)
        return S_new, o

    S0 = jnp.zeros((Bsz, H, DK, DV), jnp.float32)
    _, o = lax.scan(step, S0, (to_chunks(q), to_chunks(k), to_chunks(v), to_chunks(log_f)))
    return o.transpose(1, 0, 3, 2, 4).reshape(Bsz, T, H, DV)


def ssd_chunked_scan(xs, dt, A, Bm, Cm):
    Bsz, T, H, P = xs.shape
    G, N = Bm.shape[2], Bm.shape[3]
    J = H // G
    C = T // CHUNK
    L = CHUNK
    xc = (xs * dt[..., None]).reshape(Bsz, C, L, G, J, P)
    dA = (dt * A).reshape(Bsz, C, L, G, J).transpose(0, 3, 4, 1, 2)
    Bc = Bm.reshape(Bsz, C, L, G, N)
    Cc = Cm.reshape(Bsz, C, L, G, N)
    A_cs = jnp.cumsum(dA, axis=-1)
    decay_intra = segsum_exp(dA)
    cb = jnp.einsum('bclgn,bcsgn->bgcls', Cc, Bc)
    y_diag = jnp.einsum('bgjcls,bcsgjp->bclgjp', cb[:, :, None] * decay_intra, xc)
    decay_to_end = jnp.exp(A_cs[..., -1:] - A_cs)
    chunk_states = jnp.einsum('bclgn,bgjcl,bclgjp->bcgjpn', Bc, decay_to_end, xc)
    chunk_states = jnp.concatenate([jnp.zeros_like(chunk_states[:, :1]), chunk_states], axis=1)
    chunk_decay = jnp.pad(A_cs[..., -1], ((0, 0), (0, 0), (0, 0), (1, 0)))
    decay_chunk = segsum_exp(chunk_decay)
    states_in = jnp.einsum('bgjzc,bcgjpn->bzgjpn', decay_chunk, chunk_states)[:, :-1]
    y_off = jnp.einsum('bclgn,bcgjpn,bgjcl->bclgjp', Cc, states_in, jnp.exp(A_cs))
    return (y_diag + y_off).reshape(Bsz, T, H, P)


def hgrn_ssd_mixer(h, w_in, lb, out_norm_w, conv_w, conv_b, dt_bias, a_log, d_skip, ssd_norm_w, w_out):
    Bsz, T, _ = h.shape
    points = [int(p) for p in np.cumsum(AB_SPLIT_SIZES)[:-1]]
    q, f_raw, i_val, g, z, xbc, dt_raw = jnp.split(h @ w_in, points, axis=-1)
    f32 = f_raw.astype(jnp.float32)
    log_f = jnp.log(lb + (1.0 - lb) * jax.nn.sigmoid(f32))
    k = (1.0 - lb) * jax.nn.sigmoid(-f32)
    qh = q.astype(jnp.float32).reshape(Bsz, T, HGRN_HEADS, HGRN_DK) * (HGRN_DK ** -0.5)
    o_a = hgrn2_chunk_scan(qh, k.reshape(Bsz, T, HGRN_HEADS, HGRN_DK),
                           i_val.astype(jnp.float32).reshape(Bsz, T, HGRN_HEADS, HGRN_DV),
                           log_f.reshape(Bsz, T, HGRN_HEADS, HGRN_DK))
    o_a = rms_normalize(o_a).reshape(Bsz, T, HGRN_VAL) * out_norm_w.astype(jnp.float32)
    o_a = (o_a * jax.nn.silu(g.astype(jnp.float32))).astype(h.dtype)
    xbc = jax.nn.silu(causal_depthwise_conv(xbc, conv_w, conv_b)).astype(jnp.float32)
    xs, Bm, Cm = jnp.split(xbc, [SSD_DINNER, SSD_DINNER + SSD_GROUPS * SSD_STATE], axis=-1)
    xs = xs.reshape(Bsz, T, SSD_HEADS, SSD_HEADDIM)
    dt = jax.nn.softplus(dt_raw.astype(jnp.float32) + dt_bias.astype(jnp.float32))
    A = -jnp.exp(a_log.astype(jnp.float32))
    y = ssd_chunked_scan(xs, dt, A, Bm.reshape(Bsz, T, SSD_GROUPS, SSD_STATE),
                         Cm.reshape(Bsz, T, SSD_GROUPS, SSD_STATE))
    y = (y + xs * d_skip.astype(jnp.float32)[:, None]).reshape(Bsz, T, SSD_DINNER)
    y = y * jax.nn.silu(z.astype(jnp.float32))
    y = rms_normalize(y.reshape(Bsz, T, SSD_GROUPS, SSD_DINNER // SSD_GROUPS)).reshape(Bsz, T, SSD_DINNER)
    y = (y * ssd_norm_w.astype(jnp.float32)).astype(h.dtype)
    return jnp.concatenate([o_a, y], axis=-1) @ w_out


def conformer_conv_module(h, w_pw1, b_pw1, w_dw, b_dw, ln_w, ln_b, w_pw2, b_pw2):
    a, gate = jnp.split(h @ w_pw1 + b_pw1, 2, axis=-1)
    u = a * jax.nn.sigmoid(gate)
    u = causal_depthwise_conv(u, w_dw, b_dw)
    u = jax.nn.silu(layer_norm(u, ln_w, ln_b))
    return u @ w_pw2 + b_pw2


def memory_cross_attention(h, mem_n, wq, wk, wv, wo):
    Bsz, T, D = h.shape
    M = mem_n.shape[1]
    q = (h @ wq).reshape(Bsz, T, XATTN_HEADS, XATTN_HD)
    k = (mem_n @ wk).reshape(Bsz, M, XATTN_HEADS, XATTN_HD)
    v = (mem_n @ wv).reshape(Bsz, M, XATTN_HEADS, XATTN_HD)
    s = jnp.einsum('bthd,bmhd->bhtm', q, k).astype(jnp.float32) * (XATTN_HD ** -0.5)
    p = jax.nn.softmax(s, axis=-1).astype(v.dtype)
    o = jnp.einsum('bhtm,bmhd->bthd', p, v).reshape(Bsz, T, D)
    return o @ wo


def sq_relu_mlp(h, w1, w2):
    return jnp.square(jax.nn.relu(h @ w1)) @ w2


def setup_inputs(seed: int = 0) -> dict:
    key = jax.random.key(seed)
    ks = iter(jax.random.split(key, 64))
    f32 = jnp.float32

    def nrm(shape, scale):
        return jax.random.normal(next(ks), shape, f32) * scale

    def gain(shape):
        return 1.0 + 0.02 * jax.random.normal(next(ks), shape, f32)

    D = D_MODEL
    dt0 = jnp.exp(jax.random.uniform(next(ks), (N_EVEN, SSD_HEADS), f32) * (jnp.log(0.1) - jnp.log(0.001)) + jnp.log(0.001))
    return {
        "x": nrm((BATCH, SEQ, D), 1.0),
        "mem": nrm((BATCH, MEM_TOKENS, D), 1.0),
        "mem_norm_w": gain((D,)),
        "norm_mix_w": gain((DEPTH, D)),
        "ab_w_in": nrm((N_EVEN, D, AB_IN), D ** -0.5),
        "hgrn_lb_logits": nrm((N_EVEN, HGRN_KEY), 0.5),
        "hgrn_out_norm_w": gain((N_EVEN, HGRN_VAL)),
        "ssd_conv_w": nrm((N_EVEN, SSD_CONV, SSD_CONV_CH), SSD_CONV ** -0.5),
        "ssd_conv_b": nrm((N_EVEN, SSD_CONV_CH), 0.02),
        "ssd_dt_bias": dt0 + jnp.log(-jnp.expm1(-dt0)),
        "ssd_a_log": jnp.log(jax.random.uniform(next(ks), (N_EVEN, SSD_HEADS), f32, 1.0, 16.0)),
        "ssd_d": gain((N_EVEN, SSD_HEADS)),
        "ssd_norm_w": gain((N_EVEN, SSD_DINNER)),
        "ab_w_out": nrm((N_EVEN, AB_OUT, D), AB_OUT ** -0.5),
        "cv_w_pw1": nrm((N_ODD, D, 2 * D), D ** -0.5),
        "cv_b_pw1": nrm((N_ODD, 2 * D), 0.02),
        "cv_w_dw": nrm((N_ODD, CONF_KERNEL, D), CONF_KERNEL ** -0.5),
        "cv_b_dw": nrm((N_ODD, D), 0.02),
        "cv_ln_w": gain((N_ODD, D)),
        "cv_ln_b": nrm((N_ODD, D), 0.02),
        "cv_w_pw2": nrm((N_ODD, D, D), D ** -0.5),
        "cv_b_pw2": nrm((N_ODD, D), 0.02),
        "norm_xattn_w": gain((DEPTH, D)),
        "xattn_wq": nrm((DEPTH, D, D), D ** -0.5),
        "xattn_wk": nrm((DEPTH, D, D), D ** -0.5),
        "xattn_wv": nrm((DEPTH, D, D), D ** -0.5),
        "xattn_wo": nrm((DEPTH, D, D), D ** -0.5),
        "norm_mlp_w": gain((DEPTH, D)),
        "mlp_w1": nrm((DEPTH, D, D_FF), D ** -0.5),
        "mlp_w2": nrm((DEPTH, D_FF, D), D_FF ** -0.5),
        "final_norm_w": gain((D,)),
    }


def reference(x, mem, mem_norm_w, norm_mix_w, ab_w_in, hgrn_lb_logits, hgrn_out_norm_w,
              ssd_conv_w, ssd_conv_b, ssd_dt_bias, ssd_a_log, ssd_d, ssd_norm_w, ab_w_out,
              cv_w_pw1, cv_b_pw1, cv_w_dw, cv_b_dw, cv_ln_w, cv_ln_b, cv_w_pw2, cv_b_pw2,
              norm_xattn_w, xattn_wq, xattn_wk, xattn_wv, xattn_wo,
              norm_mlp_w, mlp_w1, mlp_w2, final_norm_w):
    mem_n = rms_norm(mem, mem_norm_w)
    p = jax.nn.softmax(hgrn_lb_logits.astype(jnp.float32), axis=0)
    lower_bounds = jnp.cumsum(p, axis=0) - p[0:1]
    for layer in range(DEPTH):
        h = rms_norm(x, norm_mix_w[layer])
        if layer % 2 == 0:
            e = layer // 2
            h = hgrn_ssd_mixer(h, ab_w_in[e], lower_bounds[e], hgrn_out_norm_w[e], ssd_conv_w[e],
                               ssd_conv_b[e], ssd_dt_bias[e], ssd_a_log[e], ssd_d[e], ssd_norm_w[e],
                               ab_w_out[e])
        else:
            o = layer // 2
            h = conformer_conv_module(h, cv_w_pw1[o], cv_b_pw1[o], cv_w_dw[o], cv_b_dw[o],
                                      cv_ln_w[o], cv_ln_b[o], cv_w_pw2[o], cv_b_pw2[o])
        x = x + h
        x = x + memory_cross_attention(rms_norm(x, norm_xattn_w[layer]), mem_n, xattn_wq[layer],
                                       xattn_wk[layer], xattn_wv[layer], xattn_wo[layer])
        x = x + sq_relu_mlp(rms_norm(x, norm_mlp_w[layer]), mlp_w1[layer], mlp_w2[layer])
    return rms_norm(x, final_norm_w)
```

```python
import numpy as np
import concourse.bass as bass
import concourse.mybir as mybir
from concourse.bass_utils import run_bass_kernel_spmd

F32 = mybir.dt.float32
BF16 = mybir.dt.bfloat16
ALU = mybir.AluOpType
AF = mybir.ActivationFunctionType

ENGS = ("pe", "act", "dve", "pool", "sp")

D = 1024
T = 4096
TB = 512
NST = T // TB
KC = 8
MEM = 256
DFF = 4096
EPS = 1e-6
AB_IN = 3592


class Op:
    __slots__ = ("eng", "idx", "fn", "deps", "dma_waits", "signal", "val", "dma_sem")

    def __init__(self, eng, idx, fn):
        self.eng = eng
        self.idx = idx
        self.fn = fn
        self.deps = {}
        self.dma_waits = {}
        self.signal = False
        self.val = 0
        self.dma_sem = None


class Prog:
    def __init__(self, nc):
        self.nc = nc
        self.ops = {e: [] for e in ENGS}
        self.seen = {e: {} for e in ENGS}
        self.seen_dma = {e: {} for e in ENGS}
        self.lastw = {}
        self.readers = {}
        self.dma_sems = {}
        self.dma_last = {}
        self.esem = {}
        self.inherit = {}
        self.bufkeys = {}
        self.read_hook = None
        self.pending = []
        self.out_tokens = []
        self.do_schedule = True
        self.gseq = 0
        self.gseq_c = {}
        self.gseq_d = {}
        self.opclk = {}
        self.dmaclk = {}
        self.dma_prev = {}

    @staticmethod
    def _bufname(k):
        return k[0] if isinstance(k, tuple) else k

    def _record(self, op, tok, reads, writes):
        eng = op.eng
        cc = {}
        cd = {}

        def add(t):
            if t is None:
                return
            if t[0] == "c":
                _, se, si = t
                if se == eng and se == "pe":
                    return
                if cc.get(se, -1) < si:
                    cc[se] = si
            else:
                _, sn, val = t
                if cd.get(sn, 0) < val:
                    cd[sn] = val

        for k in reads:
            add(self.lastw.get(k))
        for k in writes:
            if k not in self.lastw:
                for t in self.inherit.get(self._bufname(k), ()):
                    add(t)
            add(self.lastw.get(k))
            for t in self.readers.get(k, {}).values():
                add(t)
        if tok[0] == "d":
            prev = self.dma_prev.get(tok[1])
            if prev:
                add(("d", tok[1], prev))
        clk, dclk = self.seen[eng], self.seen_dma[eng]
        cands = [(self.gseq_c[(se, si)], "c", se, si) for se, si in cc.items()]
        cands += [(self.gseq_d[(sn, val)], "d", sn, val) for sn, val in cd.items()]
        cands.sort(reverse=True)
        for _, kind, a, b in cands:
            if kind == "c":
                if clk.get(a, -1) >= b:
                    continue
                op.deps[a] = b
                self.ops[a][b].signal = True
                snap = self.opclk[(a, b)]
                clk[a] = b
            else:
                if dclk.get(a, 0) >= b:
                    continue
                op.dma_waits[a] = b
                snap = self.dmaclk[(a, b)]
                dclk[a] = b
            for k2, v2 in snap[0].items():
                if clk.get(k2, -1) < v2:
                    clk[k2] = v2
            for k2, v2 in snap[1].items():
                if dclk.get(k2, 0) < v2:
                    dclk[k2] = v2
        self.gseq += 1
        snapshot = (dict(clk), dict(dclk))
        if tok[0] == "c":
            self.gseq_c[(tok[1], tok[2])] = self.gseq
            self.opclk[(tok[1], tok[2])] = snapshot
        else:
            self.gseq_d[(tok[1], tok[2])] = self.gseq
            self.dmaclk[(tok[1], tok[2])] = snapshot
        srckey = tok[1]
        for k in reads:
            self.readers.setdefault(k, {})[srckey] = tok
            self.bufkeys.setdefault(self._bufname(k), set()).add(k)
        for k in writes:
            self.lastw[k] = tok
            self.readers[k] = {}
            self.bufkeys.setdefault(self._bufname(k), set()).add(k)

    DEF_W = {"pe": 256, "act": 384, "dve": 384, "pool": 512, "sp": 0}

    def op(self, eng, fn, reads=(), writes=(), w=None):
        self.pending.append(("c", eng, None, fn, tuple(reads), tuple(writes), w, False))
        if self.read_hook is not None:
            self.read_hook(reads)

    def dma(self, eng, semname, fn, reads=(), writes=(), w=None, is_out=False):
        self.pending.append(("d", eng, semname, fn, tuple(reads), tuple(writes), w, is_out))

    class _Fake:
        def __init__(self):
            self.call = None

        def __getattr__(self, name):
            def f(*a, **k):
                self.call = (name, a, k)
                return self
            return f

    @staticmethod
    def _fsize(ap):
        n = 1
        for d in ap.shape[1:]:
            n *= d
        return n

    ACT_CLS = None

    def _probe(self, fn, want_cls=False):
        fk = Prog._Fake()
        try:
            fn(fk)
            name, a, k = fk.call
            if want_cls:
                if name != "activation":
                    return None
                f = k.get("func")
                if f in (AF.Exp, AF.Ln):
                    return "E"
                if f in (AF.Silu, AF.Tanh):
                    return "U"
                if f == AF.Sigmoid:
                    return "S"
                return None
            if name == "matmul":
                rhs = k.get("rhs", a[2] if len(a) > 2 else None)
                lhsT = k.get("lhsT", a[1] if len(a) > 1 else None)
                w = self._fsize(rhs)
                if lhsT.dtype == F32:
                    w *= 4
                return w
            if name == "dma_start":
                out = k.get("out", a[0] if a else None)
                nb = self._fsize(out) * out.shape[0] * (4 if out.dtype == F32 else 2)
                return nb / 150e3
            out = k.get("out", a[0] if a else None)
            return self._fsize(out)
        except Exception:
            return None

    def _dur(self, rec):
        kind, eng, _, fn, _, _, w, _ = rec
        if w is None:
            w = self._probe(fn)
        if kind == "d":
            return 0.08, 2.5 + (w if w is not None else 1.0)
        if w is None:
            w = self.DEF_W[eng]
        if eng == "pe":
            d = max(0.06, w / 2350.0 + 0.012)
        elif eng == "act":
            d = 0.20 + w / 1250.0
        elif eng == "dve":
            d = 0.09 + w / 960.0
        else:
            d = 0.6 + w / 600.0
        return d, d

    def flush(self):
        recs = self.pending
        self.pending = []
        n = len(recs)
        if n == 0:
            return
        if not self.do_schedule:
            order = range(n)
        else:
            order = self._schedule(recs)
        for i in order:
            kind, eng, semname, fn, reads, writes, w, is_out = recs[i]
            if kind == "c":
                self._op_now(eng, fn, reads, writes)
            else:
                tok = self._dma_now(eng, semname, fn, reads, writes)
                if is_out:
                    self.out_tokens.append(tok)

    def _schedule(self, recs):
        import heapq
        n = len(recs)
        preds = [set() for _ in range(n)]
        lastw = {}
        readers = {}
        for i, r in enumerate(recs):
            for k in r[4]:
                j = lastw.get(k)
                if j is not None:
                    preds[i].add(j)
            for k in r[5]:
                j = lastw.get(k)
                if j is not None:
                    preds[i].add(j)
                for j in readers.get(k, ()):
                    preds[i].add(j)
            for k in r[4]:
                readers.setdefault(k, []).append(i)
            for k in r[5]:
                lastw[k] = i
                readers[k] = []
            preds[i].discard(i)
        lastsem = {}
        for i, r in enumerate(recs):
            if r[0] == "d":
                j = lastsem.get(r[2])
                if j is not None:
                    preds[i].add(j)
                lastsem[r[2]] = i
        succs = [[] for _ in range(n)]
        indeg = [0] * n
        for i in range(n):
            indeg[i] = len(preds[i])
            for j in preds[i]:
                succs[j].append(i)
        durs = [self._dur(r) for r in recs]
        acls = [self._probe(r[3], want_cls=True) if r[1] == "act" and r[0] == "c" else None for r in recs]
        cur_cls = [None]
        blevel = [0.0] * n
        for i in range(n - 1, -1, -1):
            b = 0.0
            for k in succs[i]:
                if blevel[k] > b:
                    b = blevel[k]
            blevel[i] = b + durs[i][1] + 0.35
        finish = [0.0] * n
        ready_t = [0.0] * n
        heaps = {e: [] for e in ENGS}
        for i in range(n):
            if indeg[i] == 0:
                heapq.heappush(heaps[recs[i][1]], (0.0, i))
        free = {e: 0.0 for e in ENGS}
        order = []
        LAT = 0.35
        while len(order) < n:
            best = None
            for e in ENGS:
                h = heaps[e]
                if not h:
                    continue
                rt, i = h[0]
                stt = max(rt, free[e])
                if best is None or (stt, i) < (best[0], best[2]):
                    best = (stt, e, i)
            stt, e, i = best
            h = heaps[e]
            slack = 1.0 if e == "act" else 0.05
            cand = [x for x in h if x[0] <= stt + slack]
            if len(cand) > 1:
                if e == "act":
                    pick = max(cand, key=lambda x: (0 if (acls[x[1]] is not None and acls[x[1]] != cur_cls[0]) else 1,
                                                    1 if x[0] <= stt + 0.05 else 0, blevel[x[1]], -x[1]))
                else:
                    pick = max(cand, key=lambda x: (blevel[x[1]], -x[1]))
                h.remove(pick)
                heapq.heapify(h)
                i = pick[1]
                stt = max(stt, pick[0])
            else:
                heapq.heappop(h)
            busy, lat = durs[i]
            if e == "act" and acls[i] is not None:
                if cur_cls[0] is not None and acls[i] != cur_cls[0]:
                    busy += 1.3
                    lat += 1.3
                cur_cls[0] = acls[i]
            free[e] = stt + busy
            finish[i] = stt + lat
            order.append(i)
            for k in succs[i]:
                indeg[k] -= 1
                t = finish[i] + LAT
                if t > ready_t[k]:
                    ready_t[k] = t
                if indeg[k] == 0:
                    heapq.heappush(heaps[recs[k][1]], (ready_t[k], k))
        self.sched_span = getattr(self, "sched_span", 0.0) + max(finish)
        return order

    def _op_now(self, eng, fn, reads=(), writes=()):
        o = Op(eng, len(self.ops[eng]), fn)
        self.ops[eng].append(o)
        tok = ("c", eng, o.idx)
        self._record(o, tok, reads, writes)
        return tok

    def _dma_now(self, eng, semname, fn, reads=(), writes=()):
        if semname not in self.dma_sems:
            self.dma_sems[semname] = [self.nc.alloc_semaphore("d_" + semname), 0]
        ent = self.dma_sems[semname]
        o = Op(eng, len(self.ops[eng]), fn)
        o.dma_sem = ent[0]
        self.ops[eng].append(o)
        self.dma_prev[semname] = ent[1]
        ent[1] += 16
        tok = ("d", semname, ent[1])
        self._record(o, tok, reads, writes)
        return tok

    def collect(self, bufname):
        assert not self.pending
        best = {}
        for k in self.bufkeys.get(bufname, ()):
            toks = list(self.readers.get(k, {}).values())
            if k in self.lastw:
                toks.append(self.lastw[k])
            for t in toks:
                key = (t[0], t[1])
                if key not in best or best[key][2] < t[2]:
                    best[key] = t
            self.lastw.pop(k, None)
            self.readers.pop(k, None)
        self.bufkeys.pop(bufname, None)
        self.inherit.pop(bufname, None)
        return list(best.values())

    def final_wait(self, eng, toks):
        self.flush()
        o = Op(eng, len(self.ops[eng]), None)
        self.ops[eng].append(o)
        for t in toks:
            if t[0] == "c":
                if o.deps.get(t[1], -1) < t[2]:
                    o.deps[t[1]] = t[2]
                    self.ops[t[1]][t[2]].signal = True
            else:
                if o.dma_waits.get(t[1], 0) < t[2]:
                    o.dma_waits[t[1]] = t[2]

    def emit(self):
        nc = self.nc
        for e in ENGS:
            if self.ops[e]:
                self.esem[e] = nc.alloc_semaphore("e_" + e)
        for e in ENGS:
            c = 0
            for o in self.ops[e]:
                if o.signal:
                    c += 1
                o.val = c
        engobj = {"pe": "tensor", "act": "scalar", "dve": "vector", "pool": "gpsimd", "sp": "sync"}
        self.n_inst = {e: len(self.ops[e]) for e in ENGS}
        self.n_wait = {e: 0 for e in ENGS}
        with nc.Block() as block:
            for e in ENGS:
                if not self.ops[e]:
                    continue

                def body(eng, e=e):
                    for o in self.ops[e]:
                        waits = [(self.esem[se], self.ops[se][si].val) for se, si in o.deps.items()]
                        waits += [(self.dma_sems[sn][0], val) for sn, val in o.dma_waits.items()]
                        self.n_wait[e] += len(waits)
                        if o.fn is None:
                            for sem, val in waits:
                                eng.wait_ge(sem, val)
                            continue
                        for sem, val in waits[:-1]:
                            eng.wait_ge(sem, val)
                        ins = o.fn(eng)
                        if waits:
                            ins._wait_ge(*waits[-1])
                        if o.dma_sem is not None:
                            ins.then_inc(o.dma_sem, 16)
                        elif o.signal:
                            ins.then_inc(self.esem[e], 1)

                getattr(block, engobj[e])(body)


class Arena:
    def __init__(self, nc, P, base, size):
        self.nc, self.P = nc, P
        self.base, self.size = base, size
        self.top = 0
        self.live = []
        self.freed = []
        self.uid = 0

    def alloc(self, name, shape, dtype):
        self.P.flush()
        self.uid += 1
        nm = f"{name}_{self.uid}"
        esz = 4 if dtype == F32 else 2
        n = 1
        for s in shape[1:]:
            n *= s
        nbytes = (n * esz + 63) // 64 * 64
        off = self.top
        assert off + nbytes <= self.size, f"SBUF arena overflow allocating {name}: {off + nbytes} > {self.size}"
        self.top += nbytes
        self.peak_top = max(getattr(self, "peak_top", 0), self.top)
        h = self.nc.alloc_sbuf_tensor_at(nm, list(shape), dtype, offset=self.base + off)
        toks = []
        for (fo, fn_, ft) in self.freed:
            if fo < off + nbytes and off < fo + fn_:
                toks.extend(ft)
        if toks:
            self.P.inherit[nm] = toks
        self.live.append((nm, off, nbytes))
        return nm, h.ap()

    def mark(self):
        return (self.top, len(self.live))

    def release(self, mark):
        self.P.flush()
        top, nlive = mark
        for (nm, off, nbytes) in self.live[nlive:]:
            toks = self.P.collect(nm)
            self.freed.append((off, nbytes, toks))
        del self.live[nlive:]
        self.top = top
        if len(self.freed) > 64:
            allt = {}
            lo = min(f[0] for f in self.freed)
            hi = max(f[0] + f[1] for f in self.freed)
            for f in self.freed:
                for t in f[2]:
                    key = (t[0], t[1])
                    if key not in allt or allt[key][2] < t[2]:
                        allt[key] = t
            self.freed = [(lo, hi - lo, list(allt.values()))]


def _cols(v):
    v = np.asarray(v, np.float32)
    return np.ascontiguousarray(v.reshape(-1, 128).T)


def _rep(v):
    v = np.asarray(v, np.float32).reshape(1, -1)
    return np.ascontiguousarray(np.repeat(v, 128, axis=0))


def pack_consts(inp):
    cols = []
    big = []
    off = {}
    cur = [0]

    def add(name, arr):
        off[name] = (cur[0], arr.shape[1])
        cols.append(arr)
        cur[0] += arr.shape[1]

    for l in range(4):
        add(f"g_mix{l}", _cols(inp["norm_mix_w"][l]))
        add(f"g_xat{l}", _cols(inp["norm_xattn_w"][l]))
        add(f"g_mlp{l}", _cols(inp["norm_mlp_w"][l]))
    add("g_fin", _cols(inp["final_norm_w"]))
    add("g_mem", _cols(inp["mem_norm_w"]))
    for e in range(2):
        add(f"lbl{e}", _cols(inp["hgrn_lb_logits"][e]))
        add(f"onw{e}", _cols(inp["hgrn_out_norm_w"][e]))
        cw = np.asarray(inp["ssd_conv_w"][e], np.float32)
        add(f"scw{e}", np.concatenate([_cols(cw[j]) for j in range(4)], axis=1))
        add(f"scb{e}", _cols(inp["ssd_conv_b"][e]))
        add(f"dtb{e}", _rep(inp["ssd_dt_bias"][e]))
        add(f"alog{e}", _rep(inp["ssd_a_log"][e]))
        big.append((f"dsk{e}", _rep(np.repeat(np.asarray(inp["ssd_d"][e], np.float32), 64))))
        big.append((f"snw{e}", _rep(inp["ssd_norm_w"][e])))
    for o in range(2):
        add(f"bpw1{o}", _cols(inp["cv_b_pw1"][o]))
        wd = np.asarray(inp["cv_w_dw"][o], np.float32)
        add(f"wdw{o}", np.concatenate([_cols(wd[j]) for j in range(31)], axis=1))
        add(f"bdw{o}", _cols(inp["cv_b_dw"][o]))
        add(f"lnw{o}", _cols(inp["cv_ln_w"][o]))
        add(f"lnb{o}", _cols(inp["cv_ln_b"][o]))
        add(f"bpw2{o}", _cols(inp["cv_b_pw2"][o]))
    off["_nsmall"] = (cur[0], 0)
    for name, arr in big:
        add(name, arr)
    return np.ascontiguousarray(np.concatenate(cols, axis=1)), off


def struct_consts():
    p = np.arange(128)[:, None]
    j = np.arange(128)[None, :]
    ident = (p == j).astype(np.float32)
    tri = (p <= j).astype(np.float32)
    scanmask = np.ones((128, 512), np.float32)
    scanmask[:, ::64] = 0.0
    ones = np.ones((128, 128), np.float32)
    arr = np.concatenate([ident, tri, scanmask, ones], axis=1)
    off = {"ident": (0, 128), "tri": (128, 128), "scanmask": (256, 512), "ones": (768, 128)}
    return np.ascontiguousarray(arr), off


WEIGHT_NAMES = ["ab_w_in", "ab_w_out", "cv_w_pw1", "cv_w_pw2", "xattn_wq", "xattn_wk", "xattn_wv",
                "xattn_wo", "mlp_w1", "mlp_w2"]
WEIGHT_SHAPES = {"ab_w_in": [2, 1024, 3592], "ab_w_out": [2, 1024, 1024], "cv_w_pw1": [2, 1024, 2048],
                 "cv_w_pw2": [2, 1024, 1024], "xattn_wq": [4, 1024, 1024], "xattn_wk": [4, 1024, 1024],
                 "xattn_wv": [4, 1024, 1024], "xattn_wo": [4, 1024, 1024], "mlp_w1": [4, 1024, 4096],
                 "mlp_w2": [4, 4096, 1024]}


class Builder:
    def __init__(self, nc, coff, soff, ncc, nsc):
        self.nc = nc
        self.P = Prog(nc)
        P = self.P
        self.coff, self.soff = coff, soff
        self.xT = nc.dram_tensor("xT", [D, T], F32, kind="ExternalInput").ap()
        self.memT = nc.dram_tensor("memT", [D, MEM], F32, kind="ExternalInput").ap()
        self.cst_d = nc.dram_tensor("consts", [128, ncc], F32, kind="ExternalInput").ap()
        self.sct_d = nc.dram_tensor("sconsts", [128, nsc], F32, kind="ExternalInput").ap()
        self.W = {n: nc.dram_tensor(n, WEIGHT_SHAPES[n], F32, kind="ExternalInput").ap() for n in WEIGHT_NAMES}
        self.yT = nc.dram_tensor("yT", [D, T], F32, kind="ExternalOutput").ap()
        self.scr = [nc.dram_tensor(f"scr{i}", [D, T], F32, kind="Internal").ap() for i in range(2)]
        total = nc.sbuf_bytes_remaining
        nsm = coff["_nsmall"][0]
        self.cst = nc.alloc_sbuf_tensor("cst", [128, nsm], F32).ap()
        self.sct = nc.alloc_sbuf_tensor("sct", [128, nsc], F32).ap()
        self.ones_bf = nc.alloc_sbuf_tensor("ones_bf", [128, 128], BF16).ap()
        self.ident_bf = nc.alloc_sbuf_tensor("ident_bf", [128, 128], BF16).ap()
        self.mask64 = nc.alloc_sbuf_tensor("mask64", [64, 64], F32).ap()
        self.dv = nc.alloc_sbuf_tensor("derived", [128, 64], F32).ap()
        probe = nc.alloc_sbuf_tensor("arena_probe", [128, 16], F32)
        self.arena_base = nc.lookup_mloc(probe).addr + 64
        self.A = Arena(nc, P, self.arena_base, nc.SBUF_PARTITION_SIZE_BYTES - self.arena_base)
        self.banks = [nc.alloc_psum_tensor(f"ps{i}", [128, 512], F32).ap() for i in range(8)]
        self.bank_i = 0
        self.bfhalf = 0
        self.busy = {}
        self.g_i = 0
        self.gbf_i = 0
        P.read_hook = self._on_reads
        self.wsem = 0
        P.dma("sp", "cst", lambda e: e.dma_start(out=self.cst, in_=self.cst_d[:, 0:nsm]), writes=["cst"])
        P.dma("sp", "sct", lambda e: e.dma_start(out=self.sct, in_=self.sct_d), writes=["sct"])
        so = soff
        P.op("dve", lambda e: e.tensor_copy(out=self.ones_bf, in_=self.S("ones")), reads=["sct"], writes=["ones_bf"])
        P.op("dve", lambda e: e.tensor_copy(out=self.ident_bf, in_=self.S("ident")), reads=["sct"], writes=["ident_bf"])
        P.op("dve", lambda e: e.tensor_copy(out=self.mask64, in_=self.sct[0:64, so["tri"][0]:so["tri"][0] + 64]),
             reads=["sct"], writes=["mask64"])

    def C(self, name, lo=0, n=None):
        o, w = self.coff[name]
        if n is None:
            n = w - lo
        return self.cst[:, o + lo:o + lo + n]

    def S(self, name):
        o, w = self.soff[name]
        return self.sct[:, o:o + w]

    def bank(self):
        i = self.bank_i
        self.bank_i = (i + 1) % 5
        return ("ps", i), self.banks[i]

    def _on_reads(self, reads):
        for k in reads:
            if k in self.busy:
                self.busy[k] -= 1
                if self.busy[k] <= 0:
                    del self.busy[k]

    def gbank(self, n_reads=1):
        for d in range(8):
            i = (self.g_i + d) % 8
            if ("ps", i) not in self.busy:
                self.g_i = (i + 1) % 8
                self.busy[("ps", i)] = n_reads
                return ("ps", i), self.banks[i]
        raise RuntimeError("no free PSUM bank")

    def load_w(self, name, wd, kchunks, ncols, colblk=512, c0=0, defer=False):
        nm, w = self.A.alloc(name, [128, kchunks, ncols], BF16)
        src = wd.rearrange("(k p) n -> p k n", p=128)
        nblk = (ncols + colblk - 1) // colblk
        nk = (kchunks + 7) // 8

        def keys(col_lo, col_hi, k=None):
            bl = range(col_lo // colblk, (col_hi - 1) // colblk + 1)
            if k is None:
                return [(nm, b, kk) for b in bl for kk in range(nk)]
            return [(nm, b, k // 8) for b in bl]

        def issue():
            self._issue_w(nm, w, src, kchunks, ncols, colblk, c0, nblk)
        if defer:
            return w, keys, issue
        issue()
        return w, keys

    def _issue_w(self, nm, w, src, kchunks, ncols, colblk, c0, nblk):
        for b in range(nblk):
            lo = b * colblk
            hi = min(ncols, lo + colblk)
            kstep = max(1, min(kchunks, 8))
            for k0 in range(0, kchunks, kstep):
                self.wsem = (self.wsem + 1) % 8
                self.P.dma("pool", f"w{self.wsem}",
                           lambda e, lo=lo, hi=hi, k0=k0, kstep=kstep: e.dma_start(
                               out=w[:, k0:k0 + kstep, lo:hi], in_=src[:, k0:k0 + kstep, c0 + lo:c0 + hi]),
                           writes=[(nm, b, k0 // kstep)])

    def rms_rstd(self, x, xkey, n, sq, sqn, rstd, rstdn, ncols=TB, bankfn=None):
        P = self.P
        kb, ps = self.bank() if bankfn is None else bankfn(1)
        for c in range(n):
            j = c % 2
            P.op("act", lambda e, c=c, j=j: e.activation(out=sq[:, j, 0:ncols], in_=x[:, c, 0:ncols], func=AF.Square),
                 reads=[xkey(c) if callable(xkey) else xkey], writes=[(sqn, j)])
            P.op("pe", lambda e, c=c, j=j: e.matmul(ps[:, 0:ncols], self.ones_bf, sq[:, j, 0:ncols], start=(c == 0), stop=(c == n - 1)),
                 reads=[(sqn, j), "ones_bf"], writes=[kb])
        P.op("act", lambda e: e.activation(out=rstd[:, 0:ncols], in_=ps[:, 0:ncols], func=AF.Ln, scale=1.0 / (n * 128), bias=EPS),
             reads=[kb], writes=[rstdn])
        P.op("act", lambda e: e.activation(out=rstd[:, 0:ncols], in_=rstd[:, 0:ncols], func=AF.Exp, scale=-0.5),
             reads=[rstdn], writes=[rstdn])

    def make_h(self, x, xkey, gain, rstd, rstdn, hT, hTn, ncols=TB):
        for c in range(KC):
            self.P.op("dve", lambda e, c=c: e.scalar_tensor_tensor(
                out=hT[:, c, 0:ncols], in0=x[:, c, 0:ncols], scalar=gain[:, c:c + 1], in1=rstd[:, 0:ncols],
                op0=ALU.mult, op1=ALU.mult), reads=[xkey, rstdn, "cst"], writes=[(hTn, c)])

    def proj(self, ps, kb, w, wkeys, col, hT, hTn, ncols=TB, m=128):
        for k in range(KC):
            self.P.op("pe", lambda e, k=k: e.matmul(ps[0:m, 0:ncols], w[:, k, col:col + m], hT[:, k, 0:ncols],
                                                     start=(k == 0), stop=(k == KC - 1)),
                      reads=wkeys(col, col + m) + [(hTn, k)], writes=[kb])

    def xstore(self, st, c, buf, bufname, sem):
        d = self.dst[c * 128:(c + 1) * 128, st * TB:(st + 1) * TB]
        self.P.dma("sp", sem, lambda e: e.dma_start(out=d, in_=buf), reads=[bufname], writes=[("dr", self.dst_id, st, c)],
                   is_out=(self.dst_id == "y"))

    def begin_stage(self, src, src_id, dst, dst_id):
        self.src, self.src_id, self.dst, self.dst_id = src, src_id, dst, dst_id
        self.mark = self.A.mark()

    def end_stage(self):
        self.peak = getattr(self, "peak", {})
        self.A.release(self.mark)

    def xload_keys(self, st):
        return [("dr", self.src_id, st, c) for c in range(KC)]

    def stage_mlp(self, l, src, src_id, dst, dst_id):
        P, A = self.P, self.A
        self.begin_stage(src, src_id, dst, dst_id)
        w1, k1 = self.load_w("w1", self.W["mlp_w1"][l], 8, DFF)
        w2, k2 = self.load_w("w2", self.W["mlp_w2"][l], 32, D, colblk=1024)
        xn, xin = A.alloc("xin", [128, KC, TB], F32)
        hn, hT = A.alloc("hT", [128, KC, TB], BF16)
        hidn, hid = A.alloc("hid", [128, 32, TB], BF16)
        sqn, sq = A.alloc("sq", [128, 2, TB], BF16)
        rn, rstd = A.alloc("rstd", [128, TB], F32)
        tn, tmp = A.alloc("tmp", [128, 2, TB], BF16)
        xrn, xr = A.alloc("xr", [128, 2, TB], F32)
        gain = self.C(f"g_mlp{l}")
        srcv = src.rearrange("(c p) t -> p c t", p=128)
        xri = 0
        def ld(st):
            P.dma("sp", "xin0", lambda e, st=st: e.dma_start(out=xin, in_=srcv[:, :, st * TB:(st + 1) * TB]),
                  reads=self.xload_keys(st), writes=[xn])
        ld(0)
        for st in range(NST):
            self.rms_rstd(xin, xn, KC, sq, sqn, rstd, rn)
            self.make_h(xin, xn, gain, rstd, rn, hT, hn)
            if st + 1 < NST:
                ld(st + 1)
            for f in range(32):
                kb, ps = self.bank()
                self.proj(ps, kb, w1, k1, f * 128, hT, hn)
                j = f % 2
                P.op("act", lambda e, ps=ps, j=j: e.activation(out=tmp[:, j, :], in_=ps, func=AF.Relu),
                     reads=[kb], writes=[(tn, j)])
                P.op("dve", lambda e, f=f, j=j: e.tensor_tensor(out=hid[:, f, :], in0=tmp[:, j, :], in1=tmp[:, j, :], op=ALU.mult),
                     reads=[(tn, j)], writes=[(hidn, f)])
            for o in range(KC):
                kb, ps = self.bank()
                for f in range(32):
                    P.op("pe", lambda e, f=f, o=o, ps=ps: e.matmul(ps, w2[:, f, o * 128:(o + 1) * 128], hid[:, f, :],
                                                                  start=(f == 0), stop=(f == 31)),
                         reads=k2(o * 128, (o + 1) * 128, f) + [(hidn, f)], writes=[kb])
                j = xri % 2
                xri += 1
                P.dma("sp", f"xrl{j}", lambda e, st=st, o=o, j=j: e.dma_start(
                    out=xr[:, j, :], in_=src[o * 128:(o + 1) * 128, st * TB:(st + 1) * TB]),
                    reads=[("dr", src_id, st, o)], writes=[(xrn, j)])
                P.op("dve", lambda e, ps=ps, j=j: e.tensor_tensor(out=xr[:, j, :], in0=ps, in1=xr[:, j, :], op=ALU.add),
                     reads=[kb, (xrn, j)], writes=[(xrn, j)])
                self.xstore(st, o, xr[:, j, :], (xrn, j), f"xrs{j}")
        self.end_stage()

    def stage_final(self, src, src_id, dst, dst_id):
        P, A = self.P, self.A
        self.begin_stage(src, src_id, dst, dst_id)
        xn, xin = A.alloc("xin", [128, 2, KC, TB], F32)
        sqn, sq = A.alloc("sq", [128, 2, TB], BF16)
        rn, rstd = A.alloc("rstd", [128, TB], F32)
        gain = self.C("g_fin")
        srcv = src.rearrange("(c p) t -> p c t", p=128)
        for st in range(NST):
            b = st % 2
            P.dma("sp", f"xin{b}", lambda e, st=st, b=b: e.dma_start(out=xin[:, b], in_=srcv[:, :, st * TB:(st + 1) * TB]),
                  reads=self.xload_keys(st), writes=[(xn, b)] + [(xn, b, c) for c in range(KC)])
            self.rms_rstd(xin[:, b], (xn, b), KC, sq, sqn, rstd, rn)
            for c in range(KC):
                P.op("dve", lambda e, c=c, b=b: e.scalar_tensor_tensor(
                    out=xin[:, b, c, :], in0=xin[:, b, c, :], scalar=gain[:, c:c + 1], in1=rstd,
                    op0=ALU.mult, op1=ALU.mult), reads=[(xn, b), rn, "cst"], writes=[(xn, b, c)])
                self.xstore(st, c, xin[:, b, c, :], (xn, b, c), f"fs{c % 4}")
        self.end_stage()

    def stage_xattn(self, l, src, src_id, dst, dst_id):
        P, A = self.P, self.A
        self.begin_stage(src, src_id, dst, dst_id)
        wq, kq, issue_q = self.load_w("wq", self.W["xattn_wq"][l], 8, D, defer=True)
        wo, ko, issue_o = self.load_w("wo", self.W["xattn_wo"][l], 8, D, defer=True)
        ktn, KT = A.alloc("KT", [128, KC, MEM], BF16)
        vn, V = A.alloc("V", [128, 2, D], BF16)
        sqn, sq = A.alloc("sq", [128, 2, TB], BF16)
        rn, rstd = A.alloc("rstd", [128, TB], F32)
        m2 = A.mark()
        wk, kk = self.load_w("wk", self.W["xattn_wk"][l], 8, D)
        wv, kv = self.load_w("wv", self.W["xattn_wv"][l], 8, D)
        issue_q()
        issue_o()
        mn_, mem = A.alloc("mem", [128, KC, MEM], F32)
        mnn, mnT = A.alloc("mnT", [128, KC, MEM], BF16)
        P.dma("sp", "xin0", lambda e: e.dma_start(out=mem, in_=self.memT.rearrange("(c p) t -> p c t", p=128)), writes=[mn_])
        self.rms_rstd(mem, mn_, KC, sq, sqn, rstd, rn, ncols=MEM)
        self.make_h(mem, mn_, self.C("g_mem"), rstd, rn, mnT, mnn, ncols=MEM)
        for n in range(KC):
            kb, ps = self.bank()
            self.proj(ps, kb, wk, kk, n * 128, mnT, mnn, ncols=MEM)
            P.op("act", lambda e, n=n, ps=ps: e.activation(out=KT[:, n, :], in_=ps[:, 0:MEM], func=AF.Copy),
                 reads=[kb], writes=[(ktn, n)])
        for mt in range(2):
            for nb in range(2):
                kb, ps = self.bank()
                for k in range(KC):
                    P.op("pe", lambda e, k=k, mt=mt, nb=nb, ps=ps: e.matmul(
                        ps, mnT[:, k, mt * 128:(mt + 1) * 128], wv[:, k, nb * 512:(nb + 1) * 512],
                        start=(k == 0), stop=(k == KC - 1)), reads=kv(nb * 512, (nb + 1) * 512) + [(mnn, k)], writes=[kb])
                P.op("act", lambda e, mt=mt, nb=nb, ps=ps: e.activation(out=V[:, mt, nb * 512:(nb + 1) * 512], in_=ps, func=AF.Copy),
                     reads=[kb], writes=[(vn, mt, nb)])
        A.release(m2)
        xn, xin = A.alloc("xin", [128, 2, KC, TB], F32)
        hT2 = [A.alloc(f"hT{i}", [128, KC, TB], BF16) for i in range(2)]
        qT2 = [A.alloc(f"qT{i}", [128, KC, TB], BF16) for i in range(2)]
        en, E = A.alloc("E", [128, 4, 2, TB], BF16)
        rdn, rden4 = A.alloc("rden", [128, 4, TB], F32)
        oT2 = [A.alloc(f"oT{i}", [128, KC, TB], BF16) for i in range(2)]
        sq2 = [A.alloc(f"sqx{i}", [128, 2, TB], BF16) for i in range(2)]
        rs2_ = [A.alloc(f"rstdx{i}", [128, TB], F32) for i in range(2)]
        xrn, xr = A.alloc("xr", [128, 3, TB], F32)
        gain = self.C(f"g_xat{l}")
        srcv = src.rearrange("(c p) t -> p c t", p=128)
        xri = 0

        def ld(st):
            b = st % 2
            P.dma("sp", f"xin{b}", lambda e: e.dma_start(out=xin[:, b], in_=srcv[:, :, st * TB:(st + 1) * TB]),
                  reads=self.xload_keys(st), writes=[(xn, b)])
        ld(0)
        for st in range(NST):
            b = st % 2
            if st + 1 < NST:
                ld(st + 1)
            xb = xin[:, b]
            (hn, hT), (qn, qT), (on, oT) = hT2[b], qT2[b], oT2[b]
            (sqn_, sq_), (rn_, rstd_) = sq2[b], rs2_[b]
            self.rms_rstd(xb, (xn, b), KC, sq_, sqn_, rstd_, rn_)
            self.make_h(xb, (xn, b), gain, rstd_, rn_, hT, hn)
            for n in range(KC):
                kb, ps = self.bank()
                self.proj(ps, kb, wq, kq, n * 128, hT, hn)
                P.op("act", lambda e, n=n, ps=ps, qT=qT: e.activation(out=qT[:, n, :], in_=ps, func=AF.Copy, scale=1.0 / 16.0),
                     reads=[kb], writes=[(qn, n)])
            for hd in range(4):
                eb = hd
                rden = rden4[:, hd]
                for mt in range(2):
                    kb, ps = self.bank()
                    for dc in range(2):
                        c = 2 * hd + dc
                        P.op("pe", lambda e, c=c, mt=mt, dc=dc, ps=ps, qT=qT: e.matmul(
                            ps, KT[:, c, mt * 128:(mt + 1) * 128], qT[:, c, :], start=(dc == 0), stop=(dc == 1)),
                            reads=[(ktn, c), (qn, c)], writes=[kb])
                    P.op("act", lambda e, eb=eb, mt=mt, ps=ps: e.activation(out=E[:, eb, mt, :], in_=ps, func=AF.Exp),
                         reads=[kb], writes=[(en, eb, mt)])
                kb, ps = self.bank()
                for mt in range(2):
                    P.op("pe", lambda e, eb=eb, mt=mt, ps=ps: e.matmul(ps, self.ones_bf, E[:, eb, mt, :], start=(mt == 0), stop=(mt == 1)),
                         reads=[(en, eb, mt), "ones_bf"], writes=[kb])
                P.op("dve", lambda e, ps=ps, rden=rden: e.reciprocal(out=rden, in_=ps), reads=[kb], writes=[(rdn, hd)])
                for dc in range(2):
                    c = 2 * hd + dc
                    kb, ps = self.bank()
                    for mt in range(2):
                        P.op("pe", lambda e, c=c, mt=mt, eb=eb, ps=ps: e.matmul(
                            ps, V[:, mt, c * 128:(c + 1) * 128], E[:, eb, mt, :], start=(mt == 0), stop=(mt == 1)),
                            reads=[(vn, mt, c // 4), (en, eb, mt)], writes=[kb])
                    P.op("dve", lambda e, c=c, ps=ps, rden=rden, oT=oT: e.tensor_tensor(out=oT[:, c, :], in0=ps, in1=rden, op=ALU.mult),
                         reads=[kb, (rdn, hd)], writes=[(on, c)])
            for o in range(KC):
                kb, ps = self.bank()
                self.proj(ps, kb, wo, ko, o * 128, oT, on)
                j = xri % 3
                xri += 1
                P.op("dve", lambda e, ps=ps, j=j, o=o, xb=xb: e.tensor_tensor(out=xr[:, j, :], in0=ps, in1=xb[:, o, :], op=ALU.add),
                     reads=[kb, (xn, b)], writes=[(xrn, j)])
                self.xstore(st, o, xr[:, j, :], (xrn, j), f"xrs{j}")
        self.end_stage()

    def stage_conf(self, l, src, src_id, dst, dst_id):
        P, A = self.P, self.A
        o_ = l // 2
        self.begin_stage(src, src_id, dst, dst_id)
        w1, k1 = self.load_w("pw1", self.W["cv_w_pw1"][o_], 8, 2 * D)
        w2, k2 = self.load_w("pw2", self.W["cv_w_pw2"][o_], 8, D)
        NPE = 20
        dgn, dg = A.alloc("diag", [128, NPE * KC, 128], BF16)
        wdw = self.C(f"wdw{o_}")
        for i in range(NPE * KC):
            if i % 2 == 0:
                P.op("dve", lambda e, i=i: e.tensor_scalar(out=dg[:, i, :], in0=self.S("ident"), scalar1=wdw[:, i:i + 1], scalar2=None,
                                                           op0=ALU.mult), reads=["sct", "cst"], writes=[(dgn, i)])
            else:
                P.op("act", lambda e, i=i: e.activation(out=dg[:, i, :], in_=self.S("ident"), func=AF.Copy, scale=wdw[:, i:i + 1]),
                     reads=["sct", "cst"], writes=[(dgn, i)])
        xn, xin = A.alloc("xin", [128, 2, KC, TB], F32)
        hn, hT = A.alloc("hT", [128, KC, TB], BF16)
        sqn, sq = A.alloc("sq", [128, 2, TB], BF16)
        rn, rstd = A.alloc("rstd", [128, TB], F32)
        un, ub = A.alloc("ubuf", [128, KC, 30 + TB], BF16)
        sgn, sg = A.alloc("sig", [128, 2, TB], F32)
        vbn, vb = A.alloc("vbuf", [128, KC, TB], F32)
        can, cacc = A.alloc("cacc", [128, 4, TB], F32)
        mnn, mean = A.alloc("mean", [128, TB], F32)
        r2n, rs2 = A.alloc("rs2", [128, TB], F32)
        sTn, sT = A.alloc("sT", [128, KC, TB], BF16)
        xrn, xr = A.alloc("xr", [128, 2, TB], F32)
        gain = self.C(f"g_mix{l}")
        bpw1 = self.C(f"bpw1{o_}")
        bdw = self.C(f"bdw{o_}")
        lnw = self.C(f"lnw{o_}")
        lnb = self.C(f"lnb{o_}")
        bpw2 = self.C(f"bpw2{o_}")
        srcv = src.rearrange("(c p) t -> p c t", p=128)
        for c in range(KC):
            P.op("pool", lambda e, c=c: e.memset(ub[:, c, 0:30], 0.0), writes=[(un, c)])
        xri = 0

        def ld(st):
            b = st % 2
            P.dma("sp", f"xin{b}", lambda e: e.dma_start(out=xin[:, b], in_=srcv[:, :, st * TB:(st + 1) * TB]),
                  reads=self.xload_keys(st), writes=[(xn, b)])
        ld(0)
        for st in range(NST):
            b = st % 2
            if st + 1 < NST:
                ld(st + 1)
            xb = xin[:, b]
            self.rms_rstd(xb, (xn, b), KC, sq, sqn, rstd, rn)
            self.make_h(xb, (xn, b), gain, rstd, rn, hT, hn)
            for c in range(KC):
                kg, psg = self.bank()
                self.proj(psg, kg, w1, k1, D + c * 128, hT, hn)
                j = c % 2
                P.op("act", lambda e, c=c, j=j, psg=psg: e.activation(out=sg[:, j, :], in_=psg, func=AF.Sigmoid,
                                                                      bias=bpw1[:, KC + c:KC + c + 1]),
                     reads=[kg, "cst"], writes=[(sgn, j)])
                ka, psa = self.bank()
                self.proj(psa, ka, w1, k1, c * 128, hT, hn)
                if st > 0:
                    P.op("pool", lambda e, c=c: e.tensor_copy(out=ub[:, c, 0:30], in_=ub[:, c, TB:TB + 30]),
                         reads=[(un, c)], writes=[(un, c)])
                P.op("dve", lambda e, c=c, j=j, psa=psa: e.scalar_tensor_tensor(
                    out=ub[:, c, 30:30 + TB], in0=psa, scalar=bpw1[:, c:c + 1], in1=sg[:, j, :], op0=ALU.add, op1=ALU.mult),
                    reads=[ka, (sgn, j), "cst"], writes=[(un, c)])
            for g4 in range(KC // 4):
                cs4 = range(g4 * 4, g4 * 4 + 4)
                pbs = {}
                for c in cs4:
                    kb, ps = self.bank()
                    pbs[c] = (kb, ps)
                    for j in range(NPE):
                        P.op("pe", lambda e, c=c, j=j, ps=ps: e.matmul(ps, dg[:, j * KC + c, :], ub[:, c, j:j + TB],
                                                                      start=(j == 0), stop=(j == NPE - 1)),
                             reads=[(dgn, j * KC + c), (un, c)], writes=[kb])
                for c in cs4:
                    aj = c % 4
                    P.op("dve", lambda e, c=c, aj=aj: e.tensor_scalar(out=cacc[:, aj, :], in0=ub[:, c, NPE:NPE + TB],
                                                                      scalar1=wdw[:, NPE * KC + c:NPE * KC + c + 1], scalar2=None, op0=ALU.mult),
                         reads=[(un, c), "cst"], writes=[(can, aj)])
                for j in range(NPE + 1, 31):
                    for c in cs4:
                        aj = c % 4
                        P.op("dve", lambda e, c=c, j=j, aj=aj: e.scalar_tensor_tensor(out=cacc[:, aj, :], in0=ub[:, c, j:j + TB],
                                                                                      scalar=wdw[:, j * KC + c:j * KC + c + 1], in1=cacc[:, aj, :],
                                                                                      op0=ALU.mult, op1=ALU.add),
                             reads=[(un, c), (can, aj), "cst"], writes=[(can, aj)])
                for c in cs4:
                    aj = c % 4
                    kb, ps = pbs[c]
                    P.op("dve", lambda e, c=c, aj=aj, ps=ps: e.scalar_tensor_tensor(out=vb[:, c, :], in0=ps, scalar=bdw[:, c:c + 1], in1=cacc[:, aj, :],
                                                                                    op0=ALU.add, op1=ALU.add),
                         reads=[kb, (can, aj), "cst"], writes=[(vbn, c)])
            km, psm = self.bank()
            for c in range(KC):
                P.op("pe", lambda e, c=c, psm=psm: e.matmul(psm, self.S("ones"), vb[:, c, :], start=(c == 0), stop=(c == KC - 1)),
                     reads=[(vbn, c), "sct"], writes=[km])
            P.op("act", lambda e, psm=psm: e.activation(out=mean, in_=psm, func=AF.Copy, scale=1.0 / D), reads=[km], writes=[mnn])
            for c in range(KC):
                P.op("dve", lambda e, c=c: e.tensor_tensor(out=vb[:, c, :], in0=vb[:, c, :], in1=mean, op=ALU.subtract),
                     reads=[(vbn, c), mnn], writes=[(vbn, c)])
            self.rms_rstd(vb, lambda c: (vbn, c), KC, sq, sqn, rs2, r2n)
            for c in range(KC):
                P.op("dve", lambda e, c=c: e.tensor_tensor(out=vb[:, c, :], in0=vb[:, c, :], in1=rs2, op=ALU.mult),
                     reads=[(vbn, c), r2n], writes=[(vbn, c)])
                P.op("act", lambda e, c=c: e.activation(out=sT[:, c, :], in_=vb[:, c, :], func=AF.Silu,
                                                        scale=lnw[:, c:c + 1], bias=lnb[:, c:c + 1]),
                     reads=[(vbn, c), "cst"], writes=[(sTn, c)])
            for o in range(KC):
                kb, ps = self.bank()
                self.proj(ps, kb, w2, k2, o * 128, sT, sTn)
                j = xri % 2
                xri += 1
                P.op("dve", lambda e, ps=ps, j=j, o=o, xb=xb: e.scalar_tensor_tensor(
                    out=xr[:, j, :], in0=ps, scalar=bpw2[:, o:o + 1], in1=xb[:, o, :], op0=ALU.add, op1=ALU.add),
                    reads=[kb, (xn, b), "cst"], writes=[(xrn, j)])
                self.xstore(st, o, xr[:, j, :], (xrn, j), f"xrs{j}")
        self.end_stage()

    def stage_mixer(self, l, src, src_id, dst, dst_id):
        P, A = self.P, self.A
        e_ = l // 2
        TBm = 256
        NSTm = T // TBm
        NHC = TBm // 64
        NQ = TBm // 128
        self.begin_stage(src, src_id, dst, dst_id)
        win, kin = self.load_w("win", self.W["ab_w_in"][e_], 8, AB_IN)
        wout, kout = self.load_w("wout", self.W["ab_w_out"][e_], 8, D)
        gain = self.C(f"g_mix{l}")
        dvn = f"dv{l}"
        dv = self.dv
        lb = dv[:, 0:4]
        oml = dv[:, 4:8]
        Ab = dv[:, 8:16]
        if e_ == 0:
            P.op("pool", lambda e: e.memset(lb, 0.0), writes=[(dvn, "lb")])
        else:
            P.op("dve", lambda e: e.tensor_tensor(out=lb, in0=self.C("lbl1"), in1=self.C("lbl0"), op=ALU.subtract),
                 reads=["cst"], writes=[(dvn, "lb")])
            P.op("act", lambda e: e.activation(out=lb, in_=lb, func=AF.Sigmoid), reads=[(dvn, "lb")], writes=[(dvn, "lb")])
        P.op("dve", lambda e: e.tensor_scalar(out=oml, in0=lb, scalar1=-1.0, scalar2=1.0, op0=ALU.mult, op1=ALU.add),
             reads=[(dvn, "lb")], writes=[(dvn, "oml")])
        P.op("act", lambda e: e.activation(out=Ab, in_=self.C(f"alog{e_}"), func=AF.Exp), reads=["cst"], writes=[(dvn, "A")])
        P.op("dve", lambda e: e.tensor_scalar(out=Ab, in0=Ab, scalar1=-1.0, scalar2=None, op0=ALU.mult),
             reads=[(dvn, "A")], writes=[(dvn, "A")])
        rsq = float(1.0 / np.sqrt(128.0))
        homl = dv[:, 16:20]
        lbh = dv[:, 20:24]
        qsc = dv[:, 24:28]
        P.op("dve", lambda e: e.tensor_scalar(out=homl, in0=oml, scalar1=0.5, scalar2=None, op0=ALU.mult),
             reads=[(dvn, "oml")], writes=[(dvn, "homl")])
        P.op("dve", lambda e: e.tensor_tensor(out=lbh, in0=lb, in1=homl, op=ALU.add), reads=[(dvn, "lb"), (dvn, "homl")], writes=[(dvn, "lbh")])
        P.op("act", lambda e: e.activation(out=qsc, in_=homl, func=AF.Ln), reads=[(dvn, "homl")], writes=[(dvn, "qsc")])
        dkeys = [(dvn, "lb"), (dvn, "oml"), (dvn, "A"), (dvn, "homl"), (dvn, "lbh"), (dvn, "qsc")]
        onw = self.C(f"onw{e_}")
        scw = self.C(f"scw{e_}")
        scb = self.C(f"scb{e_}")
        dtb = self.C(f"dtb{e_}")
        bgn, bigc = A.alloc("bigc", [128, 1024], F32)
        o_dsk = self.coff[f"dsk{e_}"][0]
        P.dma("sp", "cst", lambda e: e.dma_start(out=bigc, in_=self.cst_d[:, o_dsk:o_dsk + 1024]), writes=[bgn])
        dsk = bigc[:, 0:512]
        snw = bigc[:, 512:1024]
        xn, xin = A.alloc("xin", [128, KC, TBm], F32)
        hT2 = [A.alloc(f"hT{i}", [128, KC, TBm], BF16) for i in range(2)]
        sqn, sq = A.alloc("sq", [128, 2, TBm], BF16)
        sqhn, sqh = A.alloc("sqh", [128, 2, TBm], BF16)
        rn, rstd = A.alloc("rstd", [128, TBm], F32)
        yn_, yT = A.alloc("yT", [128, KC, TBm], BF16)
        xrn, xr = A.alloc("xr", [128, 2, TBm], F32)
        vtok2 = [A.alloc(f"vtok{i}", [64, NHC, 512], BF16) for i in range(2)]
        TS = []
        for s_ in range(2):
            TS.append([A.alloc(f"t{i}s{s_}", [128, TBm], F32) for i in range(4)])
        qt2 = [A.alloc(f"qt{i}", [128, 4, TBm], BF16) for i in range(2)]
        kt2 = [A.alloc(f"kt{i}", [128, 4, TBm], BF16) for i in range(2)]
        sc2 = [A.alloc(f"sc{i}", [128, 3, 4, NHC], F32) for i in range(2)]
        PT2 = [A.alloc(f"PT{i}", [64, 4, NHC, 64], BF16) for i in range(2)]
        kkn, ktok = A.alloc("ktok", [64, 4, NHC, 128], BF16)
        kvn, kvs = A.alloc("kvs", [128, NHC, 4, 128], BF16)
        Smid2 = [A.alloc(f"Smid{i}", [128, NHC, 4, 128], BF16) for i in range(2)]
        Sn, Sf = A.alloc("Sf", [128, 4, 128], F32)
        rawn, raw = A.alloc("raw", [128, 2, 3 + TBm], F32)
        hsn, hist = A.alloc("hist", [128, KC, 4], F32)
        xbc2 = [A.alloc(f"xbc{i}", [128, KC, TBm], BF16) for i in range(2)]
        accn, acc = A.alloc("acc", [128, 2, TBm], F32)
        SSn, SS = A.alloc("SS", [128, 512], F32)
        SBn, SSb2 = A.alloc("SSb", [128, 2, 512], BF16)
        upd_done = {}
        free_T = [0, 1]
        free_S = [0, 1]
        sets = []
        for s_ in range(2):
            d = {}
            for nm_, shp, dt_ in [("dts", [128, 64], F32), ("YD", [128, 8, 128], F32), ("CBm", [128, 2, 128], F32),
                                  ("MT", [128, 8, 128], BF16), ("xs", [128, 512], BF16), ("xdt", [128, 512], BF16),
                                  ("xdw", [128, 512], BF16), ("xsd", [128, 512], BF16), ("Btok", [128, 2, 128], BF16),
                                  ("sz", [128, 512], F32), ("yf", [128, 512], F32), ("ybf", [128, 512], BF16)]:
                d[nm_] = A.alloc(f"{nm_}{s_}", shp, dt_)
            sets.append(d)
        srcv = src.rearrange("(c p) t -> p c t", p=128)
        tri = self.S("tri")
        ones_f = self.S("ones")
        scanmask = self.S("scanmask")[:, 0:TBm]
        P.op("pool", lambda e: e.memset(hist, 0.0), writes=[(hsn, c) for c in range(KC)])
        P.op("pool", lambda e: e.memset(SS, 0.0), writes=[SSn])
        P.op("pool", lambda e: e.memset(SSb2, 0.0), writes=[(SBn, 0), (SBn, 1)])
        rsq = float(1.0 / np.sqrt(128.0))
        xri = [0]

        def inter(*gens):
            gens = list(gens)
            while gens:
                for g in list(gens):
                    try:
                        next(g)
                        yield
                    except StopIteration:
                        gens.remove(g)

        def rolling(genfns, width):
            pending = list(genfns)
            active = []
            while pending or active:
                while pending and len(active) < width:
                    active.append(pending.pop(0)())
                for g in list(active):
                    try:
                        next(g)
                        yield
                    except StopIteration:
                        active.remove(g)

        def seq(*gens):
            for g in gens:
                for _ in g:
                    yield

        def ld(st):
            P.dma("sp", "xin0", lambda e, st=st: e.dma_start(out=xin, in_=srcv[:, :, st * TBm:(st + 1) * TBm]),
                  reads=[("dr", src_id, st // 2, c) for c in range(KC)], writes=[xn])

        def hgrn_A(st, h):
            (hn, hT), (vtn, vtok), (qtn, qt), (ktn, kt) = hT2[st % 2], vtok2[st % 2], qt2[st % 2], kt2[st % 2]
            (scn, sc), (ptn, PT) = sc2[st % 2], PT2[st % 2]
            while not free_T:
                yield
            s_ = free_T.pop(0)
            (t1n, t1), (t2n, t2), (t3n, t3), (t4n, t4) = TS[s_]
            kf, psf = self.gbank(2)
            self.proj(psf, kf, win, kin, 512 + h * 128, hT, hn, ncols=TBm)
            P.op("act", lambda e: e.activation(out=t1, in_=psf[:, 0:TBm], func=AF.Tanh, scale=0.5), reads=[kf], writes=[t1n])
            P.op("act", lambda e: e.activation(out=t2, in_=psf[:, 0:TBm], func=AF.Tanh, scale=-0.5), reads=[kf], writes=[t2n])
            yield
            P.op("act", lambda e: e.activation(out=t1, in_=t1, func=AF.Ln, scale=homl[:, h:h + 1], bias=lbh[:, h:h + 1]),
                 reads=[t1n] + dkeys, writes=[t1n])
            yield
            P.op("dve", lambda e: e.tensor_tensor_scan(out=t3, data0=scanmask, data1=t1, initial=0.0, op0=ALU.mult, op1=ALU.add),
                 reads=[t1n, "sct"], writes=[t3n])
            b3 = t3.rearrange("p (c t) -> p c t", t=64)
            yield
            P.op("act", lambda e: e.activation(out=sc[:, 0, h, :], in_=b3[:, :, 31], func=AF.Exp), reads=[t3n], writes=[(scn, 0, h)])
            P.op("act", lambda e: e.activation(out=sc[:, 1, h, :], in_=b3[:, :, 63], func=AF.Exp), reads=[t3n], writes=[(scn, 1, h)])
            P.op("dve", lambda e: e.tensor_tensor(out=t4.rearrange("p (c t) -> p c t", t=64), in0=b3,
                                                  in1=b3[:, :, 31:32].to_broadcast([128, NHC, 64]), op=ALU.subtract),
                 reads=[t3n], writes=[t4n])
            yield
            P.op("act", lambda e: e.activation(out=t1, in_=t4, func=AF.Exp), reads=[t4n], writes=[t1n])
            P.op("act", lambda e: e.activation(out=t3, in_=t4, func=AF.Exp, scale=-1.0, bias=qsc[:, h:h + 1]),
                 reads=[t4n] + dkeys, writes=[t3n])
            yield
            kq, psq = self.gbank(1)
            self.proj(psq, kq, win, kin, h * 128, hT, hn, ncols=TBm)
            P.op("dve", lambda e: e.scalar_tensor_tensor(out=qt[:, h, :], in0=psq[:, 0:TBm], scalar=rsq, in1=t1, op0=ALU.mult, op1=ALU.mult),
                 reads=[kq, t1n], writes=[(qtn, h)])
            P.op("dve", lambda e: e.scalar_tensor_tensor(out=kt[:, h, :], in0=t2, scalar=1.0, in1=t3, op0=ALU.add, op1=ALU.mult),
                 reads=[t2n, t3n], writes=[(ktn, h)])
            P.op("dve", lambda e: e.tensor_copy(out=sc[:, 2, h, :], in_=t1.rearrange("p (c t) -> p c t", t=64)[:, :, 63]),
                 reads=[t1n], writes=[(scn, 2, h)])
            yield
            ks, pss = self.gbank(1)
            for c in range(NHC):
                cs = slice(c * 64, (c + 1) * 64)
                P.op("pe", lambda e, cs=cs: e.matmul(pss[0:64, cs], kt[:, h, cs], qt[:, h, cs], start=True, stop=True),
                     reads=[(ktn, h), (qtn, h)], writes=[ks])
            P.op("dve", lambda e: e.tensor_tensor(out=PT[:, h], in0=pss[0:64, 0:TBm].rearrange("p (c t) -> p c t", t=64),
                                                  in1=self.mask64[:, None, :].to_broadcast([64, NHC, 64]), op=ALU.mult),
                 reads=[ks, "mask64"], writes=[(ptn, h)])
            yield
            kbf, psb = self.gbank(1)
            for c in range(NHC):
                P.op("pe", lambda e, c=c: e.matmul(psb[0:64, c * 128:(c + 1) * 128], kt[:, h, c * 64:(c + 1) * 64], self.ident_bf,
                                                   start=True, stop=True),
                     reads=[(ktn, h), "ident_bf"], writes=[kbf])
            P.op("act", lambda e: e.activation(out=ktok[:, h].rearrange("p c d -> p (c d)"), in_=psb[0:64, 0:NHC * 128], func=AF.Copy),
                 reads=[kbf], writes=[(kkn, h)])
            yield
            kkv, pkv = self.gbank(1)
            for c in range(NHC):
                P.op("pe", lambda e, c=c: e.matmul(pkv[:, c * 128:(c + 1) * 128], ktok[:, h, c, :], vtok[:, c, h * 128:(h + 1) * 128],
                                                   start=True, stop=True), reads=[(kkn, h), (vtn, c)], writes=[kkv])
            P.op("dve", lambda e: e.tensor_tensor(out=kvs[:, :, h, :], in0=pkv[:, 0:NHC * 128].rearrange("p (c d) -> p c d", d=128),
                                                  in1=sc[:, 2, h, :].unsqueeze(2).to_broadcast([128, NHC, 128]), op=ALU.mult),
                 reads=[kkv, (scn, 2, h)], writes=[(kvn, h)])
            free_T.append(s_)
            yield

        def hgrn_B(st):
            (scn, sc), (smn, Smid) = sc2[st % 2], Smid2[st % 2]
            kv_keys = [(kvn, h) for h in range(4)]
            for c in range(NHC):
                first = (st == 0 and c == 0)
                if not first:
                    P.op("dve", lambda e, c=c: e.tensor_tensor(out=Smid[:, c], in0=Sf,
                                                               in1=sc[:, 0, :, c].unsqueeze(2).to_broadcast([128, 4, 128]), op=ALU.mult),
                         reads=[Sn] + [(scn, 0, h) for h in range(4)], writes=[(smn, c)])
                    P.op("dve", lambda e, c=c: e.tensor_tensor(out=Sf, in0=Sf, in1=sc[:, 1, :, c].unsqueeze(2).to_broadcast([128, 4, 128]),
                                                               op=ALU.mult), reads=[Sn] + [(scn, 1, h) for h in range(4)], writes=[Sn])
                    P.op("dve", lambda e, c=c: e.tensor_tensor(out=Sf, in0=Sf, in1=kvs[:, c], op=ALU.add), reads=[Sn] + kv_keys, writes=[Sn])
                else:
                    P.op("dve", lambda e, c=c: e.tensor_copy(out=Sf, in_=kvs[:, c]), reads=kv_keys, writes=[Sn])
                yield

        def hgrn_C(st, h):
            (hn, hT), (vtn, vtok), (qtn, qt) = hT2[st % 2], vtok2[st % 2], qt2[st % 2]
            (ptn, PT), (smn, Smid) = PT2[st % 2], Smid2[st % 2]
            while not free_T:
                yield
            s_ = free_T.pop(0)
            (t1n, t1), (t2n, t2), (t3n, t3), (t4n, t4) = TS[s_]
            ko_, pso = self.gbank(2)
            for c in range(NHC):
                first = (st == 0 and c == 0)
                cs = slice(c * 64, (c + 1) * 64)
                P.op("pe", lambda e, c=c, cs=cs, first=first: e.matmul(pso[:, cs], vtok[:, c, h * 128:(h + 1) * 128], PT[:, h, c, :],
                                                                       start=True, stop=first),
                     reads=[(vtn, c), (ptn, h)], writes=[ko_])
                if not first:
                    P.op("pe", lambda e, c=c, cs=cs: e.matmul(pso[:, cs], Smid[:, c, h, :], qt[:, h, cs], start=False, stop=True),
                         reads=[(smn, c), (qtn, h)], writes=[ko_])
            j = h % 2
            P.op("act", lambda e: e.activation(out=sq[:, j, :], in_=pso[:, 0:TBm], func=AF.Square), reads=[ko_], writes=[(sqn, j)])
            yield
            kg, psg = self.gbank(1)
            self.proj(psg, kg, win, kin, 1536 + h * 128, hT, hn, ncols=TBm)
            P.op("act", lambda e: e.activation(out=t4, in_=psg[:, 0:TBm], func=AF.Silu), reads=[kg], writes=[t4n])
            yield
            kn_, psn = self.gbank(1)
            P.op("pe", lambda e: e.matmul(psn[:, 0:TBm], self.ones_bf, sq[:, j, :], start=True, stop=True),
                 reads=[(sqn, j), "ones_bf"], writes=[kn_])
            P.op("act", lambda e: e.activation(out=t1, in_=psn[:, 0:TBm], func=AF.Ln, scale=1.0 / 128.0, bias=EPS), reads=[kn_], writes=[t1n])
            yield
            P.op("act", lambda e: e.activation(out=t1, in_=t1, func=AF.Exp, scale=-0.5), reads=[t1n], writes=[t1n])
            P.op("dve", lambda e: e.scalar_tensor_tensor(out=t3, in0=pso[:, 0:TBm], scalar=onw[:, h:h + 1], in1=t1, op0=ALU.mult, op1=ALU.mult),
                 reads=[ko_, t1n, "cst"], writes=[t3n])
            yield
            P.op("dve", lambda e: e.tensor_tensor(out=yT[:, h, :], in0=t3, in1=t4, op=ALU.mult), reads=[t3n, t4n], writes=[(yn_, h)])
            free_T.append(s_)
            yield

        def ssd_conv(st):
            (hn, hT), (xbn, xbc) = hT2[st % 2], xbc2[st % 2]
            for c0 in range(0, KC, 2):
                pair = (c0, c0 + 1)
                for c in pair:
                    kb, ps = self.gbank(1)
                    self.proj(ps, kb, win, kin, 2560 + c * 128, hT, hn, ncols=TBm)
                    rj = c % 2
                    P.op("pool", lambda e, c=c, rj=rj: e.tensor_copy(out=raw[:, rj, 0:3], in_=hist[:, c, 0:3]), reads=[(hsn, c)], writes=[(rawn, rj)])
                    P.op("act", lambda e, rj=rj, ps=ps: e.activation(out=raw[:, rj, 3:3 + TBm], in_=ps[:, 0:TBm], func=AF.Copy), reads=[kb], writes=[(rawn, rj)])
                    P.op("pool", lambda e, c=c, rj=rj: e.tensor_copy(out=hist[:, c, 0:3], in_=raw[:, rj, TBm:TBm + 3]), reads=[(rawn, rj)], writes=[(hsn, c)])
                    yield
                for c in pair:
                    rj = c % 2
                    P.op("dve", lambda e, c=c, rj=rj: e.tensor_scalar(out=acc[:, rj, :], in0=raw[:, rj, 0:TBm], scalar1=scw[:, c:c + 1], scalar2=scb[:, c:c + 1],
                                                                      op0=ALU.mult, op1=ALU.add), reads=[(rawn, rj), "cst"], writes=[(accn, rj)])
                yield
                for j in range(1, 4):
                    for c in pair:
                        rj = c % 2
                        P.op("dve", lambda e, c=c, j=j, rj=rj: e.scalar_tensor_tensor(out=acc[:, rj, :], in0=raw[:, rj, j:j + TBm],
                                                                                      scalar=scw[:, j * 8 + c:j * 8 + c + 1], in1=acc[:, rj, :],
                                                                                      op0=ALU.mult, op1=ALU.add),
                             reads=[(rawn, rj), (accn, rj), "cst"], writes=[(accn, rj)])
                    yield
                for c in pair:
                    rj = c % 2
                    P.op("act", lambda e, c=c, rj=rj: e.activation(out=xbc[:, c, :], in_=acc[:, rj, :], func=AF.Silu), reads=[(accn, rj)], writes=[(xbn, c)])
                yield

        def ssd_chunk(st, q):
            (hn, hT), (xbn, xbc) = hT2[st % 2], xbc2[st % 2]
            while not free_S:
                yield
            si_ = free_S.pop(0)
            S_ = sets[si_]
            gq = st * NQ + q
            SSb = SSb2[:, gq % 2]
            SSb_next = SSb2[:, (gq + 1) % 2]
            kSB, kSBn = (SBn, gq % 2), (SBn, (gq + 1) % 2)
            (dtn, dts), (ydn, YD), (cbn, CBm), (mtn, MT) = S_["dts"], S_["YD"], S_["CBm"], S_["MT"]
            (xsn, xs), (xdn, xdt), (xwn, xdw), (xsdn, xsd) = S_["xs"], S_["xdt"], S_["xdw"], S_["xsd"]
            (btn, Btok), (szn, sz), (yfn, yf), (ybfn, ybf) = S_["Btok"], S_["sz"], S_["yf"], S_["ybf"]
            qs = slice(q * 128, (q + 1) * 128)
            first = (st == 0 and q == 0)
            dt_ = dts[:, 0:8]
            dA = dts[:, 8:16]
            acs = dts[:, 16:32]
            nacs = dts[:, 32:40]
            eacs = dts[:, 40:48]
            wdec = dts[:, 48:56]
            eatot = dts[:, 56:64]
            kd, psd = self.gbank(1)
            for k in range(KC):
                P.op("pe", lambda e, k=k: e.matmul(psd[:, 0:8], hT[:, k, qs], win[:, k, 3584:3592], start=(k == 0), stop=(k == KC - 1)),
                     reads=kin(3584, 3592) + [(hn, k)], writes=[kd])
            P.op("dve", lambda e: e.tensor_tensor(out=dt_, in0=psd[:, 0:8], in1=dtb, op=ALU.add), reads=[kd, "cst"], writes=[(dtn, "dt")])
            yield
            P.op("act", lambda e: e.activation(out=dt_, in_=dt_, func=AF.Exp), reads=[(dtn, "dt")], writes=[(dtn, "dt")])
            P.op("act", lambda e: e.activation(out=dt_, in_=dt_, func=AF.Ln, bias=1.0), reads=[(dtn, "dt")], writes=[(dtn, "dt")])
            P.op("dve", lambda e: e.tensor_tensor(out=dA, in0=dt_, in1=Ab, op=ALU.mult), reads=[(dtn, "dt")] + dkeys, writes=[(dtn, "dA")])
            yield
            kz, psz = self.gbank(1)
            for k in range(KC):
                P.op("pe", lambda e, k=k: e.matmul(psz, hT[:, k, qs], win[:, k, 2048:2560], start=(k == 0), stop=(k == KC - 1)),
                     reads=kin(2048, 2560) + [(hn, k)], writes=[kz])
            P.op("act", lambda e: e.activation(out=sz, in_=psz, func=AF.Silu), reads=[kz], writes=[szn])
            yield
            kc_, psc = self.gbank(1)
            P.op("pe", lambda e: e.matmul(psc[:, 0:8], tri, dA, start=True, stop=True), reads=[(dtn, "dA"), "sct"], writes=[kc_])
            P.op("pe", lambda e: e.matmul(psc[:, 8:16], ones_f, dA, start=True, stop=True), reads=[(dtn, "dA"), "sct"], writes=[kc_])
            P.op("act", lambda e: e.activation(out=acs, in_=psc[:, 0:16], func=AF.Copy), reads=[kc_], writes=[(dtn, "acs")])
            yield
            P.op("dve", lambda e: e.tensor_scalar(out=nacs, in0=acs[:, 0:8], scalar1=-1.0, scalar2=None, op0=ALU.mult),
                 reads=[(dtn, "acs")], writes=[(dtn, "nacs")])
            P.op("act", lambda e: e.activation(out=eacs, in_=acs[:, 0:8], func=AF.Exp), reads=[(dtn, "acs")], writes=[(dtn, "eacs")])
            P.op("dve", lambda e: e.tensor_tensor(out=wdec, in0=acs[:, 8:16], in1=acs[:, 0:8], op=ALU.subtract),
                 reads=[(dtn, "acs")], writes=[(dtn, "wdec")])
            P.op("act", lambda e: e.activation(out=wdec, in_=wdec, func=AF.Exp), reads=[(dtn, "wdec")], writes=[(dtn, "wdec")])
            P.op("act", lambda e: e.activation(out=eatot, in_=acs[:, 8:16], func=AF.Exp), reads=[(dtn, "acs")], writes=[(dtn, "eatot")])
            yield
            P.op("dve", lambda e: e.tensor_tensor(out=YD, in0=tri[:, None, :].to_broadcast([128, 8, 128]),
                                                  in1=dA[:, :, None].to_broadcast([128, 8, 128]), op=ALU.mult),
                 reads=[(dtn, "dA"), "sct"], writes=[(ydn, 0), (ydn, 1)])
            yield
            for half in range(2):
                ka, psa = self.gbank(4)
                P.op("pe", lambda e, half=half, psa=psa: e.matmul(psa, ones_f, YD[:, half * 4:(half + 1) * 4, :].rearrange("p h t -> p (h t)"),
                                                                  start=True, stop=True), reads=[(ydn, half), "sct"], writes=[ka])
                for hh in range(4):
                    h = half * 4 + hh
                    P.op("dve", lambda e, h=h, hh=hh, psa=psa: e.tensor_scalar(out=YD[:, h, :], in0=psa[:, hh * 128:(hh + 1) * 128],
                                                                              scalar1=nacs[:, h:h + 1], scalar2=0.0, op0=ALU.add, op1=ALU.min),
                         reads=[ka, (dtn, "nacs")], writes=[(ydn, half)])
                P.op("act", lambda e, half=half: e.activation(out=YD[:, half * 4:(half + 1) * 4, :], in_=YD[:, half * 4:(half + 1) * 4, :], func=AF.Exp),
                     reads=[(ydn, half)], writes=[(ydn, half)])
                yield
            kcb, pcb = self.gbank(1)
            for g in range(2):
                P.op("pe", lambda e, g=g: e.matmul(pcb[:, g * 128:(g + 1) * 128], xbc[:, 4 + g, qs], xbc[:, 6 + g, qs], start=True, stop=True),
                     reads=[(xbn, 4 + g), (xbn, 6 + g)], writes=[kcb])
            P.op("dve", lambda e: e.tensor_tensor(out=CBm, in0=pcb[:, 0:256].rearrange("p (g t) -> p g t", g=2),
                                                  in1=tri[:, None, :].to_broadcast([128, 2, 128]), op=ALU.mult),
                 reads=[kcb, "sct"], writes=[cbn])
            yield
            P.op("dve", lambda e: e.tensor_tensor(out=MT.rearrange("p (g j) t -> p g j t", g=2),
                                                  in0=YD.rearrange("p (g j) t -> p g j t", g=2),
                                                  in1=CBm[:, :, None, :].to_broadcast([128, 2, 4, 128]), op=ALU.mult),
                 reads=[(ydn, 0), (ydn, 1), cbn], writes=[mtn])
            yield
            kbf, psb = self.gbank(1)
            for c in range(4):
                P.op("pe", lambda e, c=c: e.matmul(psb[:, c * 128:(c + 1) * 128], xbc[:, c, qs], self.ident_bf, start=True, stop=True),
                     reads=[(xbn, c), "ident_bf"], writes=[kbf])
            P.op("act", lambda e: e.activation(out=xs, in_=psb, func=AF.Copy), reads=[kbf], writes=[xsn])
            yield
            kbf2, psb2 = self.gbank(1)
            for g in range(2):
                P.op("pe", lambda e, g=g: e.matmul(psb2[:, g * 128:(g + 1) * 128], xbc[:, 4 + g, qs], self.ident_bf, start=True, stop=True),
                     reads=[(xbn, 4 + g), "ident_bf"], writes=[kbf2])
            P.op("act", lambda e: e.activation(out=Btok.rearrange("p g n -> p (g n)"), in_=psb2[:, 0:256], func=AF.Copy),
                 reads=[kbf2], writes=[btn])
            yield
            xs3 = xs.rearrange("p (h d) -> p h d", d=64)
            P.op("dve", lambda e: e.tensor_tensor(out=xdt.rearrange("p (h d) -> p h d", d=64), in0=xs3,
                                                  in1=dt_[:, :, None].to_broadcast([128, 8, 64]), op=ALU.mult),
                 reads=[xsn, (dtn, "dt")], writes=[xdn])
            P.op("dve", lambda e: e.tensor_tensor(out=xdw.rearrange("p (h d) -> p h d", d=64), in0=xdt.rearrange("p (h d) -> p h d", d=64),
                                                  in1=wdec[:, :, None].to_broadcast([128, 8, 64]), op=ALU.mult),
                 reads=[xdn, (dtn, "wdec")], writes=[xwn])
            P.op("dve", lambda e: e.tensor_tensor(out=xsd, in0=xs, in1=dsk, op=ALU.mult), reads=[xsn, bgn], writes=[xsdn])
            yield
            while gq > 0 and not upd_done.get(gq - 1):
                yield
            if not first:
                kof, pof = self.gbank(1)
                for g in range(2):
                    P.op("pe", lambda e, g=g: e.matmul(pof[:, g * 256:(g + 1) * 256], xbc[:, 6 + g, qs], SSb[:, g * 256:(g + 1) * 256],
                                                       start=True, stop=True), reads=[(xbn, 6 + g), kSB], writes=[kof])
            kst, pst = self.gbank(1)
            for g in range(2):
                P.op("pe", lambda e, g=g: e.matmul(pst[:, g * 256:(g + 1) * 256], Btok[:, g, :], xdw[:, g * 256:(g + 1) * 256], start=True, stop=True),
                     reads=[btn, xwn], writes=[kst])
            if first:
                P.op("dve", lambda e: e.tensor_copy(out=SS, in_=pst), reads=[kst], writes=[SSn])
            else:
                P.op("dve", lambda e: e.tensor_tensor(out=SS.rearrange("p (h d) -> p h d", d=64), in0=SS.rearrange("p (h d) -> p h d", d=64),
                                                      in1=eatot[:, :, None].to_broadcast([128, 8, 64]), op=ALU.mult),
                     reads=[SSn, (dtn, "eatot")], writes=[SSn])
                P.op("dve", lambda e: e.tensor_tensor(out=SS, in0=pst, in1=SS, op=ALU.add), reads=[kst, SSn], writes=[SSn])
            P.op("act", lambda e: e.activation(out=SSb_next, in_=SS, func=AF.Copy), reads=[SSn], writes=[kSBn])
            upd_done[gq] = True
            yield
            ky, psy = self.gbank(1)
            P.op("pe", lambda e: e.matmul(psy, self.ident_bf, xsd, start=True, stop=False), reads=[xsdn, "ident_bf"], writes=[ky])
            for h in range(8):
                P.op("pe", lambda e, h=h: e.matmul(psy[:, h * 64:(h + 1) * 64], MT[:, h, :], xdt[:, h * 64:(h + 1) * 64], start=False, stop=(h == 7)),
                     reads=[mtn, xdn], writes=[ky])
            if not first:
                P.op("dve", lambda e: e.tensor_tensor(out=yf.rearrange("p (h d) -> p h d", d=64), in0=pof.rearrange("p (h d) -> p h d", d=64),
                                                      in1=eacs[:, :, None].to_broadcast([128, 8, 64]), op=ALU.mult),
                     reads=[kof, (dtn, "eacs")], writes=[yfn])
                P.op("dve", lambda e: e.tensor_tensor(out=yf, in0=psy, in1=yf, op=ALU.add), reads=[ky, yfn], writes=[yfn])
            else:
                P.op("dve", lambda e: e.tensor_copy(out=yf, in_=psy), reads=[ky], writes=[yfn])
            yield
            P.op("dve", lambda e: e.tensor_tensor(out=yf, in0=yf, in1=sz, op=ALU.mult), reads=[yfn, szn], writes=[yfn])
            P.op("dve", lambda e: e.memset(acs[:, 0:2], 0.0), reads=[(dtn, "acs")], writes=[(dtn, "acs")])
            yield
            for g in range(2):
                P.op("act", lambda e, g=g: e.activation(out=ybf[:, g * 256:(g + 1) * 256], in_=yf[:, g * 256:(g + 1) * 256], func=AF.Square,
                                                        accum_out=acs[:, g:g + 1]),
                     reads=[yfn, (dtn, "acs")], writes=[ybfn, (dtn, "acs")])
            P.op("act", lambda e: e.activation(out=acs[:, 0:2], in_=acs[:, 0:2], func=AF.Ln, scale=1.0 / 256.0, bias=EPS),
                 reads=[(dtn, "acs")], writes=[(dtn, "acs")])
            P.op("act", lambda e: e.activation(out=acs[:, 0:2], in_=acs[:, 0:2], func=AF.Exp, scale=-0.5),
                 reads=[(dtn, "acs")], writes=[(dtn, "acs")])
            yield
            for g in range(2):
                P.op("dve", lambda e, g=g: e.scalar_tensor_tensor(out=ybf[:, g * 256:(g + 1) * 256], in0=yf[:, g * 256:(g + 1) * 256],
                                                                  scalar=acs[:, g:g + 1], in1=snw[:, g * 256:(g + 1) * 256], op0=ALU.mult, op1=ALU.mult),
                     reads=[yfn, (dtn, "acs"), bgn], writes=[ybfn])
            yield
            kbf3, psb3 = self.gbank(1)
            for c in range(4):
                P.op("pe", lambda e, c=c: e.matmul(psb3[:, c * 128:(c + 1) * 128], ybf[:, c * 128:(c + 1) * 128], self.ident_bf, start=True, stop=True),
                     reads=[ybfn, "ident_bf"], writes=[kbf3])
            P.op("act", lambda e: e.activation(out=yT[:, 4:8, qs], in_=psb3.rearrange("p (c t) -> p c t", c=4), func=AF.Copy),
                 reads=[kbf3], writes=[(yn_, 4), (yn_, 5), (yn_, 6), (yn_, 7)])
            free_S.append(si_)
            yield

        def head(st):
            (hn, hT), (vtn, vtok) = hT2[st % 2], vtok2[st % 2]
            self.rms_rstd(xin, xn, KC, sqh, sqhn, rstd, rn, ncols=TBm, bankfn=self.gbank)
            self.make_h(xin, xn, gain, rstd, rn, hT, hn, ncols=TBm)
            if st + 1 < NSTm:
                ld(st + 1)
            yield
            for c in range(NHC):
                kb, ps = self.gbank(1)
                for k in range(KC):
                    P.op("pe", lambda e, c=c, k=k, ps=ps: e.matmul(ps[0:64, :], hT[:, k, c * 64:(c + 1) * 64], win[:, k, 1024:1536],
                                                                  start=(k == 0), stop=(k == KC - 1)),
                         reads=kin(1024, 1536) + [(hn, k)], writes=[kb])
                P.op("act", lambda e, c=c, ps=ps: e.activation(out=vtok[:, c, :], in_=ps[0:64, :], func=AF.Copy),
                     reads=[kb], writes=[(vtn, c)])
                yield
            for _ in inter(seq(rolling([lambda h=h: hgrn_A(st, h) for h in range(4)], 2), hgrn_B(st)), ssd_conv(st)):
                yield

        def tail(st):
            for _ in inter(rolling([lambda h=h: hgrn_C(st, h) for h in range(4)], 2),
                           rolling([lambda q=q: ssd_chunk(st, q) for q in range(NQ)], 2)):
                yield
            for o in range(KC):
                kb, ps = self.gbank(1)
                self.proj(ps, kb, wout, kout, o * 128, yT, yn_, ncols=TBm)
                j = xri[0] % 2
                xri[0] += 1
                P.dma("sp", f"xrl{j}", lambda e, st=st, o=o, j=j: e.dma_start(
                    out=xr[:, j, :], in_=src[o * 128:(o + 1) * 128, st * TBm:(st + 1) * TBm]),
                    reads=[("dr", src_id, st // 2, o)], writes=[(xrn, j)])
                P.op("dve", lambda e, ps=ps, j=j: e.tensor_tensor(out=xr[:, j, :], in0=ps[:, 0:TBm], in1=xr[:, j, :], op=ALU.add),
                     reads=[kb, (xrn, j)], writes=[(xrn, j)])
                d = dst[o * 128:(o + 1) * 128, st * TBm:(st + 1) * TBm]
                P.dma("sp", f"xrs{j}", lambda e, d=d, j=j: e.dma_start(out=d, in_=xr[:, j, :]), reads=[(xrn, j)],
                      writes=[("dr", dst_id, st // 2, o)], is_out=(self.dst_id == "y"))
                yield

        ld(0)
        for _ in head(0):
            pass
        for st in range(NSTm):
            gens = [tail(st)]
            if st + 1 < NSTm:
                gens.append(head(st + 1))
            for _ in inter(*gens):
                pass
        assert not self.busy, self.busy
        self.end_stage()


def build_program(stages, ncc, nsc, coff, soff):
    nc = bass.Bass("TRN2", target_bir_lowering=False)
    B = Builder(nc, coff, soff, ncc, nsc)
    bufs = {"x": B.xT, "y": B.yT, "s0": B.scr[0], "s1": B.scr[1]}
    cur = "x"
    nxt = 0
    for i, stg in enumerate(stages):
        last = (i == len(stages) - 1)
        dst = "y" if last else f"s{nxt}"
        if not last:
            nxt = 1 - nxt
        kind = stg[0]
        if kind == "mlp":
            B.stage_mlp(stg[1], bufs[cur], cur, bufs[dst], dst)
        elif kind == "xattn":
            B.stage_xattn(stg[1], bufs[cur], cur, bufs[dst], dst)
        elif kind == "conf":
            B.stage_conf(stg[1], bufs[cur], cur, bufs[dst], dst)
        elif kind == "mixer":
            B.stage_mixer(stg[1], bufs[cur], cur, bufs[dst], dst)
        elif kind == "final":
            B.stage_final(bufs[cur], cur, bufs[dst], dst)
        cur = dst
    B.P.flush()
    B.P.final_wait("sp", B.P.out_tokens)
    B.P.emit()
    return nc, B


FULL_STAGES = []
for _l in range(4):
    FULL_STAGES.append(("mixer", _l) if _l % 2 == 0 else ("conf", _l))
    FULL_STAGES.append(("xattn", _l))
    FULL_STAGES.append(("mlp", _l))
FULL_STAGES.append(("final",))


def run(inputs, stages=None):
    stages = FULL_STAGES if stages is None else stages
    x = np.asarray(inputs["x"], np.float32)
    mem = np.asarray(inputs["mem"], np.float32)
    consts, coff = pack_consts(inputs)
    sconsts, soff = struct_consts()
    nc, B = build_program(stages, consts.shape[1], sconsts.shape[1], coff, soff)
    wts = {n: np.ascontiguousarray(np.asarray(inputs[n], np.float32)) for n in WEIGHT_NAMES}
    in_maps = []
    for b in range(8):
        m = {"xT": np.ascontiguousarray(x[b].T), "memT": np.ascontiguousarray(mem[b].T),
             "consts": consts, "sconsts": sconsts}
        m.update(wts)
        in_maps.append(m)
    res = run_bass_kernel_spmd(nc, in_maps, core_ids=list(range(8)))
    out = np.stack([np.ascontiguousarray(r["yT"].T) for r in res.results], axis=0)
    return out.astype(np.float32)


def kernel(**inputs):
    return run(inputs)
```

```python
import numpy as np
import concourse.bass as bass
import concourse.mybir as mybir
from concourse.bass_utils import run_bass_kernel_spmd

F32 = mybir.dt.float32
BF16 = mybir.dt.bfloat16
ALU = mybir.AluOpType
AF = mybir.ActivationFunctionType

ENGS = ("pe", "act", "dve", "pool", "sp")

D = 1024
T = 4096
TB = 512
NST = T // TB
KC = 8
MEM = 256
DFF = 4096
EPS = 1e-6
AB_IN = 3592


class Op:
    __slots__ = ("eng", "idx", "fn", "deps", "dma_waits", "signal", "val", "dma_sem")

    def __init__(self, eng, idx, fn):
        self.eng = eng
        self.idx = idx
        self.fn = fn
        self.deps = {}
        self.dma_waits = {}
        self.signal = False
        self.val = 0
        self.dma_sem = None


class Prog:
    def __init__(self, nc):
        self.nc = nc
        self.ops = {e: [] for e in ENGS}
        self.seen = {e: {} for e in ENGS}
        self.seen_dma = {e: {} for e in ENGS}
        self.lastw = {}
        self.readers = {}
        self.dma_sems = {}
        self.dma_last = {}
        self.esem = {}
        self.inherit = {}
        self.bufkeys = {}
        self.read_hook = None
        self.pending = []
        self.out_tokens = []
        self.do_schedule = True
        self.gseq = 0
        self.gseq_c = {}
        self.gseq_d = {}
        self.opclk = {}
        self.dmaclk = {}
        self.dma_prev = {}

    @staticmethod
    def _bufname(k):
        return k[0] if isinstance(k, tuple) else k

    def _record(self, op, tok, reads, writes):
        eng = op.eng
        cc = {}
        cd = {}

        def add(t):
            if t is None:
                return
            if t[0] == "c":
                _, se, si = t
                if se == eng and se == "pe":
                    return
                if cc.get(se, -1) < si:
                    cc[se] = si
            else:
                _, sn, val = t
                if cd.get(sn, 0) < val:
                    cd[sn] = val

        for k in reads:
            add(self.lastw.get(k))
        for k in writes:
            if k not in self.lastw:
                for t in self.inherit.get(self._bufname(k), ()):
                    add(t)
            add(self.lastw.get(k))
            for t in self.readers.get(k, {}).values():
                add(t)
        if tok[0] == "d":
            prev = self.dma_prev.get(tok[1])
            if prev:
                add(("d", tok[1], prev))
        clk, dclk = self.seen[eng], self.seen_dma[eng]
        cands = [(self.gseq_c[(se, si)], "c", se, si) for se, si in cc.items()]
        cands += [(self.gseq_d[(sn, val)], "d", sn, val) for sn, val in cd.items()]
        cands.sort(reverse=True)
        for _, kind, a, b in cands:
            if kind == "c":
                if clk.get(a, -1) >= b:
                    continue
                op.deps[a] = b
                self.ops[a][b].signal = True
                snap = self.opclk[(a, b)]
                clk[a] = b
            else:
                if dclk.get(a, 0) >= b:
                    continue
                op.dma_waits[a] = b
                snap = self.dmaclk[(a, b)]
                dclk[a] = b
            for k2, v2 in snap[0].items():
                if clk.get(k2, -1) < v2:
                    clk[k2] = v2
            for k2, v2 in snap[1].items():
                if dclk.get(k2, 0) < v2:
                    dclk[k2] = v2
        self.gseq += 1
        snapshot = (dict(clk), dict(dclk))
        if tok[0] == "c":
            self.gseq_c[(tok[1], tok[2])] = self.gseq
            self.opclk[(tok[1], tok[2])] = snapshot
        else:
            self.gseq_d[(tok[1], tok[2])] = self.gseq
            self.dmaclk[(tok[1], tok[2])] = snapshot
        srckey = tok[1]
        for k in reads:
            self.readers.setdefault(k, {})[srckey] = tok
            self.bufkeys.setdefault(self._bufname(k), set()).add(k)
        for k in writes:
            self.lastw[k] = tok
            self.readers[k] = {}
            self.bufkeys.setdefault(self._bufname(k), set()).add(k)

    DEF_W = {"pe": 256, "act": 384, "dve": 384, "pool": 512, "sp": 0}

    def op(self, eng, fn, reads=(), writes=(), w=None):
        self.pending.append(("c", eng, None, fn, tuple(reads), tuple(writes), w, False))
        if self.read_hook is not None:
            self.read_hook(reads)

    def dma(self, eng, semname, fn, reads=(), writes=(), w=None, is_out=False):
        self.pending.append(("d", eng, semname, fn, tuple(reads), tuple(writes), w, is_out))

    class _Fake:
        def __init__(self):
            self.call = None

        def __getattr__(self, name):
            def f(*a, **k):
                self.call = (name, a, k)
                return self
            return f

    @staticmethod
    def _fsize(ap):
        n = 1
        for d in ap.shape[1:]:
            n *= d
        return n

    ACT_CLS = None

    def _probe(self, fn, want_cls=False):
        fk = Prog._Fake()
        try:
            fn(fk)
            name, a, k = fk.call
            if want_cls:
                if name != "activation":
                    return None
                f = k.get("func")
                if f in (AF.Exp, AF.Ln):
                    return "E"
                if f in (AF.Silu, AF.Tanh):
                    return "U"
                if f == AF.Sigmoid:
                    return "S"
                return None
            if name == "matmul":
                rhs = k.get("rhs", a[2] if len(a) > 2 else None)
                lhsT = k.get("lhsT", a[1] if len(a) > 1 else None)
                w = self._fsize(rhs)
                if lhsT.dtype == F32:
                    w *= 4
                return w
            if name == "dma_start":
                out = k.get("out", a[0] if a else None)
                nb = self._fsize(out) * out.shape[0] * (4 if out.dtype == F32 else 2)
                return nb / 150e3
            out = k.get("out", a[0] if a else None)
            return self._fsize(out)
        except Exception:
            return None

    def _dur(self, rec):
        kind, eng, _, fn, _, _, w, _ = rec
        if w is None:
            w = self._probe(fn)
        if kind == "d":
            return 0.08, 2.5 + (w if w is not None else 1.0)
        if w is None:
            w = self.DEF_W[eng]
        if eng == "pe":
            d = max(0.06, w / 2350.0 + 0.012)
        elif eng == "act":
            d = 0.20 + w / 1250.0
        elif eng == "dve":
            d = 0.09 + w / 960.0
        else:
            d = 0.6 + w / 600.0
        return d, d

    def flush(self):
        recs = self.pending
        self.pending = []
        n = len(recs)
        if n == 0:
            return
        if not self.do_schedule:
            order = range(n)
        else:
            order = self._schedule(recs)
        for i in order:
            kind, eng, semname, fn, reads, writes, w, is_out = recs[i]
            if kind == "c":
                self._op_now(eng, fn, reads, writes)
            else:
                tok = self._dma_now(eng, semname, fn, reads, writes)
                if is_out:
                    self.out_tokens.append(tok)

    def _schedule(self, recs):
        import heapq
        n = len(recs)
        preds = [set() for _ in range(n)]
        lastw = {}
        readers = {}
        for i, r in enumerate(recs):
            for k in r[4]:
                j = lastw.get(k)
                if j is not None:
                    preds[i].add(j)
            for k in r[5]:
                j = lastw.get(k)
                if j is not None:
                    preds[i].add(j)
                for j in readers.get(k, ()):
                    preds[i].add(j)
            for k in r[4]:
                readers.setdefault(k, []).append(i)
            for k in r[5]:
                lastw[k] = i
                readers[k] = []
            preds[i].discard(i)
        lastsem = {}
        for i, r in enumerate(recs):
            if r[0] == "d":
                j = lastsem.get(r[2])
                if j is not None:
                    preds[i].add(j)
                lastsem[r[2]] = i
        succs = [[] for _ in range(n)]
        indeg = [0] * n
        for i in range(n):
            indeg[i] = len(preds[i])
            for j in preds[i]:
                succs[j].append(i)
        durs = [self._dur(r) for r in recs]
        acls = [self._probe(r[3], want_cls=True) if r[1] == "act" and r[0] == "c" else None for r in recs]
        cur_cls = [None]
        blevel = [0.0] * n
        for i in range(n - 1, -1, -1):
            b = 0.0
            for k in succs[i]:
                if blevel[k] > b:
                    b = blevel[k]
            blevel[i] = b + durs[i][1] + 0.35
        finish = [0.0] * n
        ready_t = [0.0] * n
        heaps = {e: [] for e in ENGS}
        for i in range(n):
            if indeg[i] == 0:
                heapq.heappush(heaps[recs[i][1]], (0.0, i))
        free = {e: 0.0 for e in ENGS}
        order = []
        LAT = 0.35
        while len(order) < n:
            best = None
            for e in ENGS:
                h = heaps[e]
                if not h:
                    continue
                rt, i = h[0]
                stt = max(rt, free[e])
                if best is None or (stt, i) < (best[0], best[2]):
                    best = (stt, e, i)
            stt, e, i = best
            h = heaps[e]
            slack = 1.0 if e == "act" else 0.05
            cand = [x for x in h if x[0] <= stt + slack]
            if len(cand) > 1:
                if e == "act":
                    pick = max(cand, key=lambda x: (0 if (acls[x[1]] is not None and acls[x[1]] != cur_cls[0]) else 1,
                                                    1 if x[0] <= stt + 0.05 else 0, blevel[x[1]], -x[1]))
                else:
                    pick = max(cand, key=lambda x: (blevel[x[1]], -x[1]))
                h.remove(pick)
                heapq.heapify(h)
                i = pick[1]
                stt = max(stt, pick[0])
            else:
                heapq.heappop(h)
            busy, lat = durs[i]
            if e == "act" and acls[i] is not None:
                if cur_cls[0] is not None and acls[i] != cur_cls[0]:
                    busy += 1.3
                    lat += 1.3
                cur_cls[0] = acls[i]
            free[e] = stt + busy
            finish[i] = stt + lat
            order.append(i)
            for k in succs[i]:
                indeg[k] -= 1
                t = finish[i] + LAT
                if t > ready_t[k]:
                    ready_t[k] = t
                if indeg[k] == 0:
                    heapq.heappush(heaps[recs[k][1]], (ready_t[k], k))
        self.sched_span = getattr(self, "sched_span", 0.0) + max(finish)
        return order

    def _op_now(self, eng, fn, reads=(), writes=()):
        o = Op(eng, len(self.ops[eng]), fn)
        self.ops[eng].append(o)
        tok = ("c", eng, o.idx)
        self._record(o, tok, reads, writes)
        return tok

    def _dma_now(self, eng, semname, fn, reads=(), writes=()):
        if semname not in self.dma_sems:
            self.dma_sems[semname] = [self.nc.alloc_semaphore("d_" + semname), 0]
        ent = self.dma_sems[semname]
        o = Op(eng, len(self.ops[eng]), fn)
        o.dma_sem = ent[0]
        self.ops[eng].append(o)
        self.dma_prev[semname] = ent[1]
        ent[1] += 16
        tok = ("d", semname, ent[1])
        self._record(o, tok, reads, writes)
        return tok

    def collect(self, bufname):
        assert not self.pending
        best = {}
        for k in self.bufkeys.get(bufname, ()):
            toks = list(self.readers.get(k, {}).values())
            if k in self.lastw:
                toks.append(self.lastw[k])
            for t in toks:
                key = (t[0], t[1])
                if key not in best or best[key][2] < t[2]:
                    best[key] = t
            self.lastw.pop(k, None)
            self.readers.pop(k, None)
        self.bufkeys.pop(bufname, None)
        self.inherit.pop(bufname, None)
        return list(best.values())

    def final_wait(self, eng, toks):
        self.flush()
        o = Op(eng, len(self.ops[eng]), None)
        self.ops[eng].append(o)
        for t in toks:
            if t[0] == "c":
                if o.deps.get(t[1], -1) < t[2]:
                    o.deps[t[1]] = t[2]
                    self.ops[t[1]][t[2]].signal = True
            else:
                if o.dma_waits.get(t[1], 0) < t[2]:
                    o.dma_waits[t[1]] = t[2]

    def emit(self):
        nc = self.nc
        for e in ENGS:
            if self.ops[e]:
                self.esem[e] = nc.alloc_semaphore("e_" + e)
        for e in ENGS:
            c = 0
            for o in self.ops[e]:
                if o.signal:
                    c += 1
                o.val = c
        engobj = {"pe": "tensor", "act": "scalar", "dve": "vector", "pool": "gpsimd", "sp": "sync"}
        self.n_inst = {e: len(self.ops[e]) for e in ENGS}
        self.n_wait = {e: 0 for e in ENGS}
        with nc.Block() as block:
            for e in ENGS:
                if not self.ops[e]:
                    continue

                def body(eng, e=e):
                    for o in self.ops[e]:
                        waits = [(self.esem[se], self.ops[se][si].val) for se, si in o.deps.items()]
                        waits += [(self.dma_sems[sn][0], val) for sn, val in o.dma_waits.items()]
                        self.n_wait[e] += len(waits)
                        if o.fn is None:
                            for sem, val in waits:
                                eng.wait_ge(sem, val)
                            continue
                        for sem, val in waits[:-1]:
                            eng.wait_ge(sem, val)
                        ins = o.fn(eng)
                        if waits:
                            ins._wait_ge(*waits[-1])
                        if o.dma_sem is not None:
                            ins.then_inc(o.dma_sem, 16)
                        elif o.signal:
                            ins.then_inc(self.esem[e], 1)

                getattr(block, engobj[e])(body)


class Arena:
    def __init__(self, nc, P, base, size):
        self.nc, self.P = nc, P
        self.base, self.size = base, size
        self.top = 0
        self.live = []
        self.freed = []
        self.uid = 0

    def alloc(self, name, shape, dtype):
        self.P.flush()
        self.uid += 1
        nm = f"{name}_{self.uid}"
        esz = 4 if dtype == F32 else 2
        n = 1
        for s in shape[1:]:
            n *= s
        nbytes = (n * esz + 63) // 64 * 64
        off = self.top
        assert off + nbytes <= self.size, f"SBUF arena overflow allocating {name}: {off + nbytes} > {self.size}"
        self.top += nbytes
        self.peak_top = max(getattr(self, "peak_top", 0), self.top)
        h = self.nc.alloc_sbuf_tensor_at(nm, list(shape), dtype, offset=self.base + off)
        toks = []
        for (fo, fn_, ft) in self.freed:
            if fo < off + nbytes and off < fo + fn_:
                toks.extend(ft)
        if toks:
            self.P.inherit[nm] = toks
        self.live.append((nm, off, nbytes))
        return nm, h.ap()

    def mark(self):
        return (self.top, len(self.live))

    def release(self, mark):
        self.P.flush()
        top, nlive = mark
        for (nm, off, nbytes) in self.live[nlive:]:
            toks = self.P.collect(nm)
            self.freed.append((off, nbytes, toks))
        del self.live[nlive:]
        self.top = top
        if len(self.freed) > 64:
            allt = {}
            lo = min(f[0] for f in self.freed)
            hi = max(f[0] + f[1] for f in self.freed)
            for f in self.freed:
                for t in f[2]:
                    key = (t[0], t[1])
                    if key not in allt or allt[key][2] < t[2]:
                        allt[key] = t
            self.freed = [(lo, hi - lo, list(allt.values()))]


def _cols(v):
    v = np.asarray(v, np.float32)
    return np.ascontiguousarray(v.reshape(-1, 128).T)


def _rep(v):
    v = np.asarray(v, np.float32).reshape(1, -1)
    return np.ascontiguousarray(np.repeat(v, 128, axis=0))


def pack_consts(inp):
    cols = []
    big = []
    off = {}
    cur = [0]

    def add(name, arr):
        off[name] = (cur[0], arr.shape[1])
        cols.append(arr)
        cur[0] += arr.shape[1]

    for l in range(4):
        add(f"g_mix{l}", _cols(inp["norm_mix_w"][l]))
        add(f"g_xat{l}", _cols(inp["norm_xattn_w"][l]))
        add(f"g_mlp{l}", _cols(inp["norm_mlp_w"][l]))
    add("g_fin", _cols(inp["final_norm_w"]))
    add("g_mem", _cols(inp["mem_norm_w"]))
    for e in range(2):
        add(f"lbl{e}", _cols(inp["hgrn_lb_logits"][e]))
        add(f"onw{e}", _cols(inp["hgrn_out_norm_w"][e]))
        cw = np.asarray(inp["ssd_conv_w"][e], np.float32)
        add(f"scw{e}", np.concatenate([_cols(cw[j]) for j in range(4)], axis=1))
        add(f"scb{e}", _cols(inp["ssd_conv_b"][e]))
        add(f"dtb{e}", _rep(inp["ssd_dt_bias"][e]))
        add(f"alog{e}", _rep(inp["ssd_a_log"][e]))
        big.append((f"dsk{e}", _rep(np.repeat(np.asarray(inp["ssd_d"][e], np.float32), 64))))
        big.append((f"snw{e}", _rep(inp["ssd_norm_w"][e])))
    for o in range(2):
        add(f"bpw1{o}", _cols(inp["cv_b_pw1"][o]))
        wd = np.asarray(inp["cv_w_dw"][o], np.float32)
        add(f"wdw{o}", np.concatenate([_cols(wd[j]) for j in range(31)], axis=1))
        add(f"bdw{o}", _cols(inp["cv_b_dw"][o]))
        add(f"lnw{o}", _cols(inp["cv_ln_w"][o]))
        add(f"lnb{o}", _cols(inp["cv_ln_b"][o]))
        add(f"bpw2{o}", _cols(inp["cv_b_pw2"][o]))
    off["_nsmall"] = (cur[0], 0)
    for name, arr in big:
        add(name, arr)
    return np.ascontiguousarray(np.concatenate(cols, axis=1)), off


def struct_consts():
    p = np.arange(128)[:, None]
    j = np.arange(128)[None, :]
    ident = (p == j).astype(np.float32)
    tri = (p <= j).astype(np.float32)
    scanmask = np.ones((128, 512), np.float32)
    scanmask[:, ::64] = 0.0
    ones = np.ones((128, 128), np.float32)
    arr = np.concatenate([ident, tri, scanmask, ones], axis=1)
    off = {"ident": (0, 128), "tri": (128, 128), "scanmask": (256, 512), "ones": (768, 128)}
    return np.ascontiguousarray(arr), off


WEIGHT_NAMES = ["ab_w_in", "ab_w_out", "cv_w_pw1", "cv_w_pw2", "xattn_wq", "xattn_wk", "xattn_wv",
                "xattn_wo", "mlp_w1", "mlp_w2"]
WEIGHT_SHAPES = {"ab_w_in": [2, 1024, 3592], "ab_w_out": [2, 1024, 1024], "cv_w_pw1": [2, 1024, 2048],
                 "cv_w_pw2": [2, 1024, 1024], "xattn_wq": [4, 1024, 1024], "xattn_wk": [4, 1024, 1024],
                 "xattn_wv": [4, 1024, 1024], "xattn_wo": [4, 1024, 1024], "mlp_w1": [4, 1024, 4096],
                 "mlp_w2": [4, 4096, 1024]}


class Builder:
    def __init__(self, nc, coff, soff, ncc, nsc):
        self.nc = nc
        self.P = Prog(nc)
        P = self.P
        self.coff, self.soff = coff, soff
        self.xT = nc.dram_tensor("xT", [D, T], F32, kind="ExternalInput").ap()
        self.memT = nc.dram_tensor("memT", [D, MEM], F32, kind="ExternalInput").ap()
        self.cst_d = nc.dram_tensor("consts", [128, ncc], F32, kind="ExternalInput").ap()
        self.sct_d = nc.dram_tensor("sconsts", [128, nsc], F32, kind="ExternalInput").ap()
        self.W = {n: nc.dram_tensor(n, WEIGHT_SHAPES[n], F32, kind="ExternalInput").ap() for n in WEIGHT_NAMES}
        self.yT = nc.dram_tensor("yT", [D, T], F32, kind="ExternalOutput").ap()
        self.scr = [nc.dram_tensor(f"scr{i}", [D, T], F32, kind="Internal").ap() for i in range(2)]
        total = nc.sbuf_bytes_remaining
        nsm = coff["_nsmall"][0]
        self.cst = nc.alloc_sbuf_tensor("cst", [128, nsm], F32).ap()
        self.sct = nc.alloc_sbuf_tensor("sct", [128, nsc], F32).ap()
        self.ones_bf = nc.alloc_sbuf_tensor("ones_bf", [128, 128], BF16).ap()
        self.ident_bf = nc.alloc_sbuf_tensor("ident_bf", [128, 128], BF16).ap()
        self.mask64 = nc.alloc_sbuf_tensor("mask64", [64, 64], F32).ap()
        self.dv = nc.alloc_sbuf_tensor("derived", [128, 64], F32).ap()
        probe = nc.alloc_sbuf_tensor("arena_probe", [128, 16], F32)
        self.arena_base = nc.lookup_mloc(probe).addr + 64
        self.A = Arena(nc, P, self.arena_base, nc.SBUF_PARTITION_SIZE_BYTES - self.arena_base)
        self.banks = [nc.alloc_psum_tensor(f"ps{i}", [128, 512], F32).ap() for i in range(8)]
        self.bank_i = 0
        self.bfhalf = 0
        self.busy = {}
        self.g_i = 0
        self.gbf_i = 0
        P.read_hook = self._on_reads
        self.wsem = 0
        P.dma("sp", "cst", lambda e: e.dma_start(out=self.cst, in_=self.cst_d[:, 0:nsm]), writes=["cst"])
        P.dma("sp", "sct", lambda e: e.dma_start(out=self.sct, in_=self.sct_d), writes=["sct"])
        so = soff
        P.op("dve", lambda e: e.tensor_copy(out=self.ones_bf, in_=self.S("ones")), reads=["sct"], writes=["ones_bf"])
        P.op("dve", lambda e: e.tensor_copy(out=self.ident_bf, in_=self.S("ident")), reads=["sct"], writes=["ident_bf"])
        P.op("dve", lambda e: e.tensor_copy(out=self.mask64, in_=self.sct[0:64, so["tri"][0]:so["tri"][0] + 64]),
             reads=["sct"], writes=["mask64"])

    def C(self, name, lo=0, n=None):
        o, w = self.coff[name]
        if n is None:
            n = w - lo
        return self.cst[:, o + lo:o + lo + n]

    def S(self, name):
        o, w = self.soff[name]
        return self.sct[:, o:o + w]

    def bank(self):
        i = self.bank_i
        self.bank_i = (i + 1) % 5
        return ("ps", i), self.banks[i]

    def _on_reads(self, reads):
        for k in reads:
            if k in self.busy:
                self.busy[k] -= 1
                if self.busy[k] <= 0:
                    del self.busy[k]

    def gbank(self, n_reads=1):
        for d in range(8):
            i = (self.g_i + d) % 8
            if ("ps", i) not in self.busy:
                self.g_i = (i + 1) % 8
                self.busy[("ps", i)] = n_reads
                return ("ps", i), self.banks[i]
        raise RuntimeError("no free PSUM bank")

    def load_w(self, name, wd, kchunks, ncols, colblk=512, c0=0, defer=False):
        nm, w = self.A.alloc(name, [128, kchunks, ncols], BF16)
        src = wd.rearrange("(k p) n -> p k n", p=128)
        nblk = (ncols + colblk - 1) // colblk
        nk = (kchunks + 7) // 8

        def keys(col_lo, col_hi, k=None):
            bl = range(col_lo // colblk, (col_hi - 1) // colblk + 1)
            if k is None:
                return [(nm, b, kk) for b in bl for kk in range(nk)]
            return [(nm, b, k // 8) for b in bl]

        def issue():
            self._issue_w(nm, w, src, kchunks, ncols, colblk, c0, nblk)
        if defer:
            return w, keys, issue
        issue()
        return w, keys

    def _issue_w(self, nm, w, src, kchunks, ncols, colblk, c0, nblk):
        for b in range(nblk):
            lo = b * colblk
            hi = min(ncols, lo + colblk)
            kstep = max(1, min(kchunks, 8))
            for k0 in range(0, kchunks, kstep):
                self.wsem = (self.wsem + 1) % 8
                self.P.dma("pool", f"w{self.wsem}",
                           lambda e, lo=lo, hi=hi, k0=k0, kstep=kstep: e.dma_start(
                               out=w[:, k0:k0 + kstep, lo:hi], in_=src[:, k0:k0 + kstep, c0 + lo:c0 + hi]),
                           writes=[(nm, b, k0 // kstep)])

    def rms_rstd(self, x, xkey, n, sq, sqn, rstd, rstdn, ncols=TB, bankfn=None):
        P = self.P
        kb, ps = self.bank() if bankfn is None else bankfn(1)
        for c in range(n):
            j = c % 2
            P.op("act", lambda e, c=c, j=j: e.activation(out=sq[:, j, 0:ncols], in_=x[:, c, 0:ncols], func=AF.Square),
                 reads=[xkey(c) if callable(xkey) else xkey], writes=[(sqn, j)])
            P.op("pe", lambda e, c=c, j=j: e.matmul(ps[:, 0:ncols], self.ones_bf, sq[:, j, 0:ncols], start=(c == 0), stop=(c == n - 1)),
                 reads=[(sqn, j), "ones_bf"], writes=[kb])
        P.op("act", lambda e: e.activation(out=rstd[:, 0:ncols], in_=ps[:, 0:ncols], func=AF.Ln, scale=1.0 / (n * 128), bias=EPS),
             reads=[kb], writes=[rstdn])
        P.op("act", lambda e: e.activation(out=rstd[:, 0:ncols], in_=rstd[:, 0:ncols], func=AF.Exp, scale=-0.5),
             reads=[rstdn], writes=[rstdn])

    def make_h(self, x, xkey, gain, rstd, rstdn, hT, hTn, ncols=TB):
        for c in range(KC):
            self.P.op("dve", lambda e, c=c: e.scalar_tensor_tensor(
                out=hT[:, c, 0:ncols], in0=x[:, c, 0:ncols], scalar=gain[:, c:c + 1], in1=rstd[:, 0:ncols],
                op0=ALU.mult, op1=ALU.mult), reads=[xkey, rstdn, "cst"], writes=[(hTn, c)])

    def proj(self, ps, kb, w, wkeys, col, hT, hTn, ncols=TB, m=128):
        for k in range(KC):
            self.P.op("pe", lambda e, k=k: e.matmul(ps[0:m, 0:ncols], w[:, k, col:col + m], hT[:, k, 0:ncols],
                                                     start=(k == 0), stop=(k == KC - 1)),
                      reads=wkeys(col, col + m) + [(hTn, k)], writes=[kb])

    def xstore(self, st, c, buf, bufname, sem):
        d = self.dst[c * 128:(c + 1) * 128, st * TB:(st + 1) * TB]
        self.P.dma("sp", sem, lambda e: e.dma_start(out=d, in_=buf), reads=[bufname], writes=[("dr", self.dst_id, st, c)],
                   is_out=(self.dst_id == "y"))

    def begin_stage(self, src, src_id, dst, dst_id):
        self.src, self.src_id, self.dst, self.dst_id = src, src_id, dst, dst_id
        self.mark = self.A.mark()

    def end_stage(self):
        self.peak = getattr(self, "peak", {})
        self.A.release(self.mark)

    def xload_keys(self, st):
        return [("dr", self.src_id, st, c) for c in range(KC)]

    def stage_mlp(self, l, src, src_id, dst, dst_id):
        P, A = self.P, self.A
        self.begin_stage(src, src_id, dst, dst_id)
        w1, k1 = self.load_w("w1", self.W["mlp_w1"][l], 8, DFF)
        w2, k2 = self.load_w("w2", self.W["mlp_w2"][l], 32, D, colblk=1024)
        xn, xin = A.alloc("xin", [128, KC, TB], F32)
        hn, hT = A.alloc("hT", [128, KC, TB], BF16)
        hidn, hid = A.alloc("hid", [128, 32, TB], BF16)
        sqn, sq = A.alloc("sq", [128, 2, TB], BF16)
        rn, rstd = A.alloc("rstd", [128, TB], F32)
        tn, tmp = A.alloc("tmp", [128, 2, TB], BF16)
        xrn, xr = A.alloc("xr", [128, 2, TB], F32)
        gain = self.C(f"g_mlp{l}")
        srcv = src.rearrange("(c p) t -> p c t", p=128)
        xri = 0
        def ld(st):
            P.dma("sp", "xin0", lambda e, st=st: e.dma_start(out=xin, in_=srcv[:, :, st * TB:(st + 1) * TB]),
                  reads=self.xload_keys(st), writes=[xn])
        ld(0)
        for st in range(NST):
            self.rms_rstd(xin, xn, KC, sq, sqn, rstd, rn)
            self.make_h(xin, xn, gain, rstd, rn, hT, hn)
            if st + 1 < NST:
                ld(st + 1)
            for f in range(32):
                kb, ps = self.bank()
                self.proj(ps, kb, w1, k1, f * 128, hT, hn)
                j = f % 2
                P.op("act", lambda e, ps=ps, j=j: e.activation(out=tmp[:, j, :], in_=ps, func=AF.Relu),
                     reads=[kb], writes=[(tn, j)])
                P.op("dve", lambda e, f=f, j=j: e.tensor_tensor(out=hid[:, f, :], in0=tmp[:, j, :], in1=tmp[:, j, :], op=ALU.mult),
                     reads=[(tn, j)], writes=[(hidn, f)])
            for o in range(KC):
                kb, ps = self.bank()
                for f in range(32):
                    P.op("pe", lambda e, f=f, o=o, ps=ps: e.matmul(ps, w2[:, f, o * 128:(o + 1) * 128], hid[:, f, :],
                                                                  start=(f == 0), stop=(f == 31)),
                         reads=k2(o * 128, (o + 1) * 128, f) + [(hidn, f)], writes=[kb])
                j = xri % 2
                xri += 1
                P.dma("sp", f"xrl{j}", lambda e, st=st, o=o, j=j: e.dma_start(
                    out=xr[:, j, :], in_=src[o * 128:(o + 1) * 128, st * TB:(st + 1) * TB]),
                    reads=[("dr", src_id, st, o)], writes=[(xrn, j)])
                P.op("dve", lambda e, ps=ps, j=j: e.tensor_tensor(out=xr[:, j, :], in0=ps, in1=xr[:, j, :], op=ALU.add),
                     reads=[kb, (xrn, j)], writes=[(xrn, j)])
                self.xstore(st, o, xr[:, j, :], (xrn, j), f"xrs{j}")
        self.end_stage()

    def stage_final(self, src, src_id, dst, dst_id):
        P, A = self.P, self.A
        self.begin_stage(src, src_id, dst, dst_id)
        xn, xin = A.alloc("xin", [128, 2, KC, TB], F32)
        sqn, sq = A.alloc("sq", [128, 2, TB], BF16)
        rn, rstd = A.alloc("rstd", [128, TB], F32)
        gain = self.C("g_fin")
        srcv = src.rearrange("(c p) t -> p c t", p=128)
        for st in range(NST):
            b = st % 2
            P.dma("sp", f"xin{b}", lambda e, st=st, b=b: e.dma_start(out=xin[:, b], in_=srcv[:, :, st * TB:(st + 1) * TB]),
                  reads=self.xload_keys(st), writes=[(xn, b)] + [(xn, b, c) for c in range(KC)])
            self.rms_rstd(xin[:, b], (xn, b), KC, sq, sqn, rstd, rn)
            for c in range(KC):
                P.op("dve", lambda e, c=c, b=b: e.scalar_tensor_tensor(
                    out=xin[:, b, c, :], in0=xin[:, b, c, :], scalar=gain[:, c:c + 1], in1=rstd,
                    op0=ALU.mult, op1=ALU.mult), reads=[(xn, b), rn, "cst"], writes=[(xn, b, c)])
                self.xstore(st, c, xin[:, b, c, :], (xn, b, c), f"fs{c % 4}")
        self.end_stage()

    def stage_xattn(self, l, src, src_id, dst, dst_id):
        P, A = self.P, self.A
        self.begin_stage(src, src_id, dst, dst_id)
        wq, kq, issue_q = self.load_w("wq", self.W["xattn_wq"][l], 8, D, defer=True)
        wo, ko, issue_o = self.load_w("wo", self.W["xattn_wo"][l], 8, D, defer=True)
        ktn, KT = A.alloc("KT", [128, KC, MEM], BF16)
        vn, V = A.alloc("V", [128, 2, D], BF16)
        sqn, sq = A.alloc("sq", [128, 2, TB], BF16)
        rn, rstd = A.alloc("rstd", [128, TB], F32)
        m2 = A.mark()
        wk, kk = self.load_w("wk", self.W["xattn_wk"][l], 8, D)
        wv, kv = self.load_w("wv", self.W["xattn_wv"][l], 8, D)
        issue_q()
        issue_o()
        mn_, mem = A.alloc("mem", [128, KC, MEM], F32)
        mnn, mnT = A.alloc("mnT", [128, KC, MEM], BF16)
        P.dma("sp", "xin0", lambda e: e.dma_start(out=mem, in_=self.memT.rearrange("(c p) t -> p c t", p=128)), writes=[mn_])
        self.rms_rstd(mem, mn_, KC, sq, sqn, rstd, rn, ncols=MEM)
        self.make_h(mem, mn_, self.C("g_mem"), rstd, rn, mnT, mnn, ncols=MEM)
        for n in range(KC):
            kb, ps = self.bank()
            self.proj(ps, kb, wk, kk, n * 128, mnT, mnn, ncols=MEM)
            P.op("act", lambda e, n=n, ps=ps: e.activation(out=KT[:, n, :], in_=ps[:, 0:MEM], func=AF.Copy),
                 reads=[kb], writes=[(ktn, n)])
        for mt in range(2):
            for nb in range(2):
                kb, ps = self.bank()
                for k in range(KC):
                    P.op("pe", lambda e, k=k, mt=mt, nb=nb, ps=ps: e.matmul(
                        ps, mnT[:, k, mt * 128:(mt + 1) * 128], wv[:, k, nb * 512:(nb + 1) * 512],
                        start=(k == 0), stop=(k == KC - 1)), reads=kv(nb * 512, (nb + 1) * 512) + [(mnn, k)], writes=[kb])
                P.op("act", lambda e, mt=mt, nb=nb, ps=ps: e.activation(out=V[:, mt, nb * 512:(nb + 1) * 512], in_=ps, func=AF.Copy),
                     reads=[kb], writes=[(vn, mt, nb)])
        A.release(m2)
        xn, xin = A.alloc("xin", [128, 2, KC, TB], F32)
        hT2 = [A.alloc(f"hT{i}", [128, KC, TB], BF16) for i in range(2)]
        qT2 = [A.alloc(f"qT{i}", [128, KC, TB], BF16) for i in range(2)]
        en, E = A.alloc("E", [128, 4, 2, TB], BF16)
        rdn, rden4 = A.alloc("rden", [128, 4, TB], F32)
        oT2 = [A.alloc(f"oT{i}", [128, KC, TB], BF16) for i in range(2)]
        sq2 = [A.alloc(f"sqx{i}", [128, 2, TB], BF16) for i in range(2)]
        rs2_ = [A.alloc(f"rstdx{i}", [128, TB], F32) for i in range(2)]
        xrn, xr = A.alloc("xr", [128, 3, TB], F32)
        gain = self.C(f"g_xat{l}")
        srcv = src.rearrange("(c p) t -> p c t", p=128)
        xri = 0

        def ld(st):
            b = st % 2
            P.dma("sp", f"xin{b}", lambda e: e.dma_start(out=xin[:, b], in_=srcv[:, :, st * TB:(st + 1) * TB]),
                  reads=self.xload_keys(st), writes=[(xn, b)])
        ld(0)
        for st in range(NST):
            b = st % 2
            if st + 1 < NST:
                ld(st + 1)
            xb = xin[:, b]
            (hn, hT), (qn, qT), (on, oT) = hT2[b], qT2[b], oT2[b]
            (sqn_, sq_), (rn_, rstd_) = sq2[b], rs2_[b]
            self.rms_rstd(xb, (xn, b), KC, sq_, sqn_, rstd_, rn_)
            self.make_h(xb, (xn, b), gain, rstd_, rn_, hT, hn)
            for n in range(KC):
                kb, ps = self.bank()
                self.proj(ps, kb, wq, kq, n * 128, hT, hn)
                P.op("act", lambda e, n=n, ps=ps, qT=qT: e.activation(out=qT[:, n, :], in_=ps, func=AF.Copy, scale=1.0 / 16.0),
                     reads=[kb], writes=[(qn, n)])
            for hd in range(4):
                eb = hd
                rden = rden4[:, hd]
                for mt in range(2):
                    kb, ps = self.bank()
                    for dc in range(2):
                        c = 2 * hd + dc
                        P.op("pe", lambda e, c=c, mt=mt, dc=dc, ps=ps, qT=qT: e.matmul(
                            ps, KT[:, c, mt * 128:(mt + 1) * 128], qT[:, c, :], start=(dc == 0), stop=(dc == 1)),
                            reads=[(ktn, c), (qn, c)], writes=[kb])
                    P.op("act", lambda e, eb=eb, mt=mt, ps=ps: e.activation(out=E[:, eb, mt, :], in_=ps, func=AF.Exp),
                         reads=[kb], writes=[(en, eb, mt)])
                kb, ps = self.bank()
                for mt in range(2):
                    P.op("pe", lambda e, eb=eb, mt=mt, ps=ps: e.matmul(ps, self.ones_bf, E[:, eb, mt, :], start=(mt == 0), stop=(mt == 1)),
                         reads=[(en, eb, mt), "ones_bf"], writes=[kb])
                P.op("dve", lambda e, ps=ps, rden=rden: e.reciprocal(out=rden, in_=ps), reads=[kb], writes=[(rdn, hd)])
                for dc in range(2):
                    c = 2 * hd + dc
                    kb, ps = self.bank()
                    for mt in range(2):
                        P.op("pe", lambda e, c=c, mt=mt, eb=eb, ps=ps: e.matmul(
                            ps, V[:, mt, c * 128:(c + 1) * 128], E[:, eb, mt, :], start=(mt == 0), stop=(mt == 1)),
                            reads=[(vn, mt, c // 4), (en, eb, mt)], writes=[kb])
                    P.op("dve", lambda e, c=c, ps=ps, rden=rden, oT=oT: e.tensor_tensor(out=oT[:, c, :], in0=ps, in1=rden, op=ALU.mult),
                         reads=[kb, (rdn, hd)], writes=[(on, c)])
            for o in range(KC):
                kb, ps = self.bank()
                self.proj(ps, kb, wo, ko, o * 128, oT, on)
                j = xri % 3
                xri += 1
                P.op("dve", lambda e, ps=ps, j=j, o=o, xb=xb: e.tensor_tensor(out=xr[:, j, :], in0=ps, in1=xb[:, o, :], op=ALU.add),
                     reads=[kb, (xn, b)], writes=[(xrn, j)])
                self.xstore(st, o, xr[:, j, :], (xrn, j), f"xrs{j}")
        self.end_stage()

    def stage_conf(self, l, src, src_id, dst, dst_id):
        P, A = self.P, self.A
        o_ = l // 2
        self.begin_stage(src, src_id, dst, dst_id)
        w1, k1 = self.load_w("pw1", self.W["cv_w_pw1"][o_], 8, 2 * D)
        w2, k2 = self.load_w("pw2", self.W["cv_w_pw2"][o_], 8, D)
        NPE = 20
        dgn, dg = A.alloc("diag", [128, NPE * KC, 128], BF16)
        wdw = self.C(f"wdw{o_}")
        for i in range(NPE * KC):
            if i % 2 == 0:
                P.op("dve", lambda e, i=i: e.tensor_scalar(out=dg[:, i, :], in0=self.S("ident"), scalar1=wdw[:, i:i + 1], scalar2=None,
                                                           op0=ALU.mult), reads=["sct", "cst"], writes=[(dgn, i)])
            else:
                P.op("act", lambda e, i=i: e.activation(out=dg[:, i, :], in_=self.S("ident"), func=AF.Copy, scale=wdw[:, i:i + 1]),
                     reads=["sct", "cst"], writes=[(dgn, i)])
        xn, xin = A.alloc("xin", [128, 2, KC, TB], F32)
        hT2 = [A.alloc(f"hT{i}", [128, KC, TB], BF16) for i in range(2)]
        sqn, sq = A.alloc("sq", [128, 2, TB], BF16)
        rn, rstd = A.alloc("rstd", [128, TB], F32)
        ub2 = [A.alloc(f"ubuf{i}", [128, KC, 30 + TB], BF16) for i in range(2)]
        sgn, sg = A.alloc("sig", [128, 2, TB], BF16)
        vbn, vb = A.alloc("vbuf", [128, KC, TB], F32)
        can, cacc = A.alloc("cacc", [128, 4, TB], F32)
        mnn, mean = A.alloc("mean", [128, TB], F32)
        r2n, rs2 = A.alloc("rs2", [128, TB], F32)
        sT2 = [A.alloc("sT", [128, KC, TB], BF16)] * 2
        xrn, xr = A.alloc("xr", [128, 2, TB], F32)
        gain = self.C(f"g_mix{l}")
        bpw1 = self.C(f"bpw1{o_}")
        bdw = self.C(f"bdw{o_}")
        lnw = self.C(f"lnw{o_}")
        lnb = self.C(f"lnb{o_}")
        bpw2 = self.C(f"bpw2{o_}")
        srcv = src.rearrange("(c p) t -> p c t", p=128)
        for c in range(KC):
            P.op("pool", lambda e, c=c: e.memset(ub2[0][1][:, c, 0:30], 0.0), writes=[(ub2[0][0], c)])
        xri = 0

        def ld(st):
            b = st % 2
            P.dma("sp", f"xin{b}", lambda e: e.dma_start(out=xin[:, b], in_=srcv[:, :, st * TB:(st + 1) * TB]),
                  reads=self.xload_keys(st), writes=[(xn, b)])
        ld(0)
        for st in range(NST):
            b = st % 2
            if st + 1 < NST:
                ld(st + 1)
            xb = xin[:, b]
            (hn, hT), (un, ub), (sTn, sT) = hT2[b], ub2[b], sT2[b]
            (unp, ubp) = ub2[1 - b]
            self.rms_rstd(xb, (xn, b), KC, sq, sqn, rstd, rn)
            self.make_h(xb, (xn, b), gain, rstd, rn, hT, hn)
            for c in range(KC):
                kg, psg = self.bank()
                self.proj(psg, kg, w1, k1, D + c * 128, hT, hn)
                j = c % 2
                P.op("act", lambda e, c=c, j=j, psg=psg: e.activation(out=sg[:, j, :], in_=psg, func=AF.Sigmoid,
                                                                      bias=bpw1[:, KC + c:KC + c + 1]),
                     reads=[kg, "cst"], writes=[(sgn, j)])
                ka, psa = self.bank()
                self.proj(psa, ka, w1, k1, c * 128, hT, hn)
                if st > 0:
                    P.op("pool", lambda e, c=c, ub=ub, ubp=ubp: e.tensor_copy(out=ub[:, c, 0:30], in_=ubp[:, c, TB:TB + 30]),
                         reads=[(unp, c)], writes=[(un, c)])
                P.op("dve", lambda e, c=c, j=j, psa=psa, ub=ub: e.scalar_tensor_tensor(
                    out=ub[:, c, 30:30 + TB], in0=psa, scalar=bpw1[:, c:c + 1], in1=sg[:, j, :], op0=ALU.add, op1=ALU.mult),
                    reads=[ka, (sgn, j), "cst"], writes=[(un, c)])
            for g4 in range(KC // 4):
                cs4 = range(g4 * 4, g4 * 4 + 4)
                pbs = {}
                for c in cs4:
                    kb, ps = self.bank()
                    pbs[c] = (kb, ps)
                    for j in range(NPE):
                        P.op("pe", lambda e, c=c, j=j, ps=ps, ub=ub: e.matmul(ps, dg[:, j * KC + c, :], ub[:, c, j:j + TB],
                                                                      start=(j == 0), stop=(j == NPE - 1)),
                             reads=[(dgn, j * KC + c), (un, c)], writes=[kb])
                for c in cs4:
                    aj = c % 4
                    P.op("dve", lambda e, c=c, aj=aj, ub=ub: e.tensor_scalar(out=cacc[:, aj, :], in0=ub[:, c, NPE:NPE + TB],
                                                                      scalar1=wdw[:, NPE * KC + c:NPE * KC + c + 1], scalar2=None, op0=ALU.mult),
                         reads=[(un, c), "cst"], writes=[(can, aj)])
                for j in range(NPE + 1, 31):
                    for c in cs4:
                        aj = c % 4
                        P.op("dve", lambda e, c=c, j=j, aj=aj, ub=ub: e.scalar_tensor_tensor(out=cacc[:, aj, :], in0=ub[:, c, j:j + TB],
                                                                                      scalar=wdw[:, j * KC + c:j * KC + c + 1], in1=cacc[:, aj, :],
                                                                                      op0=ALU.mult, op1=ALU.add),
                             reads=[(un, c), (can, aj), "cst"], writes=[(can, aj)])
                for c in cs4:
                    aj = c % 4
                    kb, ps = pbs[c]
                    P.op("dve", lambda e, c=c, aj=aj, ps=ps: e.scalar_tensor_tensor(out=vb[:, c, :], in0=ps, scalar=bdw[:, c:c + 1], in1=cacc[:, aj, :],
                                                                                    op0=ALU.add, op1=ALU.add),
                         reads=[kb, (can, aj), "cst"], writes=[(vbn, c)])
            km, psm = self.bank()
            for c in range(KC):
                P.op("pe", lambda e, c=c, psm=psm: e.matmul(psm, self.S("ones"), vb[:, c, :], start=(c == 0), stop=(c == KC - 1)),
                     reads=[(vbn, c), "sct"], writes=[km])
            P.op("act", lambda e, psm=psm: e.activation(out=mean, in_=psm, func=AF.Copy, scale=1.0 / D), reads=[km], writes=[mnn])
            for c in range(KC):
                P.op("dve", lambda e, c=c: e.tensor_tensor(out=vb[:, c, :], in0=vb[:, c, :], in1=mean, op=ALU.subtract),
                     reads=[(vbn, c), mnn], writes=[(vbn, c)])
            self.rms_rstd(vb, lambda c: (vbn, c), KC, sq, sqn, rs2, r2n)
            for c in range(KC):
                P.op("dve", lambda e, c=c: e.tensor_tensor(out=vb[:, c, :], in0=vb[:, c, :], in1=rs2, op=ALU.mult),
                     reads=[(vbn, c), r2n], writes=[(vbn, c)])
                P.op("act", lambda e, c=c, sT=sT: e.activation(out=sT[:, c, :], in_=vb[:, c, :], func=AF.Silu,
                                                        scale=lnw[:, c:c + 1], bias=lnb[:, c:c + 1]),
                     reads=[(vbn, c), "cst"], writes=[(sTn, c)])
            for o in range(KC):
                kb, ps = self.bank()
                self.proj(ps, kb, w2, k2, o * 128, sT, sTn)
                j = xri % 2
                xri += 1
                P.op("dve", lambda e, ps=ps, j=j, o=o, xb=xb: e.scalar_tensor_tensor(
                    out=xr[:, j, :], in0=ps, scalar=bpw2[:, o:o + 1], in1=xb[:, o, :], op0=ALU.add, op1=ALU.add),
                    reads=[kb, (xn, b), "cst"], writes=[(xrn, j)])
                self.xstore(st, o, xr[:, j, :], (xrn, j), f"xrs{j}")
        self.end_stage()

    def stage_mixer(self, l, src, src_id, dst, dst_id):
        P, A = self.P, self.A
        e_ = l // 2
        TBm = 256
        NSTm = T // TBm
        NHC = TBm // 64
        NQ = TBm // 128
        self.begin_stage(src, src_id, dst, dst_id)
        win, kin = self.load_w("win", self.W["ab_w_in"][e_], 8, AB_IN)
        wout, kout = self.load_w("wout", self.W["ab_w_out"][e_], 8, D)
        gain = self.C(f"g_mix{l}")
        dvn = f"dv{l}"
        dv = self.dv
        lb = dv[:, 0:4]
        oml = dv[:, 4:8]
        Ab = dv[:, 8:16]
        if e_ == 0:
            P.op("pool", lambda e: e.memset(lb, 0.0), writes=[(dvn, "lb")])
        else:
            P.op("dve", lambda e: e.tensor_tensor(out=lb, in0=self.C("lbl1"), in1=self.C("lbl0"), op=ALU.subtract),
                 reads=["cst"], writes=[(dvn, "lb")])
            P.op("act", lambda e: e.activation(out=lb, in_=lb, func=AF.Sigmoid), reads=[(dvn, "lb")], writes=[(dvn, "lb")])
        P.op("dve", lambda e: e.tensor_scalar(out=oml, in0=lb, scalar1=-1.0, scalar2=1.0, op0=ALU.mult, op1=ALU.add),
             reads=[(dvn, "lb")], writes=[(dvn, "oml")])
        P.op("act", lambda e: e.activation(out=Ab, in_=self.C(f"alog{e_}"), func=AF.Exp), reads=["cst"], writes=[(dvn, "A")])
        P.op("dve", lambda e: e.tensor_scalar(out=Ab, in0=Ab, scalar1=-1.0, scalar2=None, op0=ALU.mult),
             reads=[(dvn, "A")], writes=[(dvn, "A")])
        rsq = float(1.0 / np.sqrt(128.0))
        homl = dv[:, 16:20]
        lbh = dv[:, 20:24]
        qsc = dv[:, 24:28]
        P.op("dve", lambda e: e.tensor_scalar(out=homl, in0=oml, scalar1=0.5, scalar2=None, op0=ALU.mult),
             reads=[(dvn, "oml")], writes=[(dvn, "homl")])
        P.op("dve", lambda e: e.tensor_tensor(out=lbh, in0=lb, in1=homl, op=ALU.add), reads=[(dvn, "lb"), (dvn, "homl")], writes=[(dvn, "lbh")])
        P.op("act", lambda e: e.activation(out=qsc, in_=homl, func=AF.Ln), reads=[(dvn, "homl")], writes=[(dvn, "qsc")])
        dkeys = [(dvn, "lb"), (dvn, "oml"), (dvn, "A"), (dvn, "homl"), (dvn, "lbh"), (dvn, "qsc")]
        onw = self.C(f"onw{e_}")
        scw = self.C(f"scw{e_}")
        scb = self.C(f"scb{e_}")
        dtb = self.C(f"dtb{e_}")
        bgn, bigc = A.alloc("bigc", [128, 1024], F32)
        o_dsk = self.coff[f"dsk{e_}"][0]
        P.dma("sp", "cst", lambda e: e.dma_start(out=bigc, in_=self.cst_d[:, o_dsk:o_dsk + 1024]), writes=[bgn])
        dsk = bigc[:, 0:512]
        snw = bigc[:, 512:1024]
        xn, xin = A.alloc("xin", [128, KC, TBm], F32)
        hT2 = [A.alloc(f"hT{i}", [128, KC, TBm], BF16) for i in range(2)]
        sqn, sq = A.alloc("sq", [128, 2, TBm], BF16)
        sqhn, sqh = A.alloc("sqh", [128, 2, TBm], BF16)
        rn, rstd = A.alloc("rstd", [128, TBm], F32)
        yn_, yT = A.alloc("yT", [128, KC, TBm], BF16)
        xrn, xr = A.alloc("xr", [128, 2, TBm], F32)
        vtok2 = [A.alloc(f"vtok{i}", [64, NHC, 512], BF16) for i in range(2)]
        TS = []
        for s_ in range(2):
            TS.append([A.alloc(f"t{i}s{s_}", [128, TBm], F32) for i in range(4)])
        qt2 = [A.alloc(f"qt{i}", [128, 4, TBm], BF16) for i in range(2)]
        kt2 = [A.alloc(f"kt{i}", [128, 4, TBm], BF16) for i in range(2)]
        sc2 = [A.alloc(f"sc{i}", [128, 3, 4, NHC], F32) for i in range(2)]
        PT2 = [A.alloc(f"PT{i}", [64, 4, NHC, 64], BF16) for i in range(2)]
        kkn, ktok = A.alloc("ktok", [64, 4, NHC, 128], BF16)
        kvn, kvs = A.alloc("kvs", [128, NHC, 4, 128], BF16)
        Smid2 = [A.alloc(f"Smid{i}", [128, NHC, 4, 128], BF16) for i in range(2)]
        Sn, Sf = A.alloc("Sf", [128, 4, 128], F32)
        rawn, raw = A.alloc("raw", [128, 2, 3 + TBm], F32)
        hsn, hist = A.alloc("hist", [128, KC, 4], F32)
        xbc2 = [A.alloc(f"xbc{i}", [128, KC, TBm], BF16) for i in range(2)]
        accn, acc = A.alloc("acc", [128, 2, TBm], F32)
        SSn, SS = A.alloc("SS", [128, 512], F32)
        SBn, SSb2 = A.alloc("SSb", [128, 2, 512], BF16)
        upd_done = {}
        free_T = [0, 1]
        free_S = [0, 1]
        sets = []
        for s_ in range(2):
            d = {}
            for nm_, shp, dt_ in [("dts", [128, 64], F32), ("YD", [128, 8, 128], F32), ("CBm", [128, 2, 128], F32),
                                  ("MT", [128, 8, 128], BF16), ("xs", [128, 512], BF16), ("xdt", [128, 512], BF16),
                                  ("xdw", [128, 512], BF16), ("xsd", [128, 512], BF16), ("Btok", [128, 2, 128], BF16),
                                  ("sz", [128, 512], F32), ("yf", [128, 512], F32), ("ybf", [128, 512], BF16)]:
                d[nm_] = A.alloc(f"{nm_}{s_}", shp, dt_)
            sets.append(d)
        srcv = src.rearrange("(c p) t -> p c t", p=128)
        tri = self.S("tri")
        ones_f = self.S("ones")
        scanmask = self.S("scanmask")[:, 0:TBm]
        P.op("pool", lambda e: e.memset(hist, 0.0), writes=[(hsn, c) for c in range(KC)])
        P.op("pool", lambda e: e.memset(SS, 0.0), writes=[SSn])
        P.op("pool", lambda e: e.memset(SSb2, 0.0), writes=[(SBn, 0), (SBn, 1)])
        rsq = float(1.0 / np.sqrt(128.0))
        xri = [0]

        def inter(*gens):
            gens = list(gens)
            while gens:
                for g in list(gens):
                    try:
                        next(g)
                        yield
                    except StopIteration:
                        gens.remove(g)

        def rolling(genfns, width):
            pending = list(genfns)
            active = []
            while pending or active:
                while pending and len(active) < width:
                    active.append(pending.pop(0)())
                for g in list(active):
                    try:
                        next(g)
                        yield
                    except StopIteration:
                        active.remove(g)

        def seq(*gens):
            for g in gens:
                for _ in g:
                    yield

        def ld(st):
            P.dma("sp", "xin0", lambda e, st=st: e.dma_start(out=xin, in_=srcv[:, :, st * TBm:(st + 1) * TBm]),
                  reads=[("dr", src_id, st // 2, c) for c in range(KC)], writes=[xn])

        def hgrn_A(st, h):
            (hn, hT), (vtn, vtok), (qtn, qt), (ktn, kt) = hT2[st % 2], vtok2[st % 2], qt2[st % 2], kt2[st % 2]
            (scn, sc), (ptn, PT) = sc2[st % 2], PT2[st % 2]
            while not free_T:
                yield
            s_ = free_T.pop(0)
            (t1n, t1), (t2n, t2), (t3n, t3), (t4n, t4) = TS[s_]
            kf, psf = self.gbank(2)
            self.proj(psf, kf, win, kin, 512 + h * 128, hT, hn, ncols=TBm)
            P.op("act", lambda e: e.activation(out=t1, in_=psf[:, 0:TBm], func=AF.Tanh, scale=0.5), reads=[kf], writes=[t1n])
            P.op("act", lambda e: e.activation(out=t2, in_=psf[:, 0:TBm], func=AF.Tanh, scale=-0.5), reads=[kf], writes=[t2n])
            yield
            P.op("act", lambda e: e.activation(out=t1, in_=t1, func=AF.Ln, scale=homl[:, h:h + 1], bias=lbh[:, h:h + 1]),
                 reads=[t1n] + dkeys, writes=[t1n])
            yield
            P.op("dve", lambda e: e.tensor_tensor_scan(out=t3, data0=scanmask, data1=t1, initial=0.0, op0=ALU.mult, op1=ALU.add),
                 reads=[t1n, "sct"], writes=[t3n])
            b3 = t3.rearrange("p (c t) -> p c t", t=64)
            yield
            P.op("act", lambda e: e.activation(out=sc[:, 0, h, :], in_=b3[:, :, 31], func=AF.Exp), reads=[t3n], writes=[(scn, 0, h)])
            P.op("act", lambda e: e.activation(out=sc[:, 1, h, :], in_=b3[:, :, 63], func=AF.Exp), reads=[t3n], writes=[(scn, 1, h)])
            P.op("dve", lambda e: e.tensor_tensor(out=t4.rearrange("p (c t) -> p c t", t=64), in0=b3,
                                                  in1=b3[:, :, 31:32].to_broadcast([128, NHC, 64]), op=ALU.subtract),
                 reads=[t3n], writes=[t4n])
            yield
            P.op("act", lambda e: e.activation(out=t1, in_=t4, func=AF.Exp), reads=[t4n], writes=[t1n])
            P.op("act", lambda e: e.activation(out=t3, in_=t4, func=AF.Exp, scale=-1.0, bias=qsc[:, h:h + 1]),
                 reads=[t4n] + dkeys, writes=[t3n])
            yield
            kq, psq = self.gbank(1)
            self.proj(psq, kq, win, kin, h * 128, hT, hn, ncols=TBm)
            P.op("dve", lambda e: e.scalar_tensor_tensor(out=qt[:, h, :], in0=psq[:, 0:TBm], scalar=rsq, in1=t1, op0=ALU.mult, op1=ALU.mult),
                 reads=[kq, t1n], writes=[(qtn, h)])
            P.op("dve", lambda e: e.scalar_tensor_tensor(out=kt[:, h, :], in0=t2, scalar=1.0, in1=t3, op0=ALU.add, op1=ALU.mult),
                 reads=[t2n, t3n], writes=[(ktn, h)])
            P.op("dve", lambda e: e.tensor_copy(out=sc[:, 2, h, :], in_=t1.rearrange("p (c t) -> p c t", t=64)[:, :, 63]),
                 reads=[t1n], writes=[(scn, 2, h)])
            yield
            ks, pss = self.gbank(1)
            for c in range(NHC):
                cs = slice(c * 64, (c + 1) * 64)
                P.op("pe", lambda e, cs=cs: e.matmul(pss[0:64, cs], kt[:, h, cs], qt[:, h, cs], start=True, stop=True),
                     reads=[(ktn, h), (qtn, h)], writes=[ks])
            P.op("dve", lambda e: e.tensor_tensor(out=PT[:, h], in0=pss[0:64, 0:TBm].rearrange("p (c t) -> p c t", t=64),
                                                  in1=self.mask64[:, None, :].to_broadcast([64, NHC, 64]), op=ALU.mult),
                 reads=[ks, "mask64"], writes=[(ptn, h)])
            yield
            kbf, psb = self.gbank(1)
            for c in range(NHC):
                P.op("pe", lambda e, c=c: e.matmul(psb[0:64, c * 128:(c + 1) * 128], kt[:, h, c * 64:(c + 1) * 64], self.ident_bf,
                                                   start=True, stop=True),
                     reads=[(ktn, h), "ident_bf"], writes=[kbf])
            P.op("act", lambda e: e.activation(out=ktok[:, h].rearrange("p c d -> p (c d)"), in_=psb[0:64, 0:NHC * 128], func=AF.Copy),
                 reads=[kbf], writes=[(kkn, h)])
            yield
            kkv, pkv = self.gbank(1)
            for c in range(NHC):
                P.op("pe", lambda e, c=c: e.matmul(pkv[:, c * 128:(c + 1) * 128], ktok[:, h, c, :], vtok[:, c, h * 128:(h + 1) * 128],
                                                   start=True, stop=True), reads=[(kkn, h), (vtn, c)], writes=[kkv])
            P.op("dve", lambda e: e.tensor_tensor(out=kvs[:, :, h, :], in0=pkv[:, 0:NHC * 128].rearrange("p (c d) -> p c d", d=128),
                                                  in1=sc[:, 2, h, :].unsqueeze(2).to_broadcast([128, NHC, 128]), op=ALU.mult),
                 reads=[kkv, (scn, 2, h)], writes=[(kvn, h)])
            free_T.append(s_)
            yield

        def hgrn_B(st):
            (scn, sc), (smn, Smid) = sc2[st % 2], Smid2[st % 2]
            kv_keys = [(kvn, h) for h in range(4)]
            for c in range(NHC):
                first = (st == 0 and c == 0)
                if not first:
                    P.op("dve", lambda e, c=c: e.tensor_tensor(out=Smid[:, c], in0=Sf,
                                                               in1=sc[:, 0, :, c].unsqueeze(2).to_broadcast([128, 4, 128]), op=ALU.mult),
                         reads=[Sn] + [(scn, 0, h) for h in range(4)], writes=[(smn, c)])
                    P.op("dve", lambda e, c=c: e.tensor_tensor(out=Sf, in0=Sf, in1=sc[:, 1, :, c].unsqueeze(2).to_broadcast([128, 4, 128]),
                                                               op=ALU.mult), reads=[Sn] + [(scn, 1, h) for h in range(4)], writes=[Sn])
                    P.op("dve", lambda e, c=c: e.tensor_tensor(out=Sf, in0=Sf, in1=kvs[:, c], op=ALU.add), reads=[Sn] + kv_keys, writes=[Sn])
                else:
                    P.op("dve", lambda e, c=c: e.tensor_copy(out=Sf, in_=kvs[:, c]), reads=kv_keys, writes=[Sn])
                yield

        def hgrn_C(st, h):
            (hn, hT), (vtn, vtok), (qtn, qt) = hT2[st % 2], vtok2[st % 2], qt2[st % 2]
            (ptn, PT), (smn, Smid) = PT2[st % 2], Smid2[st % 2]
            while not free_T:
                yield
            s_ = free_T.pop(0)
            (t1n, t1), (t2n, t2), (t3n, t3), (t4n, t4) = TS[s_]
            ko_, pso = self.gbank(2)
            for c in range(NHC):
                first = (st == 0 and c == 0)
                cs = slice(c * 64, (c + 1) * 64)
                P.op("pe", lambda e, c=c, cs=cs, first=first: e.matmul(pso[:, cs], vtok[:, c, h * 128:(h + 1) * 128], PT[:, h, c, :],
                                                                       start=True, stop=first),
                     reads=[(vtn, c), (ptn, h)], writes=[ko_])
                if not first:
                    P.op("pe", lambda e, c=c, cs=cs: e.matmul(pso[:, cs], Smid[:, c, h, :], qt[:, h, cs], start=False, stop=True),
                         reads=[(smn, c), (qtn, h)], writes=[ko_])
            j = h % 2
            P.op("act", lambda e: e.activation(out=sq[:, j, :], in_=pso[:, 0:TBm], func=AF.Square), reads=[ko_], writes=[(sqn, j)])
            yield
            kg, psg = self.gbank(1)
            self.proj(psg, kg, win, kin, 1536 + h * 128, hT, hn, ncols=TBm)
            P.op("act", lambda e: e.activation(out=t4, in_=psg[:, 0:TBm], func=AF.Silu), reads=[kg], writes=[t4n])
            yield
            kn_, psn = self.gbank(1)
            P.op("pe", lambda e: e.matmul(psn[:, 0:TBm], self.ones_bf, sq[:, j, :], start=True, stop=True),
                 reads=[(sqn, j), "ones_bf"], writes=[kn_])
            P.op("act", lambda e: e.activation(out=t1, in_=psn[:, 0:TBm], func=AF.Ln, scale=1.0 / 128.0, bias=EPS), reads=[kn_], writes=[t1n])
            yield
            P.op("act", lambda e: e.activation(out=t1, in_=t1, func=AF.Exp, scale=-0.5), reads=[t1n], writes=[t1n])
            P.op("dve", lambda e: e.scalar_tensor_tensor(out=t3, in0=pso[:, 0:TBm], scalar=onw[:, h:h + 1], in1=t1, op0=ALU.mult, op1=ALU.mult),
                 reads=[ko_, t1n, "cst"], writes=[t3n])
            yield
            P.op("dve", lambda e: e.tensor_tensor(out=yT[:, h, :], in0=t3, in1=t4, op=ALU.mult), reads=[t3n, t4n], writes=[(yn_, h)])
            free_T.append(s_)
            yield

        def ssd_conv(st):
            (hn, hT), (xbn, xbc) = hT2[st % 2], xbc2[st % 2]
            for c0 in range(0, KC, 2):
                pair = (c0, c0 + 1)
                for c in pair:
                    kb, ps = self.gbank(1)
                    self.proj(ps, kb, win, kin, 2560 + c * 128, hT, hn, ncols=TBm)
                    rj = c % 2
                    P.op("pool", lambda e, c=c, rj=rj: e.tensor_copy(out=raw[:, rj, 0:3], in_=hist[:, c, 0:3]), reads=[(hsn, c)], writes=[(rawn, rj)])
                    P.op("act", lambda e, rj=rj, ps=ps: e.activation(out=raw[:, rj, 3:3 + TBm], in_=ps[:, 0:TBm], func=AF.Copy), reads=[kb], writes=[(rawn, rj)])
                    P.op("pool", lambda e, c=c, rj=rj: e.tensor_copy(out=hist[:, c, 0:3], in_=raw[:, rj, TBm:TBm + 3]), reads=[(rawn, rj)], writes=[(hsn, c)])
                    yield
                for c in pair:
                    rj = c % 2
                    P.op("dve", lambda e, c=c, rj=rj: e.tensor_scalar(out=acc[:, rj, :], in0=raw[:, rj, 0:TBm], scalar1=scw[:, c:c + 1], scalar2=scb[:, c:c + 1],
                                                                      op0=ALU.mult, op1=ALU.add), reads=[(rawn, rj), "cst"], writes=[(accn, rj)])
                yield
                for j in range(1, 4):
                    for c in pair:
                        rj = c % 2
                        P.op("dve", lambda e, c=c, j=j, rj=rj: e.scalar_tensor_tensor(out=acc[:, rj, :], in0=raw[:, rj, j:j + TBm],
                                                                                      scalar=scw[:, j * 8 + c:j * 8 + c + 1], in1=acc[:, rj, :],
                                                                                      op0=ALU.mult, op1=ALU.add),
                             reads=[(rawn, rj), (accn, rj), "cst"], writes=[(accn, rj)])
                    yield
                for c in pair:
                    rj = c % 2
                    P.op("act", lambda e, c=c, rj=rj: e.activation(out=xbc[:, c, :], in_=acc[:, rj, :], func=AF.Silu), reads=[(accn, rj)], writes=[(xbn, c)])
                yield

        def ssd_chunk(st, q):
            (hn, hT), (xbn, xbc) = hT2[st % 2], xbc2[st % 2]
            while not free_S:
                yield
            si_ = free_S.pop(0)
            S_ = sets[si_]
            gq = st * NQ + q
            SSb = SSb2[:, gq % 2]
            SSb_next = SSb2[:, (gq + 1) % 2]
            kSB, kSBn = (SBn, gq % 2), (SBn, (gq + 1) % 2)
            (dtn, dts), (ydn, YD), (cbn, CBm), (mtn, MT) = S_["dts"], S_["YD"], S_["CBm"], S_["MT"]
            (xsn, xs), (xdn, xdt), (xwn, xdw), (xsdn, xsd) = S_["xs"], S_["xdt"], S_["xdw"], S_["xsd"]
            (btn, Btok), (szn, sz), (yfn, yf), (ybfn, ybf) = S_["Btok"], S_["sz"], S_["yf"], S_["ybf"]
            qs = slice(q * 128, (q + 1) * 128)
            first = (st == 0 and q == 0)
            dt_ = dts[:, 0:8]
            dA = dts[:, 8:16]
            acs = dts[:, 16:32]
            nacs = dts[:, 32:40]
            eacs = dts[:, 40:48]
            wdec = dts[:, 48:56]
            eatot = dts[:, 56:64]
            kd, psd = self.gbank(1)
            for k in range(KC):
                P.op("pe", lambda e, k=k: e.matmul(psd[:, 0:8], hT[:, k, qs], win[:, k, 3584:3592], start=(k == 0), stop=(k == KC - 1)),
                     reads=kin(3584, 3592) + [(hn, k)], writes=[kd])
            P.op("dve", lambda e: e.tensor_tensor(out=dt_, in0=psd[:, 0:8], in1=dtb, op=ALU.add), reads=[kd, "cst"], writes=[(dtn, "dt")])
            yield
            P.op("act", lambda e: e.activation(out=dt_, in_=dt_, func=AF.Exp), reads=[(dtn, "dt")], writes=[(dtn, "dt")])
            P.op("act", lambda e: e.activation(out=dt_, in_=dt_, func=AF.Ln, bias=1.0), reads=[(dtn, "dt")], writes=[(dtn, "dt")])
            P.op("dve", lambda e: e.tensor_tensor(out=dA, in0=dt_, in1=Ab, op=ALU.mult), reads=[(dtn, "dt")] + dkeys, writes=[(dtn, "dA")])
            yield
            kz, psz = self.gbank(1)
            for k in range(KC):
                P.op("pe", lambda e, k=k: e.matmul(psz, hT[:, k, qs], win[:, k, 2048:2560], start=(k == 0), stop=(k == KC - 1)),
                     reads=kin(2048, 2560) + [(hn, k)], writes=[kz])
            P.op("act", lambda e: e.activation(out=sz, in_=psz, func=AF.Silu), reads=[kz], writes=[szn])
            yield
            kc_, psc = self.gbank(1)
            P.op("pe", lambda e: e.matmul(psc[:, 0:8], tri, dA, start=True, stop=True), reads=[(dtn, "dA"), "sct"], writes=[kc_])
            P.op("pe", lambda e: e.matmul(psc[:, 8:16], ones_f, dA, start=True, stop=True), reads=[(dtn, "dA"), "sct"], writes=[kc_])
            P.op("act", lambda e: e.activation(out=acs, in_=psc[:, 0:16], func=AF.Copy), reads=[kc_], writes=[(dtn, "acs")])
            yield
            P.op("dve", lambda e: e.tensor_scalar(out=nacs, in0=acs[:, 0:8], scalar1=-1.0, scalar2=None, op0=ALU.mult),
                 reads=[(dtn, "acs")], writes=[(dtn, "nacs")])
            P.op("act", lambda e: e.activation(out=eacs, in_=acs[:, 0:8], func=AF.Exp), reads=[(dtn, "acs")], writes=[(dtn, "eacs")])
            P.op("dve", lambda e: e.tensor_tensor(out=wdec, in0=acs[:, 8:16], in1=acs[:, 0:8], op=ALU.subtract),
                 reads=[(dtn, "acs")], writes=[(dtn, "wdec")])
            P.op("act", lambda e: e.activation(out=wdec, in_=wdec, func=AF.Exp), reads=[(dtn, "wdec")], writes=[(dtn, "wdec")])
            P.op("act", lambda e: e.activation(out=eatot, in_=acs[:, 8:16], func=AF.Exp), reads=[(dtn, "acs")], writes=[(dtn, "eatot")])
            yield
            P.op("dve", lambda e: e.tensor_tensor(out=YD, in0=tri[:, None, :].to_broadcast([128, 8, 128]),
                                                  in1=dA[:, :, None].to_broadcast([128, 8, 128]), op=ALU.mult),
                 reads=[(dtn, "dA"), "sct"], writes=[(ydn, 0), (ydn, 1)])
            yield
            for half in range(2):
                ka, psa = self.gbank(4)
                P.op("pe", lambda e, half=half, psa=psa: e.matmul(psa, ones_f, YD[:, half * 4:(half + 1) * 4, :].rearrange("p h t -> p (h t)"),
                                                                  start=True, stop=True), reads=[(ydn, half), "sct"], writes=[ka])
                for hh in range(4):
                    h = half * 4 + hh
                    P.op("dve", lambda e, h=h, hh=hh, psa=psa: e.tensor_scalar(out=YD[:, h, :], in0=psa[:, hh * 128:(hh + 1) * 128],
                                                                              scalar1=nacs[:, h:h + 1], scalar2=0.0, op0=ALU.add, op1=ALU.min),
                         reads=[ka, (dtn, "nacs")], writes=[(ydn, half)])
                P.op("act", lambda e, half=half: e.activation(out=YD[:, half * 4:(half + 1) * 4, :], in_=YD[:, half * 4:(half + 1) * 4, :], func=AF.Exp),
                     reads=[(ydn, half)], writes=[(ydn, half)])
                yield
            kcb, pcb = self.gbank(1)
            for g in range(2):
                P.op("pe", lambda e, g=g: e.matmul(pcb[:, g * 128:(g + 1) * 128], xbc[:, 4 + g, qs], xbc[:, 6 + g, qs], start=True, stop=True),
                     reads=[(xbn, 4 + g), (xbn, 6 + g)], writes=[kcb])
            P.op("dve", lambda e: e.tensor_tensor(out=CBm, in0=pcb[:, 0:256].rearrange("p (g t) -> p g t", g=2),
                                                  in1=tri[:, None, :].to_broadcast([128, 2, 128]), op=ALU.mult),
                 reads=[kcb, "sct"], writes=[cbn])
            yield
            P.op("dve", lambda e: e.tensor_tensor(out=MT.rearrange("p (g j) t -> p g j t", g=2),
                                                  in0=YD.rearrange("p (g j) t -> p g j t", g=2),
                                                  in1=CBm[:, :, None, :].to_broadcast([128, 2, 4, 128]), op=ALU.mult),
                 reads=[(ydn, 0), (ydn, 1), cbn], writes=[mtn])
            yield
            kbf, psb = self.gbank(1)
            for c in range(4):
                P.op("pe", lambda e, c=c: e.matmul(psb[:, c * 128:(c + 1) * 128], xbc[:, c, qs], self.ident_bf, start=True, stop=True),
                     reads=[(xbn, c), "ident_bf"], writes=[kbf])
            P.op("act", lambda e: e.activation(out=xs, in_=psb, func=AF.Copy), reads=[kbf], writes=[xsn])
            yield
            kbf2, psb2 = self.gbank(1)
            for g in range(2):
                P.op("pe", lambda e, g=g: e.matmul(psb2[:, g * 128:(g + 1) * 128], xbc[:, 4 + g, qs], self.ident_bf, start=True, stop=True),
                     reads=[(xbn, 4 + g), "ident_bf"], writes=[kbf2])
            P.op("act", lambda e: e.activation(out=Btok.rearrange("p g n -> p (g n)"), in_=psb2[:, 0:256], func=AF.Copy),
                 reads=[kbf2], writes=[btn])
            yield
            xs3 = xs.rearrange("p (h d) -> p h d", d=64)
            P.op("dve", lambda e: e.tensor_tensor(out=xdt.rearrange("p (h d) -> p h d", d=64), in0=xs3,
                                                  in1=dt_[:, :, None].to_broadcast([128, 8, 64]), op=ALU.mult),
                 reads=[xsn, (dtn, "dt")], writes=[xdn])
            P.op("dve", lambda e: e.tensor_tensor(out=xdw.rearrange("p (h d) -> p h d", d=64), in0=xdt.rearrange("p (h d) -> p h d", d=64),
                                                  in1=wdec[:, :, None].to_broadcast([128, 8, 64]), op=ALU.mult),
                 reads=[xdn, (dtn, "wdec")], writes=[xwn])
            P.op("dve", lambda e: e.tensor_tensor(out=xsd, in0=xs, in1=dsk, op=ALU.mult), reads=[xsn, bgn], writes=[xsdn])
            yield
            while gq > 0 and not upd_done.get(gq - 1):
                yield
            if not first:
                kof, pof = self.gbank(1)
                for g in range(2):
                    P.op("pe", lambda e, g=g: e.matmul(pof[:, g * 256:(g + 1) * 256], xbc[:, 6 + g, qs], SSb[:, g * 256:(g + 1) * 256],
                                                       start=True, stop=True), reads=[(xbn, 6 + g), kSB], writes=[kof])
            kst, pst = self.gbank(1)
            for g in range(2):
                P.op("pe", lambda e, g=g: e.matmul(pst[:, g * 256:(g + 1) * 256], Btok[:, g, :], xdw[:, g * 256:(g + 1) * 256], start=True, stop=True),
                     reads=[btn, xwn], writes=[kst])
            if first:
                P.op("dve", lambda e: e.tensor_copy(out=SS, in_=pst), reads=[kst], writes=[SSn])
            else:
                P.op("dve", lambda e: e.tensor_tensor(out=SS.rearrange("p (h d) -> p h d", d=64), in0=SS.rearrange("p (h d) -> p h d", d=64),
                                                      in1=eatot[:, :, None].to_broadcast([128, 8, 64]), op=ALU.mult),
                     reads=[SSn, (dtn, "eatot")], writes=[SSn])
                P.op("dve", lambda e: e.tensor_tensor(out=SS, in0=pst, in1=SS, op=ALU.add), reads=[kst, SSn], writes=[SSn])
            P.op("act", lambda e: e.activation(out=SSb_next, in_=SS, func=AF.Copy), reads=[SSn], writes=[kSBn])
            upd_done[gq] = True
            yield
            ky, psy = self.gbank(1)
            P.op("pe", lambda e: e.matmul(psy, self.ident_bf, xsd, start=True, stop=False), reads=[xsdn, "ident_bf"], writes=[ky])
            for h in range(8):
                P.op("pe", lambda e, h=h: e.matmul(psy[:, h * 64:(h + 1) * 64], MT[:, h, :], xdt[:, h * 64:(h + 1) * 64], start=False, stop=(h == 7)),
                     reads=[mtn, xdn], writes=[ky])
            if not first:
                P.op("dve", lambda e: e.tensor_tensor(out=yf.rearrange("p (h d) -> p h d", d=64), in0=pof.rearrange("p (h d) -> p h d", d=64),
                                                      in1=eacs[:, :, None].to_broadcast([128, 8, 64]), op=ALU.mult),
                     reads=[kof, (dtn, "eacs")], writes=[yfn])
                P.op("dve", lambda e: e.tensor_tensor(out=yf, in0=psy, in1=yf, op=ALU.add), reads=[ky, yfn], writes=[yfn])
            else:
                P.op("dve", lambda e: e.tensor_copy(out=yf, in_=psy), reads=[ky], writes=[yfn])
            yield
            P.op("dve", lambda e: e.tensor_tensor(out=yf, in0=yf, in1=sz, op=ALU.mult), reads=[yfn, szn], writes=[yfn])
            P.op("dve", lambda e: e.memset(acs[:, 0:2], 0.0), reads=[(dtn, "acs")], writes=[(dtn, "acs")])
            yield
            for g in range(2):
                P.op("act", lambda e, g=g: e.activation(out=ybf[:, g * 256:(g + 1) * 256], in_=yf[:, g * 256:(g + 1) * 256], func=AF.Square,
                                                        accum_out=acs[:, g:g + 1]),
                     reads=[yfn, (dtn, "acs")], writes=[ybfn, (dtn, "acs")])
            P.op("act", lambda e: e.activation(out=acs[:, 0:2], in_=acs[:, 0:2], func=AF.Ln, scale=1.0 / 256.0, bias=EPS),
                 reads=[(dtn, "acs")], writes=[(dtn, "acs")])
            P.op("act", lambda e: e.activation(out=acs[:, 0:2], in_=acs[:, 0:2], func=AF.Exp, scale=-0.5),
                 reads=[(dtn, "acs")], writes=[(dtn, "acs")])
            yield
            for g in range(2):
                P.op("dve", lambda e, g=g: e.scalar_tensor_tensor(out=ybf[:, g * 256:(g + 1) * 256], in0=yf[:, g * 256:(g + 1) * 256],
                                                                  scalar=acs[:, g:g + 1], in1=snw[:, g * 256:(g + 1) * 256], op0=ALU.mult, op1=ALU.mult),
                     reads=[yfn, (dtn, "acs"), bgn], writes=[ybfn])
            yield
            kbf3, psb3 = self.gbank(1)
            for c in range(4):
                P.op("pe", lambda e, c=c: e.matmul(psb3[:, c * 128:(c + 1) * 128], ybf[:, c * 128:(c + 1) * 128], self.ident_bf, start=True, stop=True),
                     reads=[ybfn, "ident_bf"], writes=[kbf3])
            P.op("act", lambda e: e.activation(out=yT[:, 4:8, qs], in_=psb3.rearrange("p (c t) -> p c t", c=4), func=AF.Copy),
                 reads=[kbf3], writes=[(yn_, 4), (yn_, 5), (yn_, 6), (yn_, 7)])
            free_S.append(si_)
            yield

        def head(st):
            (hn, hT), (vtn, vtok) = hT2[st % 2], vtok2[st % 2]
            self.rms_rstd(xin, xn, KC, sqh, sqhn, rstd, rn, ncols=TBm, bankfn=self.gbank)
            self.make_h(xin, xn, gain, rstd, rn, hT, hn, ncols=TBm)
            if st + 1 < NSTm:
                ld(st + 1)
            yield
            for c in range(NHC):
                kb, ps = self.gbank(1)
                for k in range(KC):
                    P.op("pe", lambda e, c=c, k=k, ps=ps: e.matmul(ps[0:64, :], hT[:, k, c * 64:(c + 1) * 64], win[:, k, 1024:1536],
                                                                  start=(k == 0), stop=(k == KC - 1)),
                         reads=kin(1024, 1536) + [(hn, k)], writes=[kb])
                P.op("act", lambda e, c=c, ps=ps: e.activation(out=vtok[:, c, :], in_=ps[0:64, :], func=AF.Copy),
                     reads=[kb], writes=[(vtn, c)])
                yield
            for _ in inter(seq(rolling([lambda h=h: hgrn_A(st, h) for h in range(4)], 2), hgrn_B(st)), ssd_conv(st)):
                yield

        def tail(st):
            for _ in inter(rolling([lambda h=h: hgrn_C(st, h) for h in range(4)], 2),
                           rolling([lambda q=q: ssd_chunk(st, q) for q in range(NQ)], 2)):
                yield
            for o in range(KC):
                kb, ps = self.gbank(1)
                self.proj(ps, kb, wout, kout, o * 128, yT, yn_, ncols=TBm)
                j = xri[0] % 2
                xri[0] += 1
                P.dma("sp", f"xrl{j}", lambda e, st=st, o=o, j=j: e.dma_start(
                    out=xr[:, j, :], in_=src[o * 128:(o + 1) * 128, st * TBm:(st + 1) * TBm]),
                    reads=[("dr", src_id, st // 2, o)], writes=[(xrn, j)])
                P.op("dve", lambda e, ps=ps, j=j: e.tensor_tensor(out=xr[:, j, :], in0=ps[:, 0:TBm], in1=xr[:, j, :], op=ALU.add),
                     reads=[kb, (xrn, j)], writes=[(xrn, j)])
                d = dst[o * 128:(o + 1) * 128, st * TBm:(st + 1) * TBm]
                P.dma("sp", f"xrs{j}", lambda e, d=d, j=j: e.dma_start(out=d, in_=xr[:, j, :]), reads=[(xrn, j)],
                      writes=[("dr", dst_id, st // 2, o)], is_out=(self.dst_id == "y"))
                yield

        ld(0)
        for _ in head(0):
            pass
        for st in range(NSTm):
            gens = [tail(st)]
            if st + 1 < NSTm:
                gens.append(head(st + 1))
            for _ in inter(*gens):
                pass
        assert not self.busy, self.busy
        self.end_stage()


def build_program(stages, ncc, nsc, coff, soff):
    nc = bass.Bass("TRN2", target_bir_lowering=False)
    B = Builder(nc, coff, soff, ncc, nsc)
    bufs = {"x": B.xT, "y": B.yT, "s0": B.scr[0], "s1": B.scr[1]}
    cur = "x"
    nxt = 0
    for i, stg in enumerate(stages):
        last = (i == len(stages) - 1)
        dst = "y" if last else f"s{nxt}"
        if not last:
            nxt = 1 - nxt
        kind = stg[0]
        if kind == "mlp":
            B.stage_mlp(stg[1], bufs[cur], cur, bufs[dst], dst)
        elif kind == "xattn":
            B.stage_xattn(stg[1], bufs[cur], cur, bufs[dst], dst)
        elif kind == "conf":
            B.stage_conf(stg[1], bufs[cur], cur, bufs[dst], dst)
        elif kind == "mixer":
            B.stage_mixer(stg[1], bufs[cur], cur, bufs[dst], dst)
        elif kind == "final":
            B.stage_final(bufs[cur], cur, bufs[dst], dst)
        cur = dst
    B.P.flush()
    B.P.final_wait("sp", B.P.out_tokens)
    B.P.emit()
    return nc, B


FULL_STAGES = []
for _l in range(4):
    FULL_STAGES.append(("mixer", _l) if _l % 2 == 0 else ("conf", _l))
    FULL_STAGES.append(("xattn", _l))
    FULL_STAGES.append(("mlp", _l))
FULL_STAGES.append(("final",))


def run(inputs, stages=None):
    stages = FULL_STAGES if stages is None else stages
    x = np.asarray(inputs["x"], np.float32)
    mem = np.asarray(inputs["mem"], np.float32)
    consts, coff = pack_consts(inputs)
    sconsts, soff = struct_consts()
    nc, B = build_program(stages, consts.shape[1], sconsts.shape[1], coff, soff)
    wts = {n: np.ascontiguousarray(np.asarray(inputs[n], np.float32)) for n in WEIGHT_NAMES}
    in_maps = []
    for b in range(8):
        m = {"xT": np.ascontiguousarray(x[b].T), "memT": np.ascontiguousarray(mem[b].T),
             "consts": consts, "sconsts": sconsts}
        m.update(wts)
        in_maps.append(m)
    res = run_bass_kernel_spmd(nc, in_maps, core_ids=list(range(8)))
    out = np.stack([np.ascontiguousarray(r["yT"].T) for r in res.results], axis=0)
    return out.astype(np.float32)


def kernel(**inputs):
    return run(inputs)
```

```python
import numpy as np
import concourse.bass as bass
import concourse.mybir as mybir
from concourse.bass_utils import run_bass_kernel_spmd

F32 = mybir.dt.float32
BF16 = mybir.dt.bfloat16
ALU = mybir.AluOpType
AF = mybir.ActivationFunctionType

ENGS = ("pe", "act", "dve", "pool", "sp")

D = 1024
T = 4096
TB = 512
NST = T // TB
KC = 8
MEM = 256
DFF = 4096
EPS = 1e-6
AB_IN = 3592


class Op:
    __slots__ = ("eng", "idx", "fn", "deps", "dma_waits", "signal", "val", "dma_sem")

    def __init__(self, eng, idx, fn):
        self.eng = eng
        self.idx = idx
        self.fn = fn
        self.deps = {}
        self.dma_waits = {}
        self.signal = False
        self.val = 0
        self.dma_sem = None


class Prog:
    def __init__(self, nc):
        self.nc = nc
        self.ops = {e: [] for e in ENGS}
        self.seen = {e: {} for e in ENGS}
        self.seen_dma = {e: {} for e in ENGS}
        self.lastw = {}
        self.readers = {}
        self.dma_sems = {}
        self.dma_last = {}
        self.esem = {}
        self.inherit = {}
        self.bufkeys = {}
        self.read_hook = None
        self.pending = []
        self.out_tokens = []
        self.do_schedule = True
        self.gseq = 0
        self.gseq_c = {}
        self.gseq_d = {}
        self.opclk = {}
        self.dmaclk = {}
        self.dma_prev = {}

    @staticmethod
    def _bufname(k):
        return k[0] if isinstance(k, tuple) else k

    def _record(self, op, tok, reads, writes):
        eng = op.eng
        cc = {}
        cd = {}

        def add(t):
            if t is None:
                return
            if t[0] == "c":
                _, se, si = t
                if se == eng and se == "pe":
                    return
                if cc.get(se, -1) < si:
                    cc[se] = si
            else:
                _, sn, val = t
                if cd.get(sn, 0) < val:
                    cd[sn] = val

        for k in reads:
            add(self.lastw.get(k))
        for k in writes:
            if k not in self.lastw:
                for t in self.inherit.get(self._bufname(k), ()):
                    add(t)
            add(self.lastw.get(k))
            for t in self.readers.get(k, {}).values():
                add(t)
        if tok[0] == "d":
            prev = self.dma_prev.get(tok[1])
            if prev:
                add(("d", tok[1], prev))
        clk, dclk = self.seen[eng], self.seen_dma[eng]
        cands = [(self.gseq_c[(se, si)], "c", se, si) for se, si in cc.items()]
        cands += [(self.gseq_d[(sn, val)], "d", sn, val) for sn, val in cd.items()]
        cands.sort(reverse=True)
        for _, kind, a, b in cands:
            if kind == "c":
                if clk.get(a, -1) >= b:
                    continue
                op.deps[a] = b
                self.ops[a][b].signal = True
                snap = self.opclk[(a, b)]
                clk[a] = b
            else:
                if dclk.get(a, 0) >= b:
                    continue
                op.dma_waits[a] = b
                snap = self.dmaclk[(a, b)]
                dclk[a] = b
            for k2, v2 in snap[0].items():
                if clk.get(k2, -1) < v2:
                    clk[k2] = v2
            for k2, v2 in snap[1].items():
                if dclk.get(k2, 0) < v2:
                    dclk[k2] = v2
        self.gseq += 1
        snapshot = (dict(clk), dict(dclk))
        if tok[0] == "c":
            self.gseq_c[(tok[1], tok[2])] = self.gseq
            self.opclk[(tok[1], tok[2])] = snapshot
        else:
            self.gseq_d[(tok[1], tok[2])] = self.gseq
            self.dmaclk[(tok[1], tok[2])] = snapshot
        srckey = tok[1]
        for k in reads:
            self.readers.setdefault(k, {})[srckey] = tok
            self.bufkeys.setdefault(self._bufname(k), set()).add(k)
        for k in writes:
            self.lastw[k] = tok
            self.readers[k] = {}
            self.bufkeys.setdefault(self._bufname(k), set()).add(k)

    DEF_W = {"pe": 256, "act": 384, "dve": 384, "pool": 512, "sp": 0}

    def op(self, eng, fn, reads=(), writes=(), w=None):
        self.pending.append(("c", eng, None, fn, tuple(reads), tuple(writes), w, False))
        if self.read_hook is not None:
            self.read_hook(reads)

    def dma(self, eng, semname, fn, reads=(), writes=(), w=None, is_out=False):
        self.pending.append(("d", eng, semname, fn, tuple(reads), tuple(writes), w, is_out))

    class _Fake:
        def __init__(self):
            self.call = None

        def __getattr__(self, name):
            def f(*a, **k):
                self.call = (name, a, k)
                return self
            return f

    @staticmethod
    def _fsize(ap):
        n = 1
        for d in ap.shape[1:]:
            n *= d
        return n

    ACT_CLS = None

    def _probe(self, fn, want_cls=False):
        fk = Prog._Fake()
        try:
            fn(fk)
            name, a, k = fk.call
            if want_cls:
                if name != "activation":
                    return None
                f = k.get("func")
                if f in (AF.Exp, AF.Ln):
                    return "E"
                if f in (AF.Silu, AF.Tanh):
                    return "U"
                if f == AF.Sigmoid:
                    return "S"
                return None
            if name == "matmul":
                rhs = k.get("rhs", a[2] if len(a) > 2 else None)
                lhsT = k.get("lhsT", a[1] if len(a) > 1 else None)
                w = self._fsize(rhs)
                if lhsT.dtype == F32:
                    w *= 4
                return w
            if name == "dma_start":
                out = k.get("out", a[0] if a else None)
                nb = self._fsize(out) * out.shape[0] * (4 if out.dtype == F32 else 2)
                return nb / 150e3
            out = k.get("out", a[0] if a else None)
            return self._fsize(out)
        except Exception:
            return None

    def _dur(self, rec):
        kind, eng, _, fn, _, _, w, _ = rec
        if w is None:
            w = self._probe(fn)
        if kind == "d":
            return 0.08, 2.5 + (w if w is not None else 1.0)
        if w is None:
            w = self.DEF_W[eng]
        if eng == "pe":
            d = max(0.06, w / 2350.0 + 0.012)
        elif eng == "act":
            d = 0.20 + w / 1250.0
        elif eng == "dve":
            d = 0.09 + w / 960.0
        else:
            d = 0.6 + w / 600.0
        return d, d

    def flush(self):
        recs = self.pending
        self.pending = []
        n = len(recs)
        if n == 0:
            return
        if not self.do_schedule:
            order = range(n)
        else:
            order = self._schedule(recs)
        for i in order:
            kind, eng, semname, fn, reads, writes, w, is_out = recs[i]
            if kind == "c":
                self._op_now(eng, fn, reads, writes)
            else:
                tok = self._dma_now(eng, semname, fn, reads, writes)
                if is_out:
                    self.out_tokens.append(tok)

    def _schedule(self, recs):
        import heapq
        n = len(recs)
        preds = [set() for _ in range(n)]
        lastw = {}
        readers = {}
        for i, r in enumerate(recs):
            for k in r[4]:
                j = lastw.get(k)
                if j is not None:
                    preds[i].add(j)
            for k in r[5]:
                j = lastw.get(k)
                if j is not None:
                    preds[i].add(j)
                for j in readers.get(k, ()):
                    preds[i].add(j)
            for k in r[4]:
                readers.setdefault(k, []).append(i)
            for k in r[5]:
                lastw[k] = i
                readers[k] = []
            preds[i].discard(i)
        lastsem = {}
        for i, r in enumerate(recs):
            if r[0] == "d":
                j = lastsem.get(r[2])
                if j is not None:
                    preds[i].add(j)
                lastsem[r[2]] = i
        succs = [[] for _ in range(n)]
        indeg = [0] * n
        for i in range(n):
            indeg[i] = len(preds[i])
            for j in preds[i]:
                succs[j].append(i)
        durs = [self._dur(r) for r in recs]
        acls = [self._probe(r[3], want_cls=True) if r[1] == "act" and r[0] == "c" else None for r in recs]
        cur_cls = [None]
        blevel = [0.0] * n
        for i in range(n - 1, -1, -1):
            b = 0.0
            for k in succs[i]:
                if blevel[k] > b:
                    b = blevel[k]
            blevel[i] = b + durs[i][1] + 0.35
        finish = [0.0] * n
        ready_t = [0.0] * n
        heaps = {e: [] for e in ENGS}
        for i in range(n):
            if indeg[i] == 0:
                heapq.heappush(heaps[recs[i][1]], (0.0, i))
        free = {e: 0.0 for e in ENGS}
        order = []
        LAT = 0.35
        while len(order) < n:
            best = None
            for e in ENGS:
                h = heaps[e]
                if not h:
                    continue
                rt, i = h[0]
                stt = max(rt, free[e])
                if best is None or (stt, i) < (best[0], best[2]):
                    best = (stt, e, i)
            stt, e, i = best
            h = heaps[e]
            slack = 1.0 if e == "act" else 0.05
            cand = [x for x in h if x[0] <= stt + slack]
            if len(cand) > 1:
                if e == "act":
                    pick = max(cand, key=lambda x: (0 if (acls[x[1]] is not None and acls[x[1]] != cur_cls[0]) else 1,
                                                    1 if x[0] <= stt + 0.05 else 0, blevel[x[1]], -x[1]))
                else:
                    pick = max(cand, key=lambda x: (blevel[x[1]], -x[1]))
                h.remove(pick)
                heapq.heapify(h)
                i = pick[1]
                stt = max(stt, pick[0])
            else:
                heapq.heappop(h)
            busy, lat = durs[i]
            if e == "act" and acls[i] is not None:
                if cur_cls[0] is not None and acls[i] != cur_cls[0]:
                    busy += 1.3
                    lat += 1.3
                cur_cls[0] = acls[i]
            free[e] = stt + busy
            finish[i] = stt + lat
            order.append(i)
            for k in succs[i]:
                indeg[k] -= 1
                t = finish[i] + LAT
                if t > ready_t[k]:
                    ready_t[k] = t
                if indeg[k] == 0:
                    heapq.heappush(heaps[recs[k][1]], (ready_t[k], k))
        self.sched_span = getattr(self, "sched_span", 0.0) + max(finish)
        return order

    def _op_now(self, eng, fn, reads=(), writes=()):
        o = Op(eng, len(self.ops[eng]), fn)
        self.ops[eng].append(o)
        tok = ("c", eng, o.idx)
        self._record(o, tok, reads, writes)
        return tok

    def _dma_now(self, eng, semname, fn, reads=(), writes=()):
        if semname not in self.dma_sems:
            self.dma_sems[semname] = [self.nc.alloc_semaphore("d_" + semname), 0]
        ent = self.dma_sems[semname]
        o = Op(eng, len(self.ops[eng]), fn)
        o.dma_sem = ent[0]
        self.ops[eng].append(o)
        self.dma_prev[semname] = ent[1]
        ent[1] += 16
        tok = ("d", semname, ent[1])
        self._record(o, tok, reads, writes)
        return tok

    def collect(self, bufname):
        assert not self.pending
        best = {}
        for k in self.bufkeys.get(bufname, ()):
            toks = list(self.readers.get(k, {}).values())
            if k in self.lastw:
                toks.append(self.lastw[k])
            for t in toks:
                key = (t[0], t[1])
                if key not in best or best[key][2] < t[2]:
                    best[key] = t
            self.lastw.pop(k, None)
            self.readers.pop(k, None)
        self.bufkeys.pop(bufname, None)
        self.inherit.pop(bufname, None)
        return list(best.values())

    def final_wait(self, eng, toks):
        self.flush()
        o = Op(eng, len(self.ops[eng]), None)
        self.ops[eng].append(o)
        for t in toks:
            if t[0] == "c":
                if o.deps.get(t[1], -1) < t[2]:
                    o.deps[t[1]] = t[2]
                    self.ops[t[1]][t[2]].signal = True
            else:
                if o.dma_waits.get(t[1], 0) < t[2]:
                    o.dma_waits[t[1]] = t[2]

    def emit(self):
        nc = self.nc
        for e in ENGS:
            if self.ops[e]:
                self.esem[e] = nc.alloc_semaphore("e_" + e)
        for e in ENGS:
            c = 0
            for o in self.ops[e]:
                if o.signal:
                    c += 1
                o.val = c
        engobj = {"pe": "tensor", "act": "scalar", "dve": "vector", "pool": "gpsimd", "sp": "sync"}
        self.n_inst = {e: len(self.ops[e]) for e in ENGS}
        self.n_wait = {e: 0 for e in ENGS}
        with nc.Block() as block:
            for e in ENGS:
                if not self.ops[e]:
                    continue

                def body(eng, e=e):
                    for o in self.ops[e]:
                        waits = [(self.esem[se], self.ops[se][si].val) for se, si in o.deps.items()]
                        waits += [(self.dma_sems[sn][0], val) for sn, val in o.dma_waits.items()]
                        self.n_wait[e] += len(waits)
                        if o.fn is None:
                            for sem, val in waits:
                                eng.wait_ge(sem, val)
                            continue
                        for sem, val in waits[:-1]:
                            eng.wait_ge(sem, val)
                        ins = o.fn(eng)
                        if waits:
                            ins._wait_ge(*waits[-1])
                        if o.dma_sem is not None:
                            ins.then_inc(o.dma_sem, 16)
                        elif o.signal:
                            ins.then_inc(self.esem[e], 1)

                getattr(block, engobj[e])(body)


class Arena:
    def __init__(self, nc, P, base, size):
        self.nc, self.P = nc, P
        self.base, self.size = base, size
        self.top = 0
        self.live = []
        self.freed = []
        self.uid = 0

    def alloc(self, name, shape, dtype):
        self.P.flush()
        self.uid += 1
        nm = f"{name}_{self.uid}"
        esz = 4 if dtype == F32 else 2
        n = 1
        for s in shape[1:]:
            n *= s
        nbytes = (n * esz + 63) // 64 * 64
        off = self.top
        assert off + nbytes <= self.size, f"SBUF arena overflow allocating {name}: {off + nbytes} > {self.size}"
        self.top += nbytes
        self.peak_top = max(getattr(self, "peak_top", 0), self.top)
        h = self.nc.alloc_sbuf_tensor_at(nm, list(shape), dtype, offset=self.base + off)
        toks = []
        for (fo, fn_, ft) in self.freed:
            if fo < off + nbytes and off < fo + fn_:
                toks.extend(ft)
        if toks:
            self.P.inherit[nm] = toks
        self.live.append((nm, off, nbytes))
        return nm, h.ap()

    def mark(self):
        return (self.top, len(self.live))

    def release(self, mark):
        self.P.flush()
        top, nlive = mark
        for (nm, off, nbytes) in self.live[nlive:]:
            toks = self.P.collect(nm)
            self.freed.append((off, nbytes, toks))
        del self.live[nlive:]
        self.top = top
        if len(self.freed) > 64:
            allt = {}
            lo = min(f[0] for f in self.freed)
            hi = max(f[0] + f[1] for f in self.freed)
            for f in self.freed:
                for t in f[2]:
                    key = (t[0], t[1])
                    if key not in allt or allt[key][2] < t[2]:
                        allt[key] = t
            self.freed = [(lo, hi - lo, list(allt.values()))]


def _cols(v):
    v = np.asarray(v, np.float32)
    return np.ascontiguousarray(v.reshape(-1, 128).T)


def _rep(v):
    v = np.asarray(v, np.float32).reshape(1, -1)
    return np.ascontiguousarray(np.repeat(v, 128, axis=0))


def pack_consts(inp):
    cols = []
    big = []
    off = {}
    cur = [0]

    def add(name, arr):
        off[name] = (cur[0], arr.shape[1])
        cols.append(arr)
        cur[0] += arr.shape[1]

    for l in range(4):
        add(f"g_mix{l}", _cols(inp["norm_mix_w"][l]))
        add(f"g_xat{l}", _cols(inp["norm_xattn_w"][l]))
        add(f"g_mlp{l}", _cols(inp["norm_mlp_w"][l]))
    add("g_fin", _cols(inp["final_norm_w"]))
    add("g_mem", _cols(inp["mem_norm_w"]))
    for e in range(2):
        add(f"lbl{e}", _cols(inp["hgrn_lb_logits"][e]))
        add(f"onw{e}", _cols(inp["hgrn_out_norm_w"][e]))
        cw = np.asarray(inp["ssd_conv_w"][e], np.float32)
        add(f"scw{e}", np.concatenate([_cols(cw[j]) for j in range(4)], axis=1))
        add(f"scb{e}", _cols(inp["ssd_conv_b"][e]))
        add(f"dtb{e}", _rep(inp["ssd_dt_bias"][e]))
        add(f"alog{e}", _rep(inp["ssd_a_log"][e]))
        big.append((f"dsk{e}", _rep(np.repeat(np.asarray(inp["ssd_d"][e], np.float32), 64))))
        big.append((f"snw{e}", _rep(inp["ssd_norm_w"][e])))
    for o in range(2):
        add(f"bpw1{o}", _cols(inp["cv_b_pw1"][o]))
        wd = np.asarray(inp["cv_w_dw"][o], np.float32)
        add(f"wdw{o}", np.concatenate([_cols(wd[j]) for j in range(31)], axis=1))
        add(f"bdw{o}", _cols(inp["cv_b_dw"][o]))
        add(f"lnw{o}", _cols(inp["cv_ln_w"][o]))
        add(f"lnb{o}", _cols(inp["cv_ln_b"][o]))
        add(f"bpw2{o}", _cols(inp["cv_b_pw2"][o]))
    off["_nsmall"] = (cur[0], 0)
    for name, arr in big:
        add(name, arr)
    return np.ascontiguousarray(np.concatenate(cols, axis=1)), off


def struct_consts():
    p = np.arange(128)[:, None]
    j = np.arange(128)[None, :]
    ident = (p == j).astype(np.float32)
    tri = (p <= j).astype(np.float32)
    scanmask = np.ones((128, 512), np.float32)
    scanmask[:, ::64] = 0.0
    ones = np.ones((128, 128), np.float32)
    arr = np.concatenate([ident, tri, scanmask, ones], axis=1)
    off = {"ident": (0, 128), "tri": (128, 128), "scanmask": (256, 512), "ones": (768, 128)}
    return np.ascontiguousarray(arr), off


WEIGHT_NAMES = ["ab_w_in", "ab_w_out", "cv_w_pw1", "cv_w_pw2", "xattn_wq", "xattn_wk", "xattn_wv",
                "xattn_wo", "mlp_w1", "mlp_w2"]
WEIGHT_SHAPES = {"ab_w_in": [2, 1024, 3592], "ab_w_out": [2, 1024, 1024], "cv_w_pw1": [2, 1024, 2048],
                 "cv_w_pw2": [2, 1024, 1024], "xattn_wq": [4, 1024, 1024], "xattn_wk": [4, 1024, 1024],
                 "xattn_wv": [4, 1024, 1024], "xattn_wo": [4, 1024, 1024], "mlp_w1": [4, 1024, 4096],
                 "mlp_w2": [4, 4096, 1024]}


class Builder:
    def __init__(self, nc, coff, soff, ncc, nsc):
        self.nc = nc
        self.P = Prog(nc)
        P = self.P
        self.coff, self.soff = coff, soff
        self.xT = nc.dram_tensor("xT", [D, T], F32, kind="ExternalInput").ap()
        self.memT = nc.dram_tensor("memT", [D, MEM], F32, kind="ExternalInput").ap()
        self.cst_d = nc.dram_tensor("consts", [128, ncc], F32, kind="ExternalInput").ap()
        self.sct_d = nc.dram_tensor("sconsts", [128, nsc], F32, kind="ExternalInput").ap()
        self.W = {n: nc.dram_tensor(n, WEIGHT_SHAPES[n], F32, kind="ExternalInput").ap() for n in WEIGHT_NAMES}
        self.yT = nc.dram_tensor("yT", [D, T], F32, kind="ExternalOutput").ap()
        self.scr = [nc.dram_tensor(f"scr{i}", [D, T], F32, kind="Internal").ap() for i in range(2)]
        total = nc.sbuf_bytes_remaining
        nsm = coff["_nsmall"][0]
        self.cst = nc.alloc_sbuf_tensor("cst", [128, nsm], F32).ap()
        self.sct = nc.alloc_sbuf_tensor("sct", [128, nsc], F32).ap()
        self.ones_bf = nc.alloc_sbuf_tensor("ones_bf", [128, 128], BF16).ap()
        self.ident_bf = nc.alloc_sbuf_tensor("ident_bf", [128, 128], BF16).ap()
        self.mask64 = nc.alloc_sbuf_tensor("mask64", [64, 64], F32).ap()
        self.dv = nc.alloc_sbuf_tensor("derived", [128, 64], F32).ap()
        probe = nc.alloc_sbuf_tensor("arena_probe", [128, 16], F32)
        self.arena_base = nc.lookup_mloc(probe).addr + 64
        self.A = Arena(nc, P, self.arena_base, nc.SBUF_PARTITION_SIZE_BYTES - self.arena_base)
        self.banks = [nc.alloc_psum_tensor(f"ps{i}", [128, 512], F32).ap() for i in range(8)]
        self.bank_i = 0
        self.bfhalf = 0
        self.busy = {}
        self.g_i = 0
        self.gbf_i = 0
        P.read_hook = self._on_reads
        self.wsem = 0
        P.dma("sp", "cst", lambda e: e.dma_start(out=self.cst, in_=self.cst_d[:, 0:nsm]), writes=["cst"])
        P.dma("sp", "sct", lambda e: e.dma_start(out=self.sct, in_=self.sct_d), writes=["sct"])
        so = soff
        P.op("dve", lambda e: e.tensor_copy(out=self.ones_bf, in_=self.S("ones")), reads=["sct"], writes=["ones_bf"])
        P.op("dve", lambda e: e.tensor_copy(out=self.ident_bf, in_=self.S("ident")), reads=["sct"], writes=["ident_bf"])
        P.op("dve", lambda e: e.tensor_copy(out=self.mask64, in_=self.sct[0:64, so["tri"][0]:so["tri"][0] + 64]),
             reads=["sct"], writes=["mask64"])

    def C(self, name, lo=0, n=None):
        o, w = self.coff[name]
        if n is None:
            n = w - lo
        return self.cst[:, o + lo:o + lo + n]

    def S(self, name):
        o, w = self.soff[name]
        return self.sct[:, o:o + w]

    def bank(self):
        i = self.bank_i
        self.bank_i = (i + 1) % 5
        return ("ps", i), self.banks[i]

    def _on_reads(self, reads):
        for k in reads:
            if k in self.busy:
                self.busy[k] -= 1
                if self.busy[k] <= 0:
                    del self.busy[k]

    def gbank(self, n_reads=1):
        for d in range(8):
            i = (self.g_i + d) % 8
            if ("ps", i) not in self.busy:
                self.g_i = (i + 1) % 8
                self.busy[("ps", i)] = n_reads
                return ("ps", i), self.banks[i]
        raise RuntimeError("no free PSUM bank")

    def load_w(self, name, wd, kchunks, ncols, colblk=512, c0=0, defer=False):
        nm, w = self.A.alloc(name, [128, kchunks, ncols], BF16)
        src = wd.rearrange("(k p) n -> p k n", p=128)
        nblk = (ncols + colblk - 1) // colblk
        nk = (kchunks + 7) // 8

        def keys(col_lo, col_hi, k=None):
            bl = range(col_lo // colblk, (col_hi - 1) // colblk + 1)
            if k is None:
                return [(nm, b, kk) for b in bl for kk in range(nk)]
            return [(nm, b, k // 8) for b in bl]

        def issue():
            self._issue_w(nm, w, src, kchunks, ncols, colblk, c0, nblk)
        if defer:
            return w, keys, issue
        issue()
        return w, keys

    def _issue_w(self, nm, w, src, kchunks, ncols, colblk, c0, nblk):
        for b in range(nblk):
            lo = b * colblk
            hi = min(ncols, lo + colblk)
            kstep = max(1, min(kchunks, 8))
            for k0 in range(0, kchunks, kstep):
                self.wsem = (self.wsem + 1) % 8
                self.P.dma("pool", f"w{self.wsem}",
                           lambda e, lo=lo, hi=hi, k0=k0, kstep=kstep: e.dma_start(
                               out=w[:, k0:k0 + kstep, lo:hi], in_=src[:, k0:k0 + kstep, c0 + lo:c0 + hi]),
                           writes=[(nm, b, k0 // kstep)])

    def rms_rstd(self, x, xkey, n, sq, sqn, rstd, rstdn, ncols=TB, bankfn=None):
        P = self.P
        kb, ps = self.bank() if bankfn is None else bankfn(1)
        for c in range(n):
            j = c % 2
            P.op("act", lambda e, c=c, j=j: e.activation(out=sq[:, j, 0:ncols], in_=x[:, c, 0:ncols], func=AF.Square),
                 reads=[xkey(c) if callable(xkey) else xkey], writes=[(sqn, j)])
            P.op("pe", lambda e, c=c, j=j: e.matmul(ps[:, 0:ncols], self.ones_bf, sq[:, j, 0:ncols], start=(c == 0), stop=(c == n - 1)),
                 reads=[(sqn, j), "ones_bf"], writes=[kb])
        P.op("act", lambda e: e.activation(out=rstd[:, 0:ncols], in_=ps[:, 0:ncols], func=AF.Ln, scale=1.0 / (n * 128), bias=EPS),
             reads=[kb], writes=[rstdn])
        P.op("act", lambda e: e.activation(out=rstd[:, 0:ncols], in_=rstd[:, 0:ncols], func=AF.Exp, scale=-0.5),
             reads=[rstdn], writes=[rstdn])

    def make_h(self, x, xkey, gain, rstd, rstdn, hT, hTn, ncols=TB):
        for c in range(KC):
            self.P.op("dve", lambda e, c=c: e.scalar_tensor_tensor(
                out=hT[:, c, 0:ncols], in0=x[:, c, 0:ncols], scalar=gain[:, c:c + 1], in1=rstd[:, 0:ncols],
                op0=ALU.mult, op1=ALU.mult), reads=[xkey, rstdn, "cst"], writes=[(hTn, c)])

    def proj(self, ps, kb, w, wkeys, col, hT, hTn, ncols=TB, m=128):
        for k in range(KC):
            self.P.op("pe", lambda e, k=k: e.matmul(ps[0:m, 0:ncols], w[:, k, col:col + m], hT[:, k, 0:ncols],
                                                     start=(k == 0), stop=(k == KC - 1)),
                      reads=wkeys(col, col + m) + [(hTn, k)], writes=[kb])

    def xstore(self, st, c, buf, bufname, sem):
        d = self.dst[c * 128:(c + 1) * 128, st * TB:(st + 1) * TB]
        self.P.dma("sp", sem, lambda e: e.dma_start(out=d, in_=buf), reads=[bufname], writes=[("dr", self.dst_id, st, c)],
                   is_out=(self.dst_id == "y"))

    def begin_stage(self, src, src_id, dst, dst_id):
        self.src, self.src_id, self.dst, self.dst_id = src, src_id, dst, dst_id
        self.mark = self.A.mark()

    def end_stage(self):
        self.peak = getattr(self, "peak", {})
        self.A.release(self.mark)

    def xload_keys(self, st):
        return [("dr", self.src_id, st, c) for c in range(KC)]

    def stage_mlp(self, l, src, src_id, dst, dst_id):
        P, A = self.P, self.A
        self.begin_stage(src, src_id, dst, dst_id)
        w1, k1 = self.load_w("w1", self.W["mlp_w1"][l], 8, DFF)
        w2, k2 = self.load_w("w2", self.W["mlp_w2"][l], 32, D, colblk=1024)
        xn, xin = A.alloc("xin", [128, KC, TB], F32)
        hn, hT = A.alloc("hT", [128, KC, TB], BF16)
        hidn, hid = A.alloc("hid", [128, 32, TB], BF16)
        sqn, sq = A.alloc("sq", [128, 2, TB], BF16)
        rn, rstd = A.alloc("rstd", [128, TB], F32)
        tn, tmp = A.alloc("tmp", [128, 2, TB], BF16)
        xrn, xr = A.alloc("xr", [128, 2, TB], F32)
        gain = self.C(f"g_mlp{l}")
        srcv = src.rearrange("(c p) t -> p c t", p=128)
        xri = 0
        def ld(st):
            P.dma("sp", "xin0", lambda e, st=st: e.dma_start(out=xin, in_=srcv[:, :, st * TB:(st + 1) * TB]),
                  reads=self.xload_keys(st), writes=[xn])
        ld(0)
        for st in range(NST):
            self.rms_rstd(xin, xn, KC, sq, sqn, rstd, rn)
            self.make_h(xin, xn, gain, rstd, rn, hT, hn)
            if st + 1 < NST:
                ld(st + 1)
            for f in range(32):
                kb, ps = self.bank()
                self.proj(ps, kb, w1, k1, f * 128, hT, hn)
                j = f % 2
                P.op("act", lambda e, ps=ps, j=j: e.activation(out=tmp[:, j, :], in_=ps, func=AF.Relu),
                     reads=[kb], writes=[(tn, j)])
                P.op("dve", lambda e, f=f, j=j: e.tensor_tensor(out=hid[:, f, :], in0=tmp[:, j, :], in1=tmp[:, j, :], op=ALU.mult),
                     reads=[(tn, j)], writes=[(hidn, f)])
            for o in range(KC):
                kb, ps = self.bank()
                for f in range(32):
                    P.op("pe", lambda e, f=f, o=o, ps=ps: e.matmul(ps, w2[:, f, o * 128:(o + 1) * 128], hid[:, f, :],
                                                                  start=(f == 0), stop=(f == 31)),
                         reads=k2(o * 128, (o + 1) * 128, f) + [(hidn, f)], writes=[kb])
                j = xri % 2
                xri += 1
                P.dma("sp", f"xrl{j}", lambda e, st=st, o=o, j=j: e.dma_start(
                    out=xr[:, j, :], in_=src[o * 128:(o + 1) * 128, st * TB:(st + 1) * TB]),
                    reads=[("dr", src_id, st, o)], writes=[(xrn, j)])
                P.op("dve", lambda e, ps=ps, j=j: e.tensor_tensor(out=xr[:, j, :], in0=ps, in1=xr[:, j, :], op=ALU.add),
                     reads=[kb, (xrn, j)], writes=[(xrn, j)])
                self.xstore(st, o, xr[:, j, :], (xrn, j), f"xrs{j}")
        self.end_stage()

    def stage_final(self, src, src_id, dst, dst_id):
        P, A = self.P, self.A
        self.begin_stage(src, src_id, dst, dst_id)
        xn, xin = A.alloc("xin", [128, 2, KC, TB], F32)
        sqn, sq = A.alloc("sq", [128, 2, TB], BF16)
        rn, rstd = A.alloc("rstd", [128, TB], F32)
        gain = self.C("g_fin")
        srcv = src.rearrange("(c p) t -> p c t", p=128)
        for st in range(NST):
            b = st % 2
            P.dma("sp", f"xin{b}", lambda e, st=st, b=b: e.dma_start(out=xin[:, b], in_=srcv[:, :, st * TB:(st + 1) * TB]),
                  reads=self.xload_keys(st), writes=[(xn, b)] + [(xn, b, c) for c in range(KC)])
            self.rms_rstd(xin[:, b], (xn, b), KC, sq, sqn, rstd, rn)
            for c in range(KC):
                P.op("dve", lambda e, c=c, b=b: e.scalar_tensor_tensor(
                    out=xin[:, b, c, :], in0=xin[:, b, c, :], scalar=gain[:, c:c + 1], in1=rstd,
                    op0=ALU.mult, op1=ALU.mult), reads=[(xn, b), rn, "cst"], writes=[(xn, b, c)])
                self.xstore(st, c, xin[:, b, c, :], (xn, b, c), f"fs{c % 4}")
        self.end_stage()

    def stage_xattn(self, l, src, src_id, dst, dst_id):
        P, A = self.P, self.A
        self.begin_stage(src, src_id, dst, dst_id)
        wq, kq, issue_q = self.load_w("wq", self.W["xattn_wq"][l], 8, D, defer=True)
        wo, ko, issue_o = self.load_w("wo", self.W["xattn_wo"][l], 8, D, defer=True)
        ktn, KT = A.alloc("KT", [128, KC, MEM], BF16)
        vn, V = A.alloc("V", [128, 2, D], BF16)
        sqn, sq = A.alloc("sq", [128, 2, TB], BF16)
        rn, rstd = A.alloc("rstd", [128, TB], F32)
        m2 = A.mark()
        wk, kk = self.load_w("wk", self.W["xattn_wk"][l], 8, D)
        wv, kv = self.load_w("wv", self.W["xattn_wv"][l], 8, D)
        issue_q()
        issue_o()
        mn_, mem = A.alloc("mem", [128, KC, MEM], F32)
        mnn, mnT = A.alloc("mnT", [128, KC, MEM], BF16)
        P.dma("sp", "xin0", lambda e: e.dma_start(out=mem, in_=self.memT.rearrange("(c p) t -> p c t", p=128)), writes=[mn_])
        self.rms_rstd(mem, mn_, KC, sq, sqn, rstd, rn, ncols=MEM)
        self.make_h(mem, mn_, self.C("g_mem"), rstd, rn, mnT, mnn, ncols=MEM)
        for n in range(KC):
            kb, ps = self.bank()
            self.proj(ps, kb, wk, kk, n * 128, mnT, mnn, ncols=MEM)
            P.op("act", lambda e, n=n, ps=ps: e.activation(out=KT[:, n, :], in_=ps[:, 0:MEM], func=AF.Copy),
                 reads=[kb], writes=[(ktn, n)])
        for mt in range(2):
            for nb in range(2):
                kb, ps = self.bank()
                for k in range(KC):
                    P.op("pe", lambda e, k=k, mt=mt, nb=nb, ps=ps: e.matmul(
                        ps, mnT[:, k, mt * 128:(mt + 1) * 128], wv[:, k, nb * 512:(nb + 1) * 512],
                        start=(k == 0), stop=(k == KC - 1)), reads=kv(nb * 512, (nb + 1) * 512) + [(mnn, k)], writes=[kb])
                P.op("act", lambda e, mt=mt, nb=nb, ps=ps: e.activation(out=V[:, mt, nb * 512:(nb + 1) * 512], in_=ps, func=AF.Copy),
                     reads=[kb], writes=[(vn, mt, nb)])
        A.release(m2)
        xn, xin = A.alloc("xin", [128, 2, KC, TB], F32)
        hT2 = [A.alloc(f"hT{i}", [128, KC, TB], BF16) for i in range(2)]
        qT2 = [A.alloc(f"qT{i}", [128, KC, TB], BF16) for i in range(2)]
        en, E = A.alloc("E", [128, 4, 2, TB], BF16)
        rdn, rden4 = A.alloc("rden", [128, 4, TB], F32)
        oT2 = [A.alloc(f"oT{i}", [128, KC, TB], BF16) for i in range(2)]
        sq2 = [A.alloc(f"sqx{i}", [128, 2, TB], BF16) for i in range(2)]
        rs2_ = [A.alloc(f"rstdx{i}", [128, TB], F32) for i in range(2)]
        xrn, xr = A.alloc("xr", [128, 3, TB], F32)
        gain = self.C(f"g_xat{l}")
        srcv = src.rearrange("(c p) t -> p c t", p=128)
        xri = 0

        def ld(st):
            b = st % 2
            P.dma("sp", f"xin{b}", lambda e: e.dma_start(out=xin[:, b], in_=srcv[:, :, st * TB:(st + 1) * TB]),
                  reads=self.xload_keys(st), writes=[(xn, b)])
        ld(0)
        for st in range(NST):
            b = st % 2
            if st + 1 < NST:
                ld(st + 1)
            xb = xin[:, b]
            (hn, hT), (qn, qT), (on, oT) = hT2[b], qT2[b], oT2[b]
            (sqn_, sq_), (rn_, rstd_) = sq2[b], rs2_[b]
            self.rms_rstd(xb, (xn, b), KC, sq_, sqn_, rstd_, rn_)
            self.make_h(xb, (xn, b), gain, rstd_, rn_, hT, hn)
            for n in range(KC):
                kb, ps = self.bank()
                self.proj(ps, kb, wq, kq, n * 128, hT, hn)
                P.op("act", lambda e, n=n, ps=ps, qT=qT: e.activation(out=qT[:, n, :], in_=ps, func=AF.Copy, scale=1.0 / 16.0),
                     reads=[kb], writes=[(qn, n)])
            for hd in range(4):
                eb = hd
                rden = rden4[:, hd]
                for mt in range(2):
                    kb, ps = self.bank()
                    for dc in range(2):
                        c = 2 * hd + dc
                        P.op("pe", lambda e, c=c, mt=mt, dc=dc, ps=ps, qT=qT: e.matmul(
                            ps, KT[:, c, mt * 128:(mt + 1) * 128], qT[:, c, :], start=(dc == 0), stop=(dc == 1)),
                            reads=[(ktn, c), (qn, c)], writes=[kb])
                    P.op("act", lambda e, eb=eb, mt=mt, ps=ps: e.activation(out=E[:, eb, mt, :], in_=ps, func=AF.Exp),
                         reads=[kb], writes=[(en, eb, mt)])
                kb, ps = self.bank()
                for mt in range(2):
                    P.op("pe", lambda e, eb=eb, mt=mt, ps=ps: e.matmul(ps, self.ones_bf, E[:, eb, mt, :], start=(mt == 0), stop=(mt == 1)),
                         reads=[(en, eb, mt), "ones_bf"], writes=[kb])
                P.op("dve", lambda e, ps=ps, rden=rden: e.reciprocal(out=rden, in_=ps), reads=[kb], writes=[(rdn, hd)])
                for dc in range(2):
                    c = 2 * hd + dc
                    kb, ps = self.bank()
                    for mt in range(2):
                        P.op("pe", lambda e, c=c, mt=mt, eb=eb, ps=ps: e.matmul(
                            ps, V[:, mt, c * 128:(c + 1) * 128], E[:, eb, mt, :], start=(mt == 0), stop=(mt == 1)),
                            reads=[(vn, mt, c // 4), (en, eb, mt)], writes=[kb])
                    P.op("dve", lambda e, c=c, ps=ps, rden=rden, oT=oT: e.tensor_tensor(out=oT[:, c, :], in0=ps, in1=rden, op=ALU.mult),
                         reads=[kb, (rdn, hd)], writes=[(on, c)])
            for o in range(KC):
                kb, ps = self.bank()
                self.proj(ps, kb, wo, ko, o * 128, oT, on)
                j = xri % 3
                xri += 1
                P.op("dve", lambda e, ps=ps, j=j, o=o, xb=xb: e.tensor_tensor(out=xr[:, j, :], in0=ps, in1=xb[:, o, :], op=ALU.add),
                     reads=[kb, (xn, b)], writes=[(xrn, j)])
                self.xstore(st, o, xr[:, j, :], (xrn, j), f"xrs{j}")
        self.end_stage()

    def stage_conf(self, l, src, src_id, dst, dst_id):
        P, A = self.P, self.A
        o_ = l // 2
        self.begin_stage(src, src_id, dst, dst_id)
        w1, k1 = self.load_w("pw1", self.W["cv_w_pw1"][o_], 8, 2 * D)
        w2, k2 = self.load_w("pw2", self.W["cv_w_pw2"][o_], 8, D)
        NPE = 18
        dgn, dg = A.alloc("diag", [128, NPE * KC, 128], BF16)
        wdw = self.C(f"wdw{o_}")
        for i in range(NPE * KC):
            if i % 2 == 0:
                P.op("dve", lambda e, i=i: e.tensor_scalar(out=dg[:, i, :], in0=self.S("ident"), scalar1=wdw[:, i:i + 1], scalar2=None,
                                                           op0=ALU.mult), reads=["sct", "cst"], writes=[(dgn, i)])
            else:
                P.op("act", lambda e, i=i: e.activation(out=dg[:, i, :], in_=self.S("ident"), func=AF.Copy, scale=wdw[:, i:i + 1]),
                     reads=["sct", "cst"], writes=[(dgn, i)])
        xn, xin = A.alloc("xin", [128, 2, KC, TB], F32)
        hT2 = [A.alloc(f"hT{i}", [128, KC, TB], BF16) for i in range(2)]
        sqn, sq = A.alloc("sq", [128, 2, TB], BF16)
        rn, rstd = A.alloc("rstd", [128, TB], F32)
        ub2 = [A.alloc(f"ubuf{i}", [128, KC, 30 + TB], BF16) for i in range(2)]
        sgn, sg = A.alloc("sig", [128, 2, TB], BF16)
        vbn, vb = A.alloc("vbuf", [128, KC, TB], F32)
        can, cacc = A.alloc("cacc", [128, 4, TB], F32)
        mnn, mean = A.alloc("mean", [128, TB], F32)
        r2n, rs2 = A.alloc("rs2", [128, TB], F32)
        sT2 = [A.alloc("sT", [128, KC, TB], BF16)] * 2
        xrn, xr = A.alloc("xr", [128, 2, TB], F32)
        gain = self.C(f"g_mix{l}")
        bpw1 = self.C(f"bpw1{o_}")
        bdw = self.C(f"bdw{o_}")
        lnw = self.C(f"lnw{o_}")
        lnb = self.C(f"lnb{o_}")
        bpw2 = self.C(f"bpw2{o_}")
        srcv = src.rearrange("(c p) t -> p c t", p=128)
        for c in range(KC):
            P.op("pool", lambda e, c=c: e.memset(ub2[0][1][:, c, 0:30], 0.0), writes=[(ub2[0][0], c)])
        xri = 0

        def ld(st):
            b = st % 2
            P.dma("sp", f"xin{b}", lambda e: e.dma_start(out=xin[:, b], in_=srcv[:, :, st * TB:(st + 1) * TB]),
                  reads=self.xload_keys(st), writes=[(xn, b)])
        ld(0)
        for st in range(NST):
            b = st % 2
            if st + 1 < NST:
                ld(st + 1)
            xb = xin[:, b]
            (hn, hT), (un, ub), (sTn, sT) = hT2[b], ub2[b], sT2[b]
            (unp, ubp) = ub2[1 - b]
            self.rms_rstd(xb, (xn, b), KC, sq, sqn, rstd, rn)
            self.make_h(xb, (xn, b), gain, rstd, rn, hT, hn)
            for c in range(KC):
                kg, psg = self.bank()
                self.proj(psg, kg, w1, k1, D + c * 128, hT, hn)
                j = c % 2
                P.op("act", lambda e, c=c, j=j, psg=psg: e.activation(out=sg[:, j, :], in_=psg, func=AF.Sigmoid,
                                                                      bias=bpw1[:, KC + c:KC + c + 1]),
                     reads=[kg, "cst"], writes=[(sgn, j)])
                ka, psa = self.bank()
                self.proj(psa, ka, w1, k1, c * 128, hT, hn)
                if st > 0:
                    P.op("pool", lambda e, c=c, ub=ub, ubp=ubp: e.tensor_copy(out=ub[:, c, 0:30], in_=ubp[:, c, TB:TB + 30]),
                         reads=[(unp, c)], writes=[(un, c)])
                P.op("dve", lambda e, c=c, j=j, psa=psa, ub=ub: e.scalar_tensor_tensor(
                    out=ub[:, c, 30:30 + TB], in0=psa, scalar=bpw1[:, c:c + 1], in1=sg[:, j, :], op0=ALU.add, op1=ALU.mult),
                    reads=[ka, (sgn, j), "cst"], writes=[(un, c)])
            for g4 in range(KC // 4):
                cs4 = range(g4 * 4, g4 * 4 + 4)
                pbs = {}
                for c in cs4:
                    kb, ps = self.bank()
                    pbs[c] = (kb, ps)
                    for j in range(NPE):
                        P.op("pe", lambda e, c=c, j=j, ps=ps, ub=ub: e.matmul(ps, dg[:, j * KC + c, :], ub[:, c, j:j + TB],
                                                                      start=(j == 0), stop=(j == NPE - 1)),
                             reads=[(dgn, j * KC + c), (un, c)], writes=[kb])
                for c in cs4:
                    aj = c % 4
                    P.op("dve", lambda e, c=c, aj=aj, ub=ub: e.tensor_scalar(out=cacc[:, aj, :], in0=ub[:, c, NPE:NPE + TB],
                                                                      scalar1=wdw[:, NPE * KC + c:NPE * KC + c + 1], scalar2=None, op0=ALU.mult),
                         reads=[(un, c), "cst"], writes=[(can, aj)])
                for j in range(NPE + 1, 31):
                    for c in cs4:
                        aj = c % 4
                        P.op("dve", lambda e, c=c, j=j, aj=aj, ub=ub: e.scalar_tensor_tensor(out=cacc[:, aj, :], in0=ub[:, c, j:j + TB],
                                                                                      scalar=wdw[:, j * KC + c:j * KC + c + 1], in1=cacc[:, aj, :],
                                                                                      op0=ALU.mult, op1=ALU.add),
                             reads=[(un, c), (can, aj), "cst"], writes=[(can, aj)])
                for c in cs4:
                    aj = c % 4
                    kb, ps = pbs[c]
                    P.op("dve", lambda e, c=c, aj=aj, ps=ps: e.scalar_tensor_tensor(out=vb[:, c, :], in0=ps, scalar=bdw[:, c:c + 1], in1=cacc[:, aj, :],
                                                                                    op0=ALU.add, op1=ALU.add),
                         reads=[kb, (can, aj), "cst"], writes=[(vbn, c)])
            km, psm = self.bank()
            for c in range(KC):
                P.op("pe", lambda e, c=c, psm=psm: e.matmul(psm, self.S("ones"), vb[:, c, :], start=(c == 0), stop=(c == KC - 1)),
                     reads=[(vbn, c), "sct"], writes=[km])
            P.op("act", lambda e, psm=psm: e.activation(out=mean, in_=psm, func=AF.Copy, scale=1.0 / D), reads=[km], writes=[mnn])
            for c in range(KC):
                P.op("dve", lambda e, c=c: e.tensor_tensor(out=vb[:, c, :], in0=vb[:, c, :], in1=mean, op=ALU.subtract),
                     reads=[(vbn, c), mnn], writes=[(vbn, c)])
            self.rms_rstd(vb, lambda c: (vbn, c), KC, sq, sqn, rs2, r2n)
            for c in range(KC):
                P.op("dve", lambda e, c=c: e.tensor_tensor(out=vb[:, c, :], in0=vb[:, c, :], in1=rs2, op=ALU.mult),
                     reads=[(vbn, c), r2n], writes=[(vbn, c)])
                P.op("act", lambda e, c=c, sT=sT: e.activation(out=sT[:, c, :], in_=vb[:, c, :], func=AF.Silu,
                                                        scale=lnw[:, c:c + 1], bias=lnb[:, c:c + 1]),
                     reads=[(vbn, c), "cst"], writes=[(sTn, c)])
            for o in range(KC):
                kb, ps = self.bank()
                self.proj(ps, kb, w2, k2, o * 128, sT, sTn)
                j = xri % 2
                xri += 1
                P.op("dve", lambda e, ps=ps, j=j, o=o, xb=xb: e.scalar_tensor_tensor(
                    out=xr[:, j, :], in0=ps, scalar=bpw2[:, o:o + 1], in1=xb[:, o, :], op0=ALU.add, op1=ALU.add),
                    reads=[kb, (xn, b), "cst"], writes=[(xrn, j)])
                self.xstore(st, o, xr[:, j, :], (xrn, j), f"xrs{j}")
        self.end_stage()

    def stage_mixer(self, l, src, src_id, dst, dst_id):
        P, A = self.P, self.A
        e_ = l // 2
        TBm = 256
        NSTm = T // TBm
        NHC = TBm // 64
        NQ = TBm // 128
        self.begin_stage(src, src_id, dst, dst_id)
        win, kin = self.load_w("win", self.W["ab_w_in"][e_], 8, AB_IN)
        wout, kout = self.load_w("wout", self.W["ab_w_out"][e_], 8, D)
        gain = self.C(f"g_mix{l}")
        dvn = f"dv{l}"
        dv = self.dv
        lb = dv[:, 0:4]
        oml = dv[:, 4:8]
        Ab = dv[:, 8:16]
        if e_ == 0:
            P.op("pool", lambda e: e.memset(lb, 0.0), writes=[(dvn, "lb")])
        else:
            P.op("dve", lambda e: e.tensor_tensor(out=lb, in0=self.C("lbl1"), in1=self.C("lbl0"), op=ALU.subtract),
                 reads=["cst"], writes=[(dvn, "lb")])
            P.op("act", lambda e: e.activation(out=lb, in_=lb, func=AF.Sigmoid), reads=[(dvn, "lb")], writes=[(dvn, "lb")])
        P.op("dve", lambda e: e.tensor_scalar(out=oml, in0=lb, scalar1=-1.0, scalar2=1.0, op0=ALU.mult, op1=ALU.add),
             reads=[(dvn, "lb")], writes=[(dvn, "oml")])
        P.op("act", lambda e: e.activation(out=Ab, in_=self.C(f"alog{e_}"), func=AF.Exp), reads=["cst"], writes=[(dvn, "A")])
        P.op("dve", lambda e: e.tensor_scalar(out=Ab, in0=Ab, scalar1=-1.0, scalar2=None, op0=ALU.mult),
             reads=[(dvn, "A")], writes=[(dvn, "A")])
        rsq = float(1.0 / np.sqrt(128.0))
        homl = dv[:, 16:20]
        lbh = dv[:, 20:24]
        qsc = dv[:, 24:28]
        P.op("dve", lambda e: e.tensor_scalar(out=homl, in0=oml, scalar1=0.5, scalar2=None, op0=ALU.mult),
             reads=[(dvn, "oml")], writes=[(dvn, "homl")])
        P.op("dve", lambda e: e.tensor_tensor(out=lbh, in0=lb, in1=homl, op=ALU.add), reads=[(dvn, "lb"), (dvn, "homl")], writes=[(dvn, "lbh")])
        P.op("act", lambda e: e.activation(out=qsc, in_=homl, func=AF.Ln), reads=[(dvn, "homl")], writes=[(dvn, "qsc")])
        dkeys = [(dvn, "lb"), (dvn, "oml"), (dvn, "A"), (dvn, "homl"), (dvn, "lbh"), (dvn, "qsc")]
        onw = self.C(f"onw{e_}")
        scw = self.C(f"scw{e_}")
        scb = self.C(f"scb{e_}")
        dtb = self.C(f"dtb{e_}")
        bgn, bigc = A.alloc("bigc", [128, 1024], F32)
        o_dsk = self.coff[f"dsk{e_}"][0]
        P.dma("sp", "cst", lambda e: e.dma_start(out=bigc, in_=self.cst_d[:, o_dsk:o_dsk + 1024]), writes=[bgn])
        dsk = bigc[:, 0:512]
        snw = bigc[:, 512:1024]
        xn, xin = A.alloc("xin", [128, KC, TBm], F32)
        hT2 = [A.alloc(f"hT{i}", [128, KC, TBm], BF16) for i in range(2)]
        sqn, sq = A.alloc("sq", [128, 2, TBm], BF16)
        sqhn, sqh = A.alloc("sqh", [128, 2, TBm], BF16)
        rn, rstd = A.alloc("rstd", [128, TBm], F32)
        yn_, yT = A.alloc("yT", [128, KC, TBm], BF16)
        xrn, xr = A.alloc("xr", [128, 2, TBm], F32)
        vtok2 = [A.alloc(f"vtok{i}", [64, NHC, 512], BF16) for i in range(2)]
        TS = []
        for s_ in range(2):
            TS.append([A.alloc(f"t{i}s{s_}", [128, TBm], F32) for i in range(4)])
        qt2 = [A.alloc(f"qt{i}", [128, 4, TBm], BF16) for i in range(2)]
        kt2 = [A.alloc(f"kt{i}", [128, 4, TBm], BF16) for i in range(2)]
        sc2 = [A.alloc(f"sc{i}", [128, 3, 4, NHC], F32) for i in range(2)]
        PT2 = [A.alloc(f"PT{i}", [64, 4, NHC, 64], BF16) for i in range(2)]
        kkn, ktok = A.alloc("ktok", [64, 4, NHC, 128], BF16)
        kvn, kvs = A.alloc("kvs", [128, NHC, 4, 128], BF16)
        Smid2 = [A.alloc(f"Smid{i}", [128, NHC, 4, 128], BF16) for i in range(2)]
        Sn, Sf = A.alloc("Sf", [128, 4, 128], F32)
        rawn, raw = A.alloc("raw", [128, 2, 3 + TBm], F32)
        hsn, hist = A.alloc("hist", [128, KC, 4], F32)
        xbc2 = [A.alloc(f"xbc{i}", [128, KC, TBm], BF16) for i in range(2)]
        accn, acc = A.alloc("acc", [128, 2, TBm], F32)
        SSn, SS = A.alloc("SS", [128, 512], F32)
        SBn, SSb2 = A.alloc("SSb", [128, 2, 512], BF16)
        upd_done = {}
        free_T = [0, 1]
        free_S = [0, 1]
        sets = []
        for s_ in range(2):
            d = {}
            for nm_, shp, dt_ in [("dts", [128, 64], F32), ("YD", [128, 8, 128], F32), ("CBm", [128, 2, 128], F32),
                                  ("MT", [128, 8, 128], BF16), ("xs", [128, 512], BF16), ("xdt", [128, 512], BF16),
                                  ("xdw", [128, 512], BF16), ("xsd", [128, 512], BF16), ("Btok", [128, 2, 128], BF16),
                                  ("sz", [128, 512], F32), ("yf", [128, 512], F32), ("ybf", [128, 512], BF16)]:
                d[nm_] = A.alloc(f"{nm_}{s_}", shp, dt_)
            sets.append(d)
        srcv = src.rearrange("(c p) t -> p c t", p=128)
        tri = self.S("tri")
        ones_f = self.S("ones")
        scanmask = self.S("scanmask")[:, 0:TBm]
        P.op("pool", lambda e: e.memset(hist, 0.0), writes=[(hsn, c) for c in range(KC)])
        P.op("pool", lambda e: e.memset(SS, 0.0), writes=[SSn])
        P.op("pool", lambda e: e.memset(SSb2, 0.0), writes=[(SBn, 0), (SBn, 1)])
        rsq = float(1.0 / np.sqrt(128.0))
        xri = [0]

        def inter(*gens):
            gens = list(gens)
            while gens:
                for g in list(gens):
                    try:
                        next(g)
                        yield
                    except StopIteration:
                        gens.remove(g)

        def rolling(genfns, width):
            pending = list(genfns)
            active = []
            while pending or active:
                while pending and len(active) < width:
                    active.append(pending.pop(0)())
                for g in list(active):
                    try:
                        next(g)
                        yield
                    except StopIteration:
                        active.remove(g)

        def seq(*gens):
            for g in gens:
                for _ in g:
                    yield

        def ld(st):
            P.dma("sp", "xin0", lambda e, st=st: e.dma_start(out=xin, in_=srcv[:, :, st * TBm:(st + 1) * TBm]),
                  reads=[("dr", src_id, st // 2, c) for c in range(KC)], writes=[xn])

        def hgrn_A(st, h):
            (hn, hT), (vtn, vtok), (qtn, qt), (ktn, kt) = hT2[st % 2], vtok2[st % 2], qt2[st % 2], kt2[st % 2]
            (scn, sc), (ptn, PT) = sc2[st % 2], PT2[st % 2]
            while not free_T:
                yield
            s_ = free_T.pop(0)
            (t1n, t1), (t2n, t2), (t3n, t3), (t4n, t4) = TS[s_]
            kf, psf = self.gbank(2)
            self.proj(psf, kf, win, kin, 512 + h * 128, hT, hn, ncols=TBm)
            P.op("act", lambda e: e.activation(out=t1, in_=psf[:, 0:TBm], func=AF.Tanh, scale=0.5), reads=[kf], writes=[t1n])
            P.op("act", lambda e: e.activation(out=t2, in_=psf[:, 0:TBm], func=AF.Tanh, scale=-0.5), reads=[kf], writes=[t2n])
            yield
            P.op("act", lambda e: e.activation(out=t1, in_=t1, func=AF.Ln, scale=homl[:, h:h + 1], bias=lbh[:, h:h + 1]),
                 reads=[t1n] + dkeys, writes=[t1n])
            yield
            P.op("dve", lambda e: e.tensor_tensor_scan(out=t3, data0=scanmask, data1=t1, initial=0.0, op0=ALU.mult, op1=ALU.add),
                 reads=[t1n, "sct"], writes=[t3n])
            b3 = t3.rearrange("p (c t) -> p c t", t=64)
            yield
            P.op("act", lambda e: e.activation(out=sc[:, 0, h, :], in_=b3[:, :, 31], func=AF.Exp), reads=[t3n], writes=[(scn, 0, h)])
            P.op("act", lambda e: e.activation(out=sc[:, 1, h, :], in_=b3[:, :, 63], func=AF.Exp), reads=[t3n], writes=[(scn, 1, h)])
            P.op("dve", lambda e: e.tensor_tensor(out=t4.rearrange("p (c t) -> p c t", t=64), in0=b3,
                                                  in1=b3[:, :, 31:32].to_broadcast([128, NHC, 64]), op=ALU.subtract),
                 reads=[t3n], writes=[t4n])
            yield
            P.op("act", lambda e: e.activation(out=t1, in_=t4, func=AF.Exp), reads=[t4n], writes=[t1n])
            P.op("act", lambda e: e.activation(out=t3, in_=t4, func=AF.Exp, scale=-1.0, bias=qsc[:, h:h + 1]),
                 reads=[t4n] + dkeys, writes=[t3n])
            yield
            kq, psq = self.gbank(1)
            self.proj(psq, kq, win, kin, h * 128, hT, hn, ncols=TBm)
            P.op("dve", lambda e: e.scalar_tensor_tensor(out=qt[:, h, :], in0=psq[:, 0:TBm], scalar=rsq, in1=t1, op0=ALU.mult, op1=ALU.mult),
                 reads=[kq, t1n], writes=[(qtn, h)])
            P.op("dve", lambda e: e.scalar_tensor_tensor(out=kt[:, h, :], in0=t2, scalar=1.0, in1=t3, op0=ALU.add, op1=ALU.mult),
                 reads=[t2n, t3n], writes=[(ktn, h)])
            P.op("dve", lambda e: e.tensor_copy(out=sc[:, 2, h, :], in_=t1.rearrange("p (c t) -> p c t", t=64)[:, :, 63]),
                 reads=[t1n], writes=[(scn, 2, h)])
            yield
            ks, pss = self.gbank(1)
            for c in range(NHC):
                cs = slice(c * 64, (c + 1) * 64)
                P.op("pe", lambda e, cs=cs: e.matmul(pss[0:64, cs], kt[:, h, cs], qt[:, h, cs], start=True, stop=True),
                     reads=[(ktn, h), (qtn, h)], writes=[ks])
            P.op("dve", lambda e: e.tensor_tensor(out=PT[:, h], in0=pss[0:64, 0:TBm].rearrange("p (c t) -> p c t", t=64),
                                                  in1=self.mask64[:, None, :].to_broadcast([64, NHC, 64]), op=ALU.mult),
                 reads=[ks, "mask64"], writes=[(ptn, h)])
            yield
            kbf, psb = self.gbank(1)
            for c in range(NHC):
                P.op("pe", lambda e, c=c: e.matmul(psb[0:64, c * 128:(c + 1) * 128], kt[:, h, c * 64:(c + 1) * 64], self.ident_bf,
                                                   start=True, stop=True),
                     reads=[(ktn, h), "ident_bf"], writes=[kbf])
            P.op("act", lambda e: e.activation(out=ktok[:, h].rearrange("p c d -> p (c d)"), in_=psb[0:64, 0:NHC * 128], func=AF.Copy),
                 reads=[kbf], writes=[(kkn, h)])
            yield
            kkv, pkv = self.gbank(1)
            for c in range(NHC):
                P.op("pe", lambda e, c=c: e.matmul(pkv[:, c * 128:(c + 1) * 128], ktok[:, h, c, :], vtok[:, c, h * 128:(h + 1) * 128],
                                                   start=True, stop=True), reads=[(kkn, h), (vtn, c)], writes=[kkv])
            P.op("dve", lambda e: e.tensor_tensor(out=kvs[:, :, h, :], in0=pkv[:, 0:NHC * 128].rearrange("p (c d) -> p c d", d=128),
                                                  in1=sc[:, 2, h, :].unsqueeze(2).to_broadcast([128, NHC, 128]), op=ALU.mult),
                 reads=[kkv, (scn, 2, h)], writes=[(kvn, h)])
            free_T.append(s_)
            yield

        def hgrn_B(st):
            (scn, sc), (smn, Smid) = sc2[st % 2], Smid2[st % 2]
            kv_keys = [(kvn, h) for h in range(4)]
            for c in range(NHC):
                first = (st == 0 and c == 0)
                if not first:
                    P.op("dve", lambda e, c=c: e.tensor_tensor(out=Smid[:, c], in0=Sf,
                                                               in1=sc[:, 0, :, c].unsqueeze(2).to_broadcast([128, 4, 128]), op=ALU.mult),
                         reads=[Sn] + [(scn, 0, h) for h in range(4)], writes=[(smn, c)])
                    P.op("dve", lambda e, c=c: e.tensor_tensor(out=Sf, in0=Sf, in1=sc[:, 1, :, c].unsqueeze(2).to_broadcast([128, 4, 128]),
                                                               op=ALU.mult), reads=[Sn] + [(scn, 1, h) for h in range(4)], writes=[Sn])
                    P.op("dve", lambda e, c=c: e.tensor_tensor(out=Sf, in0=Sf, in1=kvs[:, c], op=ALU.add), reads=[Sn] + kv_keys, writes=[Sn])
                else:
                    P.op("dve", lambda e, c=c: e.tensor_copy(out=Sf, in_=kvs[:, c]), reads=kv_keys, writes=[Sn])
                yield

        def hgrn_C(st, h):
            (hn, hT), (vtn, vtok), (qtn, qt) = hT2[st % 2], vtok2[st % 2], qt2[st % 2]
            (ptn, PT), (smn, Smid) = PT2[st % 2], Smid2[st % 2]
            while not free_T:
                yield
            s_ = free_T.pop(0)
            (t1n, t1), (t2n, t2), (t3n, t3), (t4n, t4) = TS[s_]
            ko_, pso = self.gbank(2)
            for c in range(NHC):
                first = (st == 0 and c == 0)
                cs = slice(c * 64, (c + 1) * 64)
                P.op("pe", lambda e, c=c, cs=cs, first=first: e.matmul(pso[:, cs], vtok[:, c, h * 128:(h + 1) * 128], PT[:, h, c, :],
                                                                       start=True, stop=first),
                     reads=[(vtn, c), (ptn, h)], writes=[ko_])
                if not first:
                    P.op("pe", lambda e, c=c, cs=cs: e.matmul(pso[:, cs], Smid[:, c, h, :], qt[:, h, cs], start=False, stop=True),
                         reads=[(smn, c), (qtn, h)], writes=[ko_])
            j = h % 2
            P.op("act", lambda e: e.activation(out=sq[:, j, :], in_=pso[:, 0:TBm], func=AF.Square), reads=[ko_], writes=[(sqn, j)])
            yield
            kg, psg = self.gbank(1)
            self.proj(psg, kg, win, kin, 1536 + h * 128, hT, hn, ncols=TBm)
            P.op("act", lambda e: e.activation(out=t4, in_=psg[:, 0:TBm], func=AF.Silu), reads=[kg], writes=[t4n])
            yield
            kn_, psn = self.gbank(1)
            P.op("pe", lambda e: e.matmul(psn[:, 0:TBm], self.ones_bf, sq[:, j, :], start=True, stop=True),
                 reads=[(sqn, j), "ones_bf"], writes=[kn_])
            P.op("act", lambda e: e.activation(out=t1, in_=psn[:, 0:TBm], func=AF.Ln, scale=1.0 / 128.0, bias=EPS), reads=[kn_], writes=[t1n])
            yield
            P.op("act", lambda e: e.activation(out=t1, in_=t1, func=AF.Exp, scale=-0.5), reads=[t1n], writes=[t1n])
            P.op("dve", lambda e: e.scalar_tensor_tensor(out=t3, in0=pso[:, 0:TBm], scalar=onw[:, h:h + 1], in1=t1, op0=ALU.mult, op1=ALU.mult),
                 reads=[ko_, t1n, "cst"], writes=[t3n])
            yield
            P.op("dve", lambda e: e.tensor_tensor(out=yT[:, h, :], in0=t3, in1=t4, op=ALU.mult), reads=[t3n, t4n], writes=[(yn_, h)])
            free_T.append(s_)
            yield

        def ssd_conv(st):
            (hn, hT), (xbn, xbc) = hT2[st % 2], xbc2[st % 2]
            for c0 in range(0, KC, 2):
                pair = (c0, c0 + 1)
                for c in pair:
                    kb, ps = self.gbank(1)
                    self.proj(ps, kb, win, kin, 2560 + c * 128, hT, hn, ncols=TBm)
                    rj = c % 2
                    P.op("pool", lambda e, c=c, rj=rj: e.tensor_copy(out=raw[:, rj, 0:3], in_=hist[:, c, 0:3]), reads=[(hsn, c)], writes=[(rawn, rj)])
                    P.op("act", lambda e, rj=rj, ps=ps: e.activation(out=raw[:, rj, 3:3 + TBm], in_=ps[:, 0:TBm], func=AF.Copy), reads=[kb], writes=[(rawn, rj)])
                    P.op("pool", lambda e, c=c, rj=rj: e.tensor_copy(out=hist[:, c, 0:3], in_=raw[:, rj, TBm:TBm + 3]), reads=[(rawn, rj)], writes=[(hsn, c)])
                    yield
                for c in pair:
                    rj = c % 2
                    P.op("dve", lambda e, c=c, rj=rj: e.tensor_scalar(out=acc[:, rj, :], in0=raw[:, rj, 0:TBm], scalar1=scw[:, c:c + 1], scalar2=scb[:, c:c + 1],
                                                                      op0=ALU.mult, op1=ALU.add), reads=[(rawn, rj), "cst"], writes=[(accn, rj)])
                yield
                for j in range(1, 4):
                    for c in pair:
                        rj = c % 2
                        P.op("dve", lambda e, c=c, j=j, rj=rj: e.scalar_tensor_tensor(out=acc[:, rj, :], in0=raw[:, rj, j:j + TBm],
                                                                                      scalar=scw[:, j * 8 + c:j * 8 + c + 1], in1=acc[:, rj, :],
                                                                                      op0=ALU.mult, op1=ALU.add),
                             reads=[(rawn, rj), (accn, rj), "cst"], writes=[(accn, rj)])
                    yield
                for c in pair:
                    rj = c % 2
                    P.op("act", lambda e, c=c, rj=rj: e.activation(out=xbc[:, c, :], in_=acc[:, rj, :], func=AF.Silu), reads=[(accn, rj)], writes=[(xbn, c)])
                yield

        def ssd_chunk(st, q):
            (hn, hT), (xbn, xbc) = hT2[st % 2], xbc2[st % 2]
            while not free_S:
                yield
            si_ = free_S.pop(0)
            S_ = sets[si_]
            gq = st * NQ + q
            SSb = SSb2[:, gq % 2]
            SSb_next = SSb2[:, (gq + 1) % 2]
            kSB, kSBn = (SBn, gq % 2), (SBn, (gq + 1) % 2)
            (dtn, dts), (ydn, YD), (cbn, CBm), (mtn, MT) = S_["dts"], S_["YD"], S_["CBm"], S_["MT"]
            (xsn, xs), (xdn, xdt), (xwn, xdw), (xsdn, xsd) = S_["xs"], S_["xdt"], S_["xdw"], S_["xsd"]
            (btn, Btok), (szn, sz), (yfn, yf), (ybfn, ybf) = S_["Btok"], S_["sz"], S_["yf"], S_["ybf"]
            qs = slice(q * 128, (q + 1) * 128)
            first = (st == 0 and q == 0)
            dt_ = dts[:, 0:8]
            dA = dts[:, 8:16]
            acs = dts[:, 16:32]
            nacs = dts[:, 32:40]
            eacs = dts[:, 40:48]
            wdec = dts[:, 48:56]
            eatot = dts[:, 56:64]
            kd, psd = self.gbank(1)
            for k in range(KC):
                P.op("pe", lambda e, k=k: e.matmul(psd[:, 0:8], hT[:, k, qs], win[:, k, 3584:3592], start=(k == 0), stop=(k == KC - 1)),
                     reads=kin(3584, 3592) + [(hn, k)], writes=[kd])
            P.op("dve", lambda e: e.tensor_tensor(out=dt_, in0=psd[:, 0:8], in1=dtb, op=ALU.add), reads=[kd, "cst"], writes=[(dtn, "dt")])
            yield
            P.op("act", lambda e: e.activation(out=dt_, in_=dt_, func=AF.Exp), reads=[(dtn, "dt")], writes=[(dtn, "dt")])
            P.op("act", lambda e: e.activation(out=dt_, in_=dt_, func=AF.Ln, bias=1.0), reads=[(dtn, "dt")], writes=[(dtn, "dt")])
            P.op("dve", lambda e: e.tensor_tensor(out=dA, in0=dt_, in1=Ab, op=ALU.mult), reads=[(dtn, "dt")] + dkeys, writes=[(dtn, "dA")])
            yield
            kz, psz = self.gbank(1)
            for k in range(KC):
                P.op("pe", lambda e, k=k: e.matmul(psz, hT[:, k, qs], win[:, k, 2048:2560], start=(k == 0), stop=(k == KC - 1)),
                     reads=kin(2048, 2560) + [(hn, k)], writes=[kz])
            P.op("act", lambda e: e.activation(out=sz, in_=psz, func=AF.Silu), reads=[kz], writes=[szn])
            yield
            kc_, psc = self.gbank(1)
            P.op("pe", lambda e: e.matmul(psc[:, 0:8], tri, dA, start=True, stop=True), reads=[(dtn, "dA"), "sct"], writes=[kc_])
            P.op("pe", lambda e: e.matmul(psc[:, 8:16], ones_f, dA, start=True, stop=True), reads=[(dtn, "dA"), "sct"], writes=[kc_])
            P.op("act", lambda e: e.activation(out=acs, in_=psc[:, 0:16], func=AF.Copy), reads=[kc_], writes=[(dtn, "acs")])
            yield
            P.op("dve", lambda e: e.tensor_scalar(out=nacs, in0=acs[:, 0:8], scalar1=-1.0, scalar2=None, op0=ALU.mult),
                 reads=[(dtn, "acs")], writes=[(dtn, "nacs")])
            P.op("act", lambda e: e.activation(out=eacs, in_=acs[:, 0:8], func=AF.Exp), reads=[(dtn, "acs")], writes=[(dtn, "eacs")])
            P.op("dve", lambda e: e.tensor_tensor(out=wdec, in0=acs[:, 8:16], in1=acs[:, 0:8], op=ALU.subtract),
                 reads=[(dtn, "acs")], writes=[(dtn, "wdec")])
            P.op("act", lambda e: e.activation(out=wdec, in_=wdec, func=AF.Exp), reads=[(dtn, "wdec")], writes=[(dtn, "wdec")])
            P.op("act", lambda e: e.activation(out=eatot, in_=acs[:, 8:16], func=AF.Exp), reads=[(dtn, "acs")], writes=[(dtn, "eatot")])
            yield
            P.op("dve", lambda e: e.tensor_tensor(out=YD, in0=tri[:, None, :].to_broadcast([128, 8, 128]),
                                                  in1=dA[:, :, None].to_broadcast([128, 8, 128]), op=ALU.mult),
                 reads=[(dtn, "dA"), "sct"], writes=[(ydn, 0), (ydn, 1)])
            yield
            for half in range(2):
                ka, psa = self.gbank(4)
                P.op("pe", lambda e, half=half, psa=psa: e.matmul(psa, ones_f, YD[:, half * 4:(half + 1) * 4, :].rearrange("p h t -> p (h t)"),
                                                                  start=True, stop=True), reads=[(ydn, half), "sct"], writes=[ka])
                for hh in range(4):
                    h = half * 4 + hh
                    P.op("dve", lambda e, h=h, hh=hh, psa=psa: e.tensor_scalar(out=YD[:, h, :], in0=psa[:, hh * 128:(hh + 1) * 128],
                                                                              scalar1=nacs[:, h:h + 1], scalar2=0.0, op0=ALU.add, op1=ALU.min),
                         reads=[ka, (dtn, "nacs")], writes=[(ydn, half)])
                P.op("act", lambda e, half=half: e.activation(out=YD[:, half * 4:(half + 1) * 4, :], in_=YD[:, half * 4:(half + 1) * 4, :], func=AF.Exp),
                     reads=[(ydn, half)], writes=[(ydn, half)])
                yield
            kcb, pcb = self.gbank(1)
            for g in range(2):
                P.op("pe", lambda e, g=g: e.matmul(pcb[:, g * 128:(g + 1) * 128], xbc[:, 4 + g, qs], xbc[:, 6 + g, qs], start=True, stop=True),
                     reads=[(xbn, 4 + g), (xbn, 6 + g)], writes=[kcb])
            P.op("dve", lambda e: e.tensor_tensor(out=CBm, in0=pcb[:, 0:256].rearrange("p (g t) -> p g t", g=2),
                                                  in1=tri[:, None, :].to_broadcast([128, 2, 128]), op=ALU.mult),
                 reads=[kcb, "sct"], writes=[cbn])
            yield
            P.op("dve", lambda e: e.tensor_tensor(out=MT.rearrange("p (g j) t -> p g j t", g=2),
                                                  in0=YD.rearrange("p (g j) t -> p g j t", g=2),
                                                  in1=CBm[:, :, None, :].to_broadcast([128, 2, 4, 128]), op=ALU.mult),
                 reads=[(ydn, 0), (ydn, 1), cbn], writes=[mtn])
            yield
            kbf, psb = self.gbank(1)
            for c in range(4):
                P.op("pe", lambda e, c=c: e.matmul(psb[:, c * 128:(c + 1) * 128], xbc[:, c, qs], self.ident_bf, start=True, stop=True),
                     reads=[(xbn, c), "ident_bf"], writes=[kbf])
            P.op("act", lambda e: e.activation(out=xs, in_=psb, func=AF.Copy), reads=[kbf], writes=[xsn])
            yield
            kbf2, psb2 = self.gbank(1)
            for g in range(2):
                P.op("pe", lambda e, g=g: e.matmul(psb2[:, g * 128:(g + 1) * 128], xbc[:, 4 + g, qs], self.ident_bf, start=True, stop=True),
                     reads=[(xbn, 4 + g), "ident_bf"], writes=[kbf2])
            P.op("act", lambda e: e.activation(out=Btok.rearrange("p g n -> p (g n)"), in_=psb2[:, 0:256], func=AF.Copy),
                 reads=[kbf2], writes=[btn])
            yield
            xs3 = xs.rearrange("p (h d) -> p h d", d=64)
            P.op("dve", lambda e: e.tensor_tensor(out=xdt.rearrange("p (h d) -> p h d", d=64), in0=xs3,
                                                  in1=dt_[:, :, None].to_broadcast([128, 8, 64]), op=ALU.mult),
                 reads=[xsn, (dtn, "dt")], writes=[xdn])
            P.op("dve", lambda e: e.tensor_tensor(out=xdw.rearrange("p (h d) -> p h d", d=64), in0=xdt.rearrange("p (h d) -> p h d", d=64),
                                                  in1=wdec[:, :, None].to_broadcast([128, 8, 64]), op=ALU.mult),
                 reads=[xdn, (dtn, "wdec")], writes=[xwn])
            P.op("dve", lambda e: e.tensor_tensor(out=xsd, in0=xs, in1=dsk, op=ALU.mult), reads=[xsn, bgn], writes=[xsdn])
            yield
            while gq > 0 and not upd_done.get(gq - 1):
                yield
            if not first:
                kof, pof = self.gbank(1)
                for g in range(2):
                    P.op("pe", lambda e, g=g: e.matmul(pof[:, g * 256:(g + 1) * 256], xbc[:, 6 + g, qs], SSb[:, g * 256:(g + 1) * 256],
                                                       start=True, stop=True), reads=[(xbn, 6 + g), kSB], writes=[kof])
            kst, pst = self.gbank(1)
            for g in range(2):
                P.op("pe", lambda e, g=g: e.matmul(pst[:, g * 256:(g + 1) * 256], Btok[:, g, :], xdw[:, g * 256:(g + 1) * 256], start=True, stop=True),
                     reads=[btn, xwn], writes=[kst])
            if first:
                P.op("dve", lambda e: e.tensor_copy(out=SS, in_=pst), reads=[kst], writes=[SSn])
            else:
                P.op("dve", lambda e: e.tensor_tensor(out=SS.rearrange("p (h d) -> p h d", d=64), in0=SS.rearrange("p (h d) -> p h d", d=64),
                                                      in1=eatot[:, :, None].to_broadcast([128, 8, 64]), op=ALU.mult),
                     reads=[SSn, (dtn, "eatot")], writes=[SSn])
                P.op("dve", lambda e: e.tensor_tensor(out=SS, in0=pst, in1=SS, op=ALU.add), reads=[kst, SSn], writes=[SSn])
            P.op("act", lambda e: e.activation(out=SSb_next, in_=SS, func=AF.Copy), reads=[SSn], writes=[kSBn])
            upd_done[gq] = True
            yield
            ky, psy = self.gbank(1)
            P.op("pe", lambda e: e.matmul(psy, self.ident_bf, xsd, start=True, stop=False), reads=[xsdn, "ident_bf"], writes=[ky])
            for h in range(8):
                P.op("pe", lambda e, h=h: e.matmul(psy[:, h * 64:(h + 1) * 64], MT[:, h, :], xdt[:, h * 64:(h + 1) * 64], start=False, stop=(h == 7)),
                     reads=[mtn, xdn], writes=[ky])
            if not first:
                P.op("dve", lambda e: e.tensor_tensor(out=yf.rearrange("p (h d) -> p h d", d=64), in0=pof.rearrange("p (h d) -> p h d", d=64),
                                                      in1=eacs[:, :, None].to_broadcast([128, 8, 64]), op=ALU.mult),
                     reads=[kof, (dtn, "eacs")], writes=[yfn])
                P.op("dve", lambda e: e.tensor_tensor(out=yf, in0=psy, in1=yf, op=ALU.add), reads=[ky, yfn], writes=[yfn])
            else:
                P.op("dve", lambda e: e.tensor_copy(out=yf, in_=psy), reads=[ky], writes=[yfn])
            yield
            P.op("dve", lambda e: e.tensor_tensor(out=yf, in0=yf, in1=sz, op=ALU.mult), reads=[yfn, szn], writes=[yfn])
            P.op("dve", lambda e: e.memset(acs[:, 0:2], 0.0), reads=[(dtn, "acs")], writes=[(dtn, "acs")])
            yield
            for g in range(2):
                P.op("act", lambda e, g=g: e.activation(out=ybf[:, g * 256:(g + 1) * 256], in_=yf[:, g * 256:(g + 1) * 256], func=AF.Square,
                                                        accum_out=acs[:, g:g + 1]),
                     reads=[yfn, (dtn, "acs")], writes=[ybfn, (dtn, "acs")])
            P.op("act", lambda e: e.activation(out=acs[:, 0:2], in_=acs[:, 0:2], func=AF.Ln, scale=1.0 / 256.0, bias=EPS),
                 reads=[(dtn, "acs")], writes=[(dtn, "acs")])
            P.op("act", lambda e: e.activation(out=acs[:, 0:2], in_=acs[:, 0:2], func=AF.Exp, scale=-0.5),
                 reads=[(dtn, "acs")], writes=[(dtn, "acs")])
            yield
            for g in range(2):
                P.op("dve", lambda e, g=g: e.scalar_tensor_tensor(out=ybf[:, g * 256:(g + 1) * 256], in0=yf[:, g * 256:(g + 1) * 256],
                                                                  scalar=acs[:, g:g + 1], in1=snw[:, g * 256:(g + 1) * 256], op0=ALU.mult, op1=ALU.mult),
                     reads=[yfn, (dtn, "acs"), bgn], writes=[ybfn])
            yield
            kbf3, psb3 = self.gbank(1)
            for c in range(4):
                P.op("pe", lambda e, c=c: e.matmul(psb3[:, c * 128:(c + 1) * 128], ybf[:, c * 128:(c + 1) * 128], self.ident_bf, start=True, stop=True),
                     reads=[ybfn, "ident_bf"], writes=[kbf3])
            P.op("act", lambda e: e.activation(out=yT[:, 4:8, qs], in_=psb3.rearrange("p (c t) -> p c t", c=4), func=AF.Copy),
                 reads=[kbf3], writes=[(yn_, 4), (yn_, 5), (yn_, 6), (yn_, 7)])
            free_S.append(si_)
            yield

        def head(st):
            (hn, hT), (vtn, vtok) = hT2[st % 2], vtok2[st % 2]
            self.rms_rstd(xin, xn, KC, sqh, sqhn, rstd, rn, ncols=TBm, bankfn=self.gbank)
            self.make_h(xin, xn, gain, rstd, rn, hT, hn, ncols=TBm)
            if st + 1 < NSTm:
                ld(st + 1)
            yield
            for c in range(NHC):
                kb, ps = self.gbank(1)
                for k in range(KC):
                    P.op("pe", lambda e, c=c, k=k, ps=ps: e.matmul(ps[0:64, :], hT[:, k, c * 64:(c + 1) * 64], win[:, k, 1024:1536],
                                                                  start=(k == 0), stop=(k == KC - 1)),
                         reads=kin(1024, 1536) + [(hn, k)], writes=[kb])
                P.op("act", lambda e, c=c, ps=ps: e.activation(out=vtok[:, c, :], in_=ps[0:64, :], func=AF.Copy),
                     reads=[kb], writes=[(vtn, c)])
                yield
            for _ in inter(seq(rolling([lambda h=h: hgrn_A(st, h) for h in range(4)], 2), hgrn_B(st)), ssd_conv(st)):
                yield

        def tail(st):
            for _ in inter(rolling([lambda h=h: hgrn_C(st, h) for h in range(4)], 2),
                           rolling([lambda q=q: ssd_chunk(st, q) for q in range(NQ)], 2)):
                yield
            for o in range(KC):
                kb, ps = self.gbank(1)
                self.proj(ps, kb, wout, kout, o * 128, yT, yn_, ncols=TBm)
                j = xri[0] % 2
                xri[0] += 1
                P.dma("sp", f"xrl{j}", lambda e, st=st, o=o, j=j: e.dma_start(
                    out=xr[:, j, :], in_=src[o * 128:(o + 1) * 128, st * TBm:(st + 1) * TBm]),
                    reads=[("dr", src_id, st // 2, o)], writes=[(xrn, j)])
                P.op("dve", lambda e, ps=ps, j=j: e.tensor_tensor(out=xr[:, j, :], in0=ps[:, 0:TBm], in1=xr[:, j, :], op=ALU.add),
                     reads=[kb, (xrn, j)], writes=[(xrn, j)])
                d = dst[o * 128:(o + 1) * 128, st * TBm:(st + 1) * TBm]
                P.dma("sp", f"xrs{j}", lambda e, d=d, j=j: e.dma_start(out=d, in_=xr[:, j, :]), reads=[(xrn, j)],
                      writes=[("dr", dst_id, st // 2, o)], is_out=(self.dst_id == "y"))
                yield

        ld(0)
        for _ in head(0):
            pass
        for st in range(NSTm):
            gens = [tail(st)]
            if st + 1 < NSTm:
                gens.append(head(st + 1))
            for _ in inter(*gens):
                pass
        assert not self.busy, self.busy
        self.end_stage()


def build_program(stages, ncc, nsc, coff, soff):
    nc = bass.Bass("TRN2", target_bir_lowering=False)
    B = Builder(nc, coff, soff, ncc, nsc)
    bufs = {"x": B.xT, "y": B.yT, "s0": B.scr[0], "s1": B.scr[1]}
    cur = "x"
    nxt = 0
    for i, stg in enumerate(stages):
        last = (i == len(stages) - 1)
        dst = "y" if last else f"s{nxt}"
        if not last:
            nxt = 1 - nxt
        kind = stg[0]
        if kind == "mlp":
            B.stage_mlp(stg[1], bufs[cur], cur, bufs[dst], dst)
        elif kind == "xattn":
            B.stage_xattn(stg[1], bufs[cur], cur, bufs[dst], dst)
        elif kind == "conf":
            B.stage_conf(stg[1], bufs[cur], cur, bufs[dst], dst)
        elif kind == "mixer":
            B.stage_mixer(stg[1], bufs[cur], cur, bufs[dst], dst)
        elif kind == "final":
            B.stage_final(bufs[cur], cur, bufs[dst], dst)
        cur = dst
    B.P.flush()
    B.P.final_wait("sp", B.P.out_tokens)
    B.P.emit()
    return nc, B


FULL_STAGES = []
for _l in range(4):
    FULL_STAGES.append(("mixer", _l) if _l % 2 == 0 else ("conf", _l))
    FULL_STAGES.append(("xattn", _l))
    FULL_STAGES.append(("mlp", _l))
FULL_STAGES.append(("final",))


def run(inputs, stages=None):
    stages = FULL_STAGES if stages is None else stages
    x = np.asarray(inputs["x"], np.float32)
    mem = np.asarray(inputs["mem"], np.float32)
    consts, coff = pack_consts(inputs)
    sconsts, soff = struct_consts()
    nc, B = build_program(stages, consts.shape[1], sconsts.shape[1], coff, soff)
    wts = {n: np.ascontiguousarray(np.asarray(inputs[n], np.float32)) for n in WEIGHT_NAMES}
    in_maps = []
    for b in range(8):
        m = {"xT": np.ascontiguousarray(x[b].T), "memT": np.ascontiguousarray(mem[b].T),
             "consts": consts, "sconsts": sconsts}
        m.update(wts)
        in_maps.append(m)
    res = run_bass_kernel_spmd(nc, in_maps, core_ids=list(range(8)))
    out = np.stack([np.ascontiguousarray(r["yT"].T) for r in res.results], axis=0)
    return out.astype(np.float32)


def kernel(**inputs):
    return run(inputs)
```
